# Optimizing a Trainium2 kernel written in Bass

```python
import math
import jax, jax.numpy as jnp
from jax import lax
import numpy as np

D_MODEL = 2048
BATCH = 16
SEQ = 256
DEPTH = 2
DEC_BATCH = 8
DEC_SEQ = 1024
PAST_LEN = 256

GRID_W = 64
N_EVEN = (DEPTH + 1) // 2
N_ODD = DEPTH // 2
HEAD_QK = 64
HEAD_V = 2 * HEAD_QK
H_A = D_MODEL // (2 * HEAD_V)
D_A_Q = H_A * 2 * HEAD_QK
D_A_V = H_A * HEAD_V
HEAD_B = 64
H_B = D_MODEL // (2 * HEAD_B)
D_B = H_B * HEAD_B
LORA_W = 96
LORA_A = 96
LORA_G = 256
D_SHIFT = 3 * D_B + LORA_W + LORA_A + LORA_G
D_IN = 2 * D_A_Q + D_A_V + D_SHIFT
D_MIX_OUT = D_A_V + D_B
D_CONV = D_MODEL
CONV_W = 31
D_FF = 5504
FFN_CONV_W = 3

RMS_EPS = 1e-6
LN_EPS = 1e-5
GN_EPS = 64e-5
Q_BLOCK = 128
ROPE_BASE = 10000.0
F32 = jnp.float32

kernel_name = 'hybrid_diffattn_rwkv7_conformer_dit_step'


def rmsnorm(x, g, eps=RMS_EPS):
    xf = x.astype(F32)
    y = xf * lax.rsqrt(jnp.mean(xf * xf, axis=-1, keepdims=True) + eps)
    return (y * g.astype(F32)).astype(x.dtype)


def layernorm(x, g, b, eps=LN_EPS):
    xf = x.astype(F32)
    mu = jnp.mean(xf, axis=-1, keepdims=True)
    var = jnp.mean(jnp.square(xf - mu), axis=-1, keepdims=True)
    y = (xf - mu) * lax.rsqrt(var + eps)
    return (y * g.astype(F32) + b.astype(F32)).astype(x.dtype)


def dwconv(x, w):
    k, ch = w.shape
    pad = (k - 1) // 2
    return lax.conv_general_dilated(x, w[:, None, :].astype(x.dtype), window_strides=(1,),
                                    padding=((pad, pad),), dimension_numbers=('NWC', 'WIO', 'NWC'),
                                    feature_group_count=ch)


def adaln(cvec, w, b):
    return (jax.nn.silu(cvec) @ w + b)[:, None, :]


def rope_2d(x):
    t = x.shape[-2]
    rows = t // GRID_W
    row = jnp.repeat(jnp.arange(rows), GRID_W).astype(F32)
    col = jnp.tile(jnp.arange(GRID_W), rows).astype(F32)
    half = HEAD_QK // 2
    n_freq = half // 2
    inv = ROPE_BASE ** (-jnp.arange(n_freq, dtype=F32) / n_freq)
    xf = x.astype(F32)

    def rot(xs, pos):
        ang = pos[:, None] * inv[None, :]
        cos, sin = jnp.cos(ang), jnp.sin(ang)
        x1, x2 = xs[..., :n_freq], xs[..., n_freq:]
        return jnp.concatenate([x1 * cos - x2 * sin, x1 * sin + x2 * cos], axis=-1)

    out = jnp.concatenate([rot(xf[..., :half], row), rot(xf[..., half:], col)], axis=-1)
    return out.astype(x.dtype)


def diff_attention(q, k, v, lam):
    b, h, _, tq, d = q.shape
    dv = v.shape[-1]
    nblk = tq // Q_BLOCK
    qb = q.astype(F32).reshape(b, h, 2, nblk, Q_BLOCK, d).transpose(3, 0, 1, 2, 4, 5)
    kf = k.astype(F32) * (d ** -0.5)
    vf = v.astype(F32)

    def one_block(q_blk):
        s = jnp.einsum('bhmqd,bhmkd->bhmqk', q_blk, kf)
        p = jax.nn.softmax(s, axis=-1)
        a = p[:, :, 0] - lam * p[:, :, 1]
        return jnp.einsum('bhqk,bhkv->bhqv', a, vf)

    o = lax.map(one_block, qb)
    return o.transpose(1, 2, 0, 3, 4).reshape(b, h, tq, dv)


def wkv7_scan(r, w, k, v, a, b, s0, reverse):
    xs = tuple(jnp.moveaxis(t, 1, 0) for t in (r, w, k, v, a, b))

    def step(s, inp):
        r_t, w_t, k_t, v_t, a_t, b_t = inp
        sa = jnp.einsum('bhvk,bhk->bhv', s, a_t)
        s = s * w_t[:, :, None, :] + sa[..., None] * b_t[:, :, None, :] + v_t[..., None] * k_t[:, :, None, :]
        y = jnp.einsum('bhvk,bhk->bhv', s, r_t)
        return s, y

    s_fin, ys = lax.scan(step, s0.astype(F32), xs, reverse=reverse)
    return jnp.moveaxis(ys, 0, 1), s_fin


def ab_mixer(h, w_in, w_out, diff_lambda, subln_g, shift_mu, w0, w2, a0, a2, g2, k_k, k_a, r_k,
             lnx_g, lnx_b, lam_init, ctx):
    bsz, t, _ = h.shape
    proj = h @ w_in
    qa, ka, va, rest = jnp.split(proj, [D_A_Q, 2 * D_A_Q, 2 * D_A_Q + D_A_V], axis=-1)
    q = qa.reshape(bsz, t, H_A, 2, HEAD_QK).transpose(0, 2, 3, 1, 4)
    k = ka.reshape(bsz, t, H_A, 2, HEAD_QK).transpose(0, 2, 3, 1, 4)
    v = va.reshape(bsz, t, H_A, HEAD_V).transpose(0, 2, 1, 3)
    if ctx is None:
        k_all, v_all = k, v
        s0_f = jnp.zeros((bsz, H_B, HEAD_B, HEAD_B), F32)
        s0_b = jnp.zeros((bsz, H_B, HEAD_B, HEAD_B), F32)
    else:
        k_ctx, v_ctx, s0_f, s0_b = ctx
        q, k = rope_2d(q), rope_2d(k)
        k_all = jnp.concatenate([k_ctx.astype(k.dtype), k], axis=3)
        v_all = jnp.concatenate([v_ctx.astype(v.dtype), v], axis=2)
    dl = diff_lambda.astype(F32)
    lam = jnp.exp(jnp.sum(dl[0] * dl[1])) - jnp.exp(jnp.sum(dl[2] * dl[3])) + lam_init
    o = diff_attention(q, k_all, v_all, lam)
    o = rmsnorm(o, subln_g, LN_EPS) * (1.0 - lam_init)
    o_a = o.transpose(0, 2, 1, 3).reshape(bsz, t, D_A_V).astype(h.dtype)
    p = rest.astype(F32)
    p_prev = jnp.pad(p, ((0, 0), (1, 0), (0, 0)))[:, :t]
    p_next = jnp.pad(p, ((0, 0), (0, 1), (0, 0)))[:, 1:]
    mu = shift_mu.astype(F32)
    p = p + mu[0] * (p_prev - p) + mu[1] * (p_next - p)
    r, kb, vb, xw, xa, xg = jnp.split(
        p, [D_B, 2 * D_B, 3 * D_B, 3 * D_B + LORA_W, 3 * D_B + LORA_W + LORA_A], axis=-1)
    heads = lambda z: z.reshape(bsz, t, H_B, HEAD_B)
    g = jax.nn.sigmoid(xg) @ g2.astype(F32)
    kk = heads(kb * k_k.astype(F32))
    kk = kk / jnp.maximum(jnp.sqrt(jnp.sum(kk * kk, axis=-1, keepdims=True)), 1e-12)
    y = jnp.zeros((bsz, t, H_B, HEAD_B), F32)
    finals = []
    for d, s0 in enumerate((s0_f, s0_b)):
        w_log = -jax.nn.softplus(-(w0[d].astype(F32) + jnp.tanh(xw) @ w2[d].astype(F32))) - 0.5
        decay = jnp.exp(-jnp.exp(w_log))
        a_lr = jax.nn.sigmoid(a0[d].astype(F32) + xa @ a2[d].astype(F32))
        k_d = kb * (1.0 + (a_lr - 1.0) * k_a.astype(F32))
        y_d, s_d = wkv7_scan(heads(r), heads(decay), heads(k_d), heads(vb), -kk,
                             kk * heads(a_lr), s0, reverse=(d == 1))
        y = y + y_d
        finals.append(s_d.astype(h.dtype))
    mu_y = jnp.mean(y, axis=-1, keepdims=True)
    var_y = jnp.mean(jnp.square(y - mu_y), axis=-1, keepdims=True)
    yn = ((y - mu_y) * lax.rsqrt(var_y + GN_EPS)).reshape(bsz, t, D_B)
    yn = yn * lnx_g.astype(F32) + lnx_b.astype(F32)
    bonus = jnp.sum(heads(r) * heads(kb) * r_k.astype(F32), axis=-1, keepdims=True) * heads(vb)
    o_b = ((yn + bonus.reshape(bsz, t, D_B)) * g).astype(h.dtype)
    out = jnp.concatenate([o_a, o_b], axis=-1) @ w_out
    return out, k, v, finals[0], finals[1]


def conformer_conv(h, w_pw1, w_dw, b_dw, cln_g, cln_b, w_pw2):
    u = h @ w_pw1
    u = u[..., :D_CONV] * jax.nn.sigmoid(u[..., D_CONV:])
    u = dwconv(u, w_dw) + b_dw
    u = jax.nn.silu(layernorm(u, cln_g, cln_b))
    return u @ w_pw2


def conv_ffn(h, w_up, w_dw, w_down):
    u = dwconv(h @ w_up, w_dw)
    gate, val = jnp.split(u, 2, axis=-1)
    return (jax.nn.silu(gate) * val) @ w_down


def setup_inputs(seed: int = 0) -> dict:
    key = jax.random.key(seed)
    ks = iter(jax.random.split(key, 48))
    nrm = lambda shape, s: jax.random.normal(next(ks), shape, F32) * s
    return {
        'x_prompt': nrm((BATCH, SEQ, D_MODEL), 1.0),
        'x_sample': nrm((DEC_BATCH, DEC_SEQ, D_MODEL), 1.0),
        'cache_k': nrm((DEC_BATCH, N_EVEN, H_A, 2, PAST_LEN, HEAD_QK), 1.0),
        'cache_v': nrm((DEC_BATCH, N_EVEN, H_A, PAST_LEN, HEAD_V), 1.0),
        'state_wkv_fwd': nrm((DEC_BATCH, N_EVEN, H_B, HEAD_B, HEAD_B), 0.3),
        'state_wkv_bwd': nrm((DEC_BATCH, N_EVEN, H_B, HEAD_B, HEAD_B), 0.3),
        'c': nrm((DEC_BATCH, D_MODEL), 1.0),
        'c_ctx': nrm((D_MODEL,), 1.0),
        'w_ada': nrm((DEPTH, D_MODEL, 6 * D_MODEL), 0.5 * D_MODEL ** -0.5),
        'b_ada': nrm((DEPTH, 6 * D_MODEL), 0.01),
        'norm_g': 1.0 + nrm((DEPTH, 2, D_MODEL), 0.02),
        'final_norm_g': 1.0 + nrm((D_MODEL,), 0.02),
        'w_in': nrm((N_EVEN, D_MODEL, D_IN), D_MODEL ** -0.5),
        'w_out': nrm((N_EVEN, D_MIX_OUT, D_MODEL), D_MIX_OUT ** -0.5),
        'diff_lambda': nrm((N_EVEN, 4, HEAD_QK), 0.1),
        'subln_g': 1.0 + nrm((N_EVEN, HEAD_V), 0.02),
        'shift_mu': 0.3 + nrm((N_EVEN, 2, D_SHIFT), 0.1),
        'w0': -2.0 + nrm((N_EVEN, 2, D_B), 0.5),
        'w2': nrm((N_EVEN, 2, LORA_W, D_B), 0.5 * LORA_W ** -0.5),
        'a0': nrm((N_EVEN, 2, D_B), 0.1),
        'a2': nrm((N_EVEN, 2, LORA_A, D_B), 0.5 * LORA_A ** -0.5),
        'g2': nrm((N_EVEN, LORA_G, D_B), LORA_G ** -0.5),
        'k_k': 0.85 + nrm((N_EVEN, D_B), 0.05),
        'k_a': 1.0 + nrm((N_EVEN, D_B), 0.05),
        'r_k': nrm((N_EVEN, H_B, HEAD_B), 0.1),
        'lnx_g': 1.0 + nrm((N_EVEN, D_B), 0.02),
        'lnx_b': nrm((N_EVEN, D_B), 0.01),
        'w_pw1': nrm((N_ODD, D_MODEL, 2 * D_CONV), D_MODEL ** -0.5),
        'w_dw': nrm((N_ODD, CONV_W, D_CONV), CONV_W ** -0.5),
        'b_dw': nrm((N_ODD, D_CONV), 0.01),
        'cln_g': 1.0 + nrm((N_ODD, D_CONV), 0.02),
        'cln_b': nrm((N_ODD, D_CONV), 0.01),
        'w_pw2': nrm((N_ODD, D_CONV, D_MODEL), D_CONV ** -0.5),
        'w_up': nrm((DEPTH, D_MODEL, 2 * D_FF), D_MODEL ** -0.5),
        'w_ffn_dw': nrm((DEPTH, FFN_CONV_W, 2 * D_FF), FFN_CONV_W ** -0.5),
        'w_down': nrm((DEPTH, D_FF, D_MODEL), D_FF ** -0.5),
    }


def reference(x_prompt, x_sample, cache_k, cache_v, state_wkv_fwd, state_wkv_bwd, c, c_ctx,
              w_ada, b_ada, norm_g, final_norm_g, w_in, w_out, diff_lambda, subln_g, shift_mu,
              w0, w2, a0, a2, g2, k_k, k_a, r_k, lnx_g, lnx_b, w_pw1, w_dw, b_dw, cln_g, cln_b,
              w_pw2, w_up, w_ffn_dw, w_down):
    xp, xs = x_prompt, x_sample
    new_k, new_v, new_sf, new_sb = [], [], [], []
    for l in range(DEPTH):
        sh1p, sc1p, g1p, sh2p, sc2p, g2p = jnp.split(adaln(c_ctx[None, :], w_ada[l], b_ada[l]), 6, axis=-1)
        sh1s, sc1s, g1s, sh2s, sc2s, g2s = jnp.split(adaln(c, w_ada[l], b_ada[l]), 6, axis=-1)
        hp = rmsnorm(xp, norm_g[l, 0]) * (1.0 + sc1p) + sh1p
        hs = rmsnorm(xs, norm_g[l, 0]) * (1.0 + sc1s) + sh1s
        if l % 2 == 0:
            e = l // 2
            prm = (w_in[e], w_out[e], diff_lambda[e], subln_g[e], shift_mu[e], w0[e], w2[e], a0[e],
                   a2[e], g2[e], k_k[e], k_a[e], r_k[e], lnx_g[e], lnx_b[e])
            lam_init = 0.8 - 0.6 * math.exp(-0.3 * l)
            mp, kc, vc, sf, sb = ab_mixer(hp, *prm, lam_init=lam_init, ctx=None)
            ms, _, _, _, _ = ab_mixer(hs, *prm, lam_init=lam_init,
                                      ctx=(cache_k[:, e], cache_v[:, e], state_wkv_fwd[:, e], state_wkv_bwd[:, e]))
            new_k.append(kc)
            new_v.append(vc)
            new_sf.append(sf)
            new_sb.append(sb)
        else:
            o = l // 2
            prm = (w_pw1[o], w_dw[o], b_dw[o], cln_g[o], cln_b[o], w_pw2[o])
            mp = conformer_conv(hp, *prm)
            ms = conformer_conv(hs, *prm)
        xp = xp + g1p * mp
        xs = xs + g1s * ms
        hp = rmsnorm(xp, norm_g[l, 1]) * (1.0 + sc2p) + sh2p
        hs = rmsnorm(xs, norm_g[l, 1]) * (1.0 + sc2s) + sh2s
        xp = xp + g2p * conv_ffn(hp, w_up[l], w_ffn_dw[l], w_down[l])
        xs = xs + g2s * conv_ffn(hs, w_up[l], w_ffn_dw[l], w_down[l])
    y_prompt = rmsnorm(xp, final_norm_g)
    y_sample = rmsnorm(xs, final_norm_g)
    new_cache_k = jnp.stack(new_k, axis=1)
    new_cache_v = jnp.stack(new_v, axis=1)
    new_state_fwd = jnp.stack(new_sf, axis=1)
    new_state_bwd = jnp.stack(new_sb, axis=1)
    return (y_prompt, y_sample, new_cache_k, new_cache_v, new_state_fwd, new_state_bwd)
```

```python
import math
import numpy as np
import concourse.bass as bass
import concourse.mybir as mybir
from concourse.bass_utils import run_bass_kernel_spmd
from contextlib import ExitStack

F32 = mybir.dt.float32
BF16 = mybir.dt.bfloat16
ALU = mybir.AluOpType
AF = mybir.ActivationFunctionType
AX = mybir.AxisListType

D = 2048
KC = 16
NT = 1536
NPAD = NT + 4
SEQS = [(0, 256), (256, 256), (512, 1024)]
POFF = [1, 258, 515]
D_FF = 5504
NFF = 43
LAM_INIT = 0.8 - 0.6 * math.exp(0.0)
RMS_EPS = 1e-6
LN_EPS = 1e-5
GN_EPS = 64e-5
N_WIN = 52


class Prog:
    NS = 8

    def __init__(self, nc, stack):
        self.nc = nc
        self.stack = stack
        self.ce = ['pe', 'dve', 'act', 'pool']
        self.engs = ['pe', 'dve', 'act', 'pool', 'sp']
        self.prog = {e: [] for e in self.engs}
        self.sems = {}
        self.cnt = {}
        self.waited = {e: {} for e in self.engs}
        self.lastw = {}
        self.readers = {}
        self.ndma = {e: 0 for e in self.engs}
        self.epoch = 0
        self.nsem = 0
        self.dead = False
        for e in self.ce:
            self._mk(('c', e, 0))

    def _mk(self, key):
        self.sems[key] = self.stack.enter_context(self.nc.semaphore("s%d" % self.nsem))
        self.nsem += 1
        self.cnt[key] = 0

    def _deps(self, reads, writes):
        deps = {}
        for r in reads:
            lw = self.lastw.get(r)
            if lw is not None and deps.get(lw[0], 0) < lw[1]:
                deps[lw[0]] = lw[1]
            if isinstance(r, str) and r.startswith('ps'):
                for sk, v in self.readers.get(r, {}).items():
                    if deps.get(sk, 0) < v:
                        deps[sk] = v
        for w in writes:
            lw = self.lastw.get(w)
            if lw is not None and deps.get(lw[0], 0) < lw[1]:
                deps[lw[0]] = lw[1]
            for sk, v in self.readers.get(w, {}).items():
                if deps.get(sk, 0) < v:
                    deps[sk] = v
        return deps

    def _waits(self, eng, deps, skip=None):
        waits = []
        wd = self.waited[eng]
        for sk, v in deps.items():
            if sk == skip:
                continue
            if wd.get(sk, 0) < v:
                waits.append((self.sems[sk], v))
                wd[sk] = v
        return waits

    def _record(self, sk, val, reads, writes):
        for r in reads:
            d = self.readers.setdefault(r, {})
            if d.get(sk, 0) < val:
                d[sk] = val
        for w in writes:
            self.lastw[w] = (sk, val)
            self.readers[w] = {}

    def op(self, eng, fn, reads=(), writes=()):
        if self.dead:
            return
        own = ('c', eng, self.epoch)
        deps = self._deps(reads, writes)
        waits = self._waits(eng, deps, skip=own if eng == 'pe' else None)
        self.cnt[own] += 1
        val = self.cnt[own]
        sem = self.sems[own]

        def emit(e):
            for s, v in waits:
                e.wait_ge(s, v)
            fn(e).then_inc(sem, 1)
        self.prog[eng].append(emit)
        self._record(own, val, reads, writes)

    def dma(self, eng, out, in_, reads=(), writes=(), **kw):
        if self.dead:
            return
        i = self.ndma[eng]
        self.ndma[eng] += 1
        sk = ('d', eng, i % self.NS)
        if sk not in self.sems:
            self._mk(sk)
        target = 16 * (i // self.NS + 1)
        deps = self._deps(reads, writes)
        if target > 16 and deps.get(sk, 0) < target - 16:
            deps[sk] = target - 16
        waits = self._waits(eng, deps)
        self.cnt[sk] = target
        sem = self.sems[sk]

        def emit(e):
            for s, v in waits:
                e.wait_ge(s, v)
            e.dma_start(out=out, in_=in_, **kw).then_inc(sem, 16)
        self.prog[eng].append(emit)
        self._record(sk, target, reads, writes)

    def _finals(self):
        return {sk: c for sk, c in self.cnt.items()
                if c > 0 and (sk[0] == 'd' or sk[2] == self.epoch)}

    def barrier(self):
        if self.dead:
            return
        finals = self._finals()
        for eng in self.engs:
            waits = self._waits(eng, finals, skip=('c', eng, self.epoch))
            if waits:
                def emit(e, waits=waits):
                    for s, v in waits:
                        e.wait_ge(s, v)
                self.prog[eng].append(emit)
        self.epoch += 1
        for e in self.ce:
            self._mk(('c', e, self.epoch))
        self.lastw = {}
        self.readers = {}

    def finish(self):
        finals = self._finals()
        for eng in self.engs:
            waits = self._waits(eng, finals, skip=('c', eng, self.epoch))

            def emit(e, waits=waits):
                for s, v in waits:
                    e.wait_ge(s, v)
            self.prog[eng].append(emit)
        prog = self.prog
        with self.nc.Block() as block:
            @block.tensor
            def _(e):
                for f in prog['pe']:
                    f(e)

            @block.vector
            def _(e):
                for f in prog['dve']:
                    f(e)

            @block.scalar
            def _(e):
                for f in prog['act']:
                    f(e)

            @block.gpsimd
            def _(e):
                for f in prog['pool']:
                    f(e)

            @block.sync
            def _(e):
                for f in prog['sp']:
                    f(e)


def par_layout():
    ents = [('b_ada0', 96), ('b_ada1', 96), ('ng00', 16), ('ng01', 16), ('ng10', 16), ('ng11', 16),
            ('fng', 16), ('mu0', 28), ('mu1', 28), ('w0_0', 8), ('w0_1', 8), ('a0_0', 8), ('a0_1', 8),
            ('k_k', 8), ('k_a', 8), ('r_k', 8), ('lnx_g', 8), ('lnx_b', 8), ('subln', 1), ('dl', 256),
            ('b_dw', 16), ('cln_g', 16), ('cln_b', 16), ('w_dw', 496), ('fdw0', 258), ('fdw1', 258)]
    lay = {}
    o = 0
    for n, w in ents:
        lay[n] = (o, w)
        o += w
    return lay, o


PLAY, NPAR = par_layout()


def win_cols():
    ch = []
    for h in range(8):
        ch.append(list(range(h * 128, h * 128 + 128)))
    for h in range(8):
        ch.append(list(range(1024 + h * 128, 1024 + h * 128 + 128)))
    for h in range(8):
        ch.append(list(range(2048 + h * 128, 2048 + h * 128 + 128)))
    ch.append(list(range(6144, 6240)) + [-1] * 32)
    ch.append(list(range(6240, 6336)) + [-1] * 32)
    ch.append(list(range(6336, 6464)))
    ch.append(list(range(6464, 6592)))
    for hp in range(8):
        for base in (3072, 4096, 5120):
            ch.append(list(range(base + hp * 128, base + hp * 128 + 128)))
    return ch


def rest_idx(c):
    if c == 24:
        return 24
    if c == 25:
        return 25
    if c in (26, 27):
        return c
    hp, t = divmod(c - 28, 3)
    return t * 8 + hp


def build(debug=0, stop=99, lite=0, sub=None):
    nc = bass.Bass('TRN2', target_bir_lowering=False)

    def din(name, shape, dt=F32):
        return nc.dram_tensor(name, list(shape), dt, kind="ExternalInput").ap()

    def dout(name, shape, dt=F32):
        return nc.dram_tensor(name, list(shape), dt, kind="ExternalOutput").ap()

    def dscr(name, shape, dt=F32):
        return nc.dram_tensor(name, list(shape), dt, kind="ExternalOutput" if debug else "Internal").ap()

    xin = din("xin", [D, NT])
    cfm_d = din("cfm", [128, KC, 2])
    pars_d = din("pars", [128, NPAR])
    ident_d = din("ident", [128, 128])
    bones_d = din("bones", [128, 128])
    rt_d = din("rt", [128, 128])
    cos_d = din("cos", [128, 1024])
    sin_d = din("sin", [128, 1024])
    cK_d = din("cK", [8, 128, 256])
    cV_d = din("cV", [8, 256, 128])
    s0_d = din("s0", [2, 128, 512])
    wada_d = din("wada", [2, 24, 128, KC, 512] if not lite else [1, 1, 128, 1, 512])
    win_d = din("win", [N_WIN, 128, KC, 128])
    lora_d = din("lora", [128, 3, 2, 1024])
    wout_d = din("wout", [16, 128, KC, 128] if not (lite and stop < 4) else [1, 1, 128, 1, 128])
    wpw1_d = din("wpw1", [32, 128, KC, 128] if not (lite and stop < 6) else [1, 1, 128, 1, 128])
    wpw2_d = din("wpw2", [16, 128, KC, 128] if not (lite and stop < 6) else [1, 1, 128, 1, 128])
    wup_d = din("wup", [2, NFF, 128, KC, 256] if not (lite and stop < 5) else [1, 1, 128, 1, 128])
    wdn_d = din("wdn", [2, NFF, 128, D] if not (lite and stop < 5) else [1, 1, 128, 1, 128])
    if lite:
        class _Dm:
            def __getitem__(self, k):
                return self

            def __getattr__(self, n):
                return lambda *a, **k: self
        if stop < 4:
            wout_d = _Dm()
        if stop < 5:
            wup_d = wdn_d = _Dm()
        if stop < 6:
            wpw1_d = wpw2_d = _Dm()
    y_d = dout("y", [D, NT])
    nk_d = dout("nk", [2, 8, 128, 256])
    nv_d = dout("nv", [2, 8, 256, 128])
    nst_d = dout("nst", [2, 2, 128, 512])
    QS = dscr("QS", [16, 128, NT], BF16)
    VS = dscr("VS", [8, NT, 128], BF16)
    TMS = dscr("TMS", [10, NT, 1024], BF16)
    VBF = dscr("VBF", [1024, NT])
    BON = dscr("BON", [8, 128, NT])
    GATE = dscr("GATE", [8, 128, NT])
    YF = dscr("YF", [2, 1024, NT])
    CONV = dscr("CONV", [16, 128, NT])
    dbg = {}

    with ExitStack() as st:
        P = Prog(nc, st)

        cur = [st]

        def sb(name, shape, dt=F32):
            return cur[-1].enter_context(nc.sbuf_tensor(name, list(shape), dt))

        PS = [st.enter_context(nc.psum_tensor("ps%d" % i, [128, 512], F32)) for i in range(8)]
        psn = [0]

        psh = [0]

        def ps_hi():
            i = 4 + psh[0] % 4
            psh[0] += 1
            return PS[i], 'ps%d' % i

        def ps_next():
            i = psn[0] % 8
            psn[0] += 1
            return PS[i], 'ps%d' % i

        class Ring:
            def __init__(self, name, n, shape, dt=F32):
                self.t = [sb("%s%d" % (name, i), shape, dt) for i in range(n)]
                self.k = ["%s%d" % (name, i) for i in range(n)]
                self.i = 0

            def next(self):
                j = self.i % len(self.t)
                self.i += 1
                return self.t[j], self.k[j]

        def TT(eng, out, in0, in1, op, r, w):
            P.op(eng, lambda e: e.tensor_tensor(out=out, in0=in0, in1=in1, op=op), r, w)

        def TS(eng, out, in0, s1, s2, op0, op1, r, w):
            if s2 is None:
                P.op(eng, lambda e: e.tensor_single_scalar(out=out, in_=in0, scalar=s1, op=op0), r, w)
            else:
                P.op(eng, lambda e: e.tensor_scalar(out=out, in0=in0, scalar1=s1, scalar2=s2, op0=op0, op1=op1), r, w)

        def STT(eng, out, in0, scalar, in1, op0, op1, r, w):
            P.op(eng, lambda e: e.scalar_tensor_tensor(out=out, in0=in0, scalar=scalar, in1=in1, op0=op0, op1=op1), r, w)

        def ACTV(out, in_, func, r, w, bias=None, scale=None):
            kw = {}
            if bias is not None:
                kw['bias'] = bias
            if scale is not None:
                kw['scale'] = scale
            P.op('act', lambda e: e.activation(out=out, in_=in_, func=func, **kw), r, w)

        EPSI = {RMS_EPS: 0, LN_EPS: 1, GN_EPS: 2}

        def RSQ(out, in_, scale, eps, r, w):
            j = EPSI[eps]
            ACTV(out, in_, AF.Ln, list(r) + ['EPS'], w, bias=EPS[:, j:j + 1], scale=scale)
            ACTV(out, out, AF.Exp, w, w, scale=-0.5)

        def CP(eng, out, in_, r, w):
            if eng == 'act':
                P.op('act', lambda e: e.copy(out=out, in_=in_), r, w)
            elif 'PSum' in type(in_.tensor).__name__:
                P.op(eng, lambda e: e.tensor_single_scalar(out=out, in_=in_, scalar=1.0, op=ALU.mult), r, w)
            else:
                P.op(eng, lambda e: e.tensor_copy(out=out, in_=in_), r, w)

        def MM(out, lhsT, rhs, start, stop, r, w):
            P.op('pe', lambda e: e.matmul(out, lhsT=lhsT, rhs=rhs, start=start, stop=stop), r, w)

        def RED(eng, out, in_, r, w):
            P.op(eng, lambda e: e.tensor_reduce(out=out, in_=in_, axis=AX.X, op=ALU.add), r, w)

        def MEMSET(eng, ap, val, w):
            P.op(eng, lambda e: e.memset(ap, val), (), w)

        def chk(k):
            if stop <= k and not P.dead:
                if debug:
                    dx = dout("dbgX", [128, KC, NT])
                    P.dma('sp', dx, X[:], reads=[('X', 0), ('X', 1), ('X', 2)])
                P.dead = True

        X = sb("X", [128, KC, NT])
        PAR = sb("PAR", [128, NPAR])
        IDF = sb("IDF", [128, 128])
        IDB = sb("IDB", [128, 128], BF16)
        ONF = sb("ONF", [128, 128])
        ONB = sb("ONB", [128, 128], BF16)
        BOF = sb("BOF", [128, 128])
        MOD = sb("MOD", [128, 2, 96, 2])
        GG = sb("GG", [128, 2, 2, KC, 2])
        NEGLAM = sb("NEGLAM", [128, 1])
        SUBG = sb("SUBG", [128, 1])
        CM = sb("CM", [128, 28])
        EPS = sb("EPS", [128, 4])

        def par(name, j0=0, n=None):
            o, w = PLAY[name]
            if n is None:
                n = w - j0
            return PAR[:, o + j0:o + j0 + n]

        P.dma('sp', PAR[:], pars_d, writes=['PAR'])
        P.dma('sp', IDF[:], ident_d, writes=['IDF'])
        P.dma('pool', IDB[:], ident_d, writes=['IDB'])
        P.dma('sp', BOF[:], bones_d, writes=['BOF'])
        MEMSET('dve', ONF[:], 1.0, ['ONF'])
        for eps_, j_ in EPSI.items():
            MEMSET('dve', EPS[:, j_:j_ + 1], eps_, ['EPS'])
        MEMSET('dve', ONB[:], 1.0, ['ONB'])
        for kc in range(KC):
            P.dma('sp', X[:, kc, :], xin[kc * 128:(kc + 1) * 128, :], writes=[('X', 0), ('X', 1), ('X', 2)])

        with ExitStack() as ph:
            cur.append(ph)
            sbp = sb
            SC = sbp("SC", [128, KC, 2])
            WA = [sbp("WA%d" % i, [128, KC, 512]) for i in range(2)]
            P.dma('sp', SC[:], cfm_d, writes=['SC'])
            ACTV(SC[:], SC[:], AF.Silu, ['SC'], ['SC'])
            psM, kM = PS[0], 'ps0'
            for l in range(2):
                for jb in range(24 if not lite else 0):
                    wa = WA[jb % 2]
                    wk = 'WA%d' % (jb % 2)
                    P.dma('sp' if jb % 2 == 0 else 'act', wa[:], wada_d[l, jb], writes=[wk])
                    for jj in range(4):
                        j = jb * 4 + jj
                        for kc in range(KC):
                            MM(psM[:, 2 * j:2 * j + 2], wa[:, kc, jj * 128:(jj + 1) * 128], SC[:, kc, :],
                               kc == 0, kc == KC - 1, [wk, 'SC'], [kM])
                bo = PLAY['b_ada%d' % l][0]
                if lite:
                    MEMSET('dve', MOD[:, l], 0.05, ['MOD'])
                else:
                    TT('dve', MOD[:, l], psM[:, 0:192].rearrange("p (j s) -> p j s", s=2),
                       PAR[:, bo:bo + 96].unsqueeze(2).to_broadcast([128, 96, 2]), ALU.add, [kM, 'PAR'], ['MOD'])
                for wh in range(2):
                    go = PLAY['ng%d%d' % (l, wh)][0]
                    STT('dve', GG[:, l, wh], MOD[:, l, (1 + 3 * wh) * 16:(2 + 3 * wh) * 16, :], 1.0,
                        PAR[:, go:go + 16].unsqueeze(2).to_broadcast([128, 16, 2]), ALU.add, ALU.mult,
                        ['MOD', 'PAR'], ['GG'])
            DT = sbp("DT", [128, 2, 64])
            DS = sbp("DS", [128, 2])
            dlo = PLAY['dl'][0]
            dlv = PAR[:, dlo:dlo + 256].rearrange("p (a b k) -> p a b k", a=2, b=2)
            TT('dve', DT[:], dlv[:, :, 0, :], dlv[:, :, 1, :], ALU.mult, ['PAR'], ['DT'])
            RED('dve', DS[:], DT[:], ['DT'], ['DS'])
            ACTV(DS[:], DS[:], AF.Exp, ['DS'], ['DS'])
            TT('dve', NEGLAM[:], DS[:, 1:2], DS[:, 0:1], ALU.subtract, ['DS'], ['NEGLAM'])
            TS('dve', NEGLAM[:], NEGLAM[:], -LAM_INIT, None, ALU.add, None, ['NEGLAM'], ['NEGLAM'])
            TS('dve', SUBG[:], par('subln'), 1.0 - LAM_INIT, None, ALU.mult, None, ['PAR'], ['SUBG'])
            TT('dve', CM[:], par('mu0'), par('mu1'), ALU.add, ['PAR'], ['CM'])
            TS('dve', CM[:], CM[:], -1.0, 1.0, ALU.mult, ALU.add, ['CM'], ['CM'])
            P.barrier()
            cur.pop()

        def norm_mod(H, l, wh, SQ, RSTD, TMPN):
            for tb in range(3):
                s = 1 if tb == 0 else 0
                psN, kN = ps_next()
                for kc in range(KC):
                    sq, sk = SQ.next()
                    ACTV(sq[:], X[:, kc, tb * 512:(tb + 1) * 512], AF.Square, [('X', tb)], [sk])
                    MM(psN[:], ONF[:], sq[:], kc == 0, kc == KC - 1, [sk, 'ONF'], [kN])
                RSQ(RSTD[:], psN[:], 1.0 / D, RMS_EPS, [kN], ['RSTD'])
                for kc in range(KC):
                    tm, tk = TMPN.next()
                    TT('dve', tm[:], X[:, kc, tb * 512:(tb + 1) * 512], RSTD[:], ALU.mult, [('X', tb), 'RSTD'], [tk])
                    ACTV(H[:, kc, tb * 512:(tb + 1) * 512], tm[:], AF.Identity, [tk, 'GG', 'MOD'], [('H', tb)],
                         bias=MOD[:, l, (3 * wh) * 16 + kc, s:s + 1], scale=GG[:, l, wh, kc, s:s + 1])

        def resid_add(ps, kps, l, wh, kc, tb):
            s = 1 if tb == 0 else 0
            xs = X[:, kc, tb * 512:(tb + 1) * 512]
            STT('dve', xs, ps[:], MOD[:, l, (2 + 3 * wh) * 16 + kc, s:s + 1], xs, ALU.mult, ALU.add,
                [kps, 'MOD', ('X', tb)], [('X', tb)])


        def pad_off(t):
            return t + 1 if t < 256 else (t + 2 if t < 512 else t + 3)
        PCS = [(1, 385), (386, 385), (771, 384), (1155, 384)]
        SEQP = [(0, 256, 1), (256, 256, 258), (512, 1024, 515)]
        RS = dscr("RS", [28, 128, NPAD])

        def load_w(WBr, src, eng='pool'):
            wb, wk = WBr.next()
            P.dma(eng, wb[:], src, writes=[wk])
            return wb, wk

        with ExitStack() as ph:
            cur.append(ph)
            H = sb("H", [128, KC, NT], BF16)
            SQ = Ring("SQ", 2, [128, 512])
            TMPN = Ring("TMPN", 2, [128, 512])
            RSTD = sb("RSTD", [128, 512])
            norm_mod(H, 0, 0, SQ, RSTD, TMPN)
            WB = Ring("WB", 2, [128, KC, 128], BF16)
            COS = sb("COS", [128, 1024])
            SIN = sb("SIN", [128, 1024])
            RTF = sb("RTF", [128, 128])
            P.dma('sp', COS[:], cos_d, writes=['COS'])
            P.dma('sp', SIN[:], sin_d, writes=['SIN'])
            P.dma('sp', RTF[:], rt_d, writes=['RTF'])
            RAWP = sb("RAWP", [128, NPAD])
            OUTP = sb("OUTP", [128, NPAD])
            MEMSET('dve', RAWP[:], 0.0, ['RAWP'])
            QB = Ring("QB", 2, [128, 512], BF16)
            XS = Ring("XS", 2, [128, 512])
            T1 = Ring("T1", 2, [128, 512])
            T2 = Ring("T2", 2, [128, 512])
            HK = [('H', 0), ('H', 1), ('H', 2)]

            def proj_tb(wb, wk, tb):
                ps, kp = ps_next()
                for kc in range(KC):
                    MM(ps[:], wb[:, kc, :], H[:, kc, tb * 512:(tb + 1) * 512], kc == 0, kc == KC - 1,
                       [wk, ('H', tb)], [kp])
                return ps, kp

            SUB = sub if sub is not None else 'qvr'
            for c in range(16 if 'q' in SUB else 0):
                wb, wk = load_w(WB, win_d[c])
                if 'L' in SUB:
                    continue
                for tb in range(3):
                    ps, kp = proj_tb(wb, wk, tb)
                    if 'M' in SUB:
                        continue
                    if 'E' in SUB and tb > 0:
                        continue
                    qb, qk = QB.next()
                    if tb == 0:
                        CP('act', qb[:], ps[:], [kp], [qk])
                        if c >= 8 and 'A' not in SUB:
                            xs, xk = XS.next()
                            CP('dve', xs[:], ps[:], [kp, qk] if 'S' in SUB else [kp], [xk])
                            for pi in range(2 if 'C' not in SUB else 0):
                                P.dma('sp', nk_d[pi, c - 8], xs[:, pi * 256:(pi + 1) * 256], reads=[xk])
                    else:
                        xs, xk = XS.next()
                        CP('act', xs[:], ps[:], [kp], [xk])
                        ps2, kp2 = ps_next()
                        MM(ps2[:], RTF[:], xs[:], True, True, ['RTF', xk], [kp2])
                        t1, k1 = T1.next()
                        t2, k2 = T2.next()
                        TT('dve', t1[:], xs[:], COS[:, (tb - 1) * 512:tb * 512], ALU.mult, [xk, 'COS'], [k1])
                        TT('dve', t2[:], ps2[:], SIN[:, (tb - 1) * 512:tb * 512], ALU.mult, [kp2, 'SIN'], [k2])
                        TT('dve', qb[:], t1[:], t2[:], ALU.add, [k1, k2], [qk])
                    if 'B' not in SUB:
                        P.dma('sp', QS[c, :, tb * 512:(tb + 1) * 512], qb[:], reads=[qk], writes=[('QS', c)])
            for h in range(8 if 'v' in SUB else 0):
                wb, wk = load_w(WB, win_d[16 + h])
                for g in range(3):
                    ps, kp = ps_next()
                    for j in range(4):
                        tt = g * 4 + j
                        for kc in range(KC):
                            MM(ps[:, j * 128:(j + 1) * 128], H[:, kc, tt * 128:(tt + 1) * 128], wb[:, kc, :],
                               kc == 0, kc == KC - 1, [wk, ('H', g)], [kp])
                    qb, qk = QB.next()
                    CP('act', qb[:], ps[:], [kp], [qk])
                    P.dma('sp', VS[h, g * 512:(g + 1) * 512, :].rearrange("(j p) d -> p j d", p=128),
                          qb[:].rearrange("p (j d) -> p j d", d=128), reads=[qk], writes=[('VS', h)])
                    if g == 0:
                        xs, xk = XS.next()
                        CP('dve', xs[:], ps[:], [kp], [xk])
                        for pi in range(2):
                            P.dma('sp', nv_d[pi, h].rearrange("(j p) d -> p j d", p=128),
                                  xs[:, pi * 256:(pi + 1) * 256].rearrange("p (j d) -> p j d", d=128), reads=[xk])
            for c in range(24, N_WIN if 'r' in SUB else 24):
                ri = rest_idx(c)
                wb, wk = load_w(WB, win_d[c])
                for tb in range(3):
                    ps, kp = proj_tb(wb, wk, tb)
                    if tb == 0:
                        CP('act', RAWP[:, 1:257], ps[:, 0:256], [kp], ['RAWP'])
                        CP('act', RAWP[:, 258:514], ps[:, 256:512], [kp], ['RAWP'])
                    else:
                        o = 515 + (tb - 1) * 512
                        CP('act', RAWP[:, o:o + 512], ps[:], [kp], ['RAWP'])
                m0o = PLAY['mu0'][0] + ri
                m1o = PLAY['mu1'][0] + ri
                TS('dve', OUTP[:, 1:1539], RAWP[:, 1:1539], CM[:, ri:ri + 1], None, ALU.mult, None, ['RAWP', 'CM'], ['OUTP'])
                STT('dve', OUTP[:, 1:1539], RAWP[:, 0:1538], PAR[:, m0o:m0o + 1], OUTP[:, 1:1539], ALU.mult, ALU.add,
                    ['RAWP', 'PAR', 'OUTP'], ['OUTP'])
                STT('dve', OUTP[:, 1:1539], RAWP[:, 2:1540], PAR[:, m1o:m1o + 1], OUTP[:, 1:1539], ALU.mult, ALU.add,
                    ['RAWP', 'PAR', 'OUTP'], ['OUTP'])
                P.dma('sp', RS[ri], OUTP[:], reads=['OUTP'], writes=[('RS', ri)])
            P.barrier()
            cur.pop()

        chk(1)
        with ExitStack() as ph:
            cur.append(ph)
            LORA = sb("LORA", [128, 3, 2, 1024], BF16)
            P.dma('pool', LORA[:], lora_d, writes=['LORA'])
            TWr = sb("TWr", [128, NPAD])
            TW = sb("TW", [128, NPAD], BF16)
            XA = sb("XA", [128, NPAD], BF16)
            SG = sb("SG", [128, 2, NPAD], BF16)
            OMK = sb("OMK", [128, 8])
            TS('dve', OMK[:], par('k_a'), -1.0, 1.0, ALU.mult, ALU.add, ['PAR'], ['OMK'])
            P.dma('sp', TWr[:], RS[24], writes=['TWr'])
            ACTV(TW[:], TWr[:], AF.Tanh, ['TWr'], ['TW'])
            P.dma('sp', TWr[:], RS[25], reads=[], writes=['TWr'])
            CP('act', XA[:], TWr[:], ['TWr'], ['XA'])
            for i in range(2):
                P.dma('sp', TWr[:], RS[26 + i], writes=['TWr'])
                ACTV(SG[:, i, :], TWr[:], AF.Sigmoid, ['TWr'], ['SG'])
            Rb = sb("Rb", [128, NPAD])
            KBb = sb("KBb", [128, NPAD])
            VBb = sb("VBb", [128, NPAD])
            KKb = sb("KKb", [128, NPAD])
            Ab = sb("Ab", [128, NPAD])
            TA = sb("TA", [128, NPAD])
            TBb = sb("TBb", [128, NPAD])
            WDb = sb("WDb", [128, NPAD])
            ALRb = sb("ALRb", [128, NPAD])
            KDb = sb("KDb", [128, NPAD])
            BDb = sb("BDb", [128, NPAD])
            TMB = Ring("TMB", 4, [128, 512], BF16)
            SL = slice(1, 1539)

            def to_tm(arr, akey, idx, hp, hilo=False):
                for g in range(3):
                    ps, kp = ps_next()
                    for j in range(4):
                        o = pad_off((g * 4 + j) * 128)
                        MM(ps[:, j * 128:(j + 1) * 128], arr[:, o:o + 128], IDF[:], True, True, [akey, 'IDF'], [kp])
                    tmb, tk = TMB.next()
                    CP('act', tmb[:], ps[:], [kp], [tk])
                    dst = TMS[idx, g * 512:(g + 1) * 512, hp * 128:(hp + 1) * 128].rearrange("(j p) c -> p j c", p=128)
                    P.dma('sp', dst, tmb[:].rearrange("p (j c) -> p j c", c=128), reads=[tk], writes=[('TMS', idx, hp)])
                    if hilo:
                        tml, tlk = TMB.next()
                        TT('dve', tml[:], ps[:], tmb[:], ALU.subtract, [kp, tk], [tlk])
                        dst2 = TMS[idx + 1, g * 512:(g + 1) * 512, hp * 128:(hp + 1) * 128].rearrange("(j p) c -> p j c", p=128)
                        P.dma('sp', dst2, tml[:].rearrange("p (j c) -> p j c", c=128), reads=[tlk], writes=[('TMS', idx + 1, hp)])

            def seq_store(dst_fn, arr, akey, wkey):
                for (t0, ln, po) in SEQP:
                    P.dma('sp', dst_fn(t0, ln), arr[:, po:po + ln], reads=[akey], writes=[wkey])

            for hp in range(8):
                P.dma('sp', Rb[:], RS[hp], writes=['Rb'])
                P.dma('sp', KBb[:], RS[8 + hp], writes=['KBb'])
                P.dma('sp', VBb[:], RS[16 + hp], writes=['VBb'])
                seq_store(lambda t0, ln: VBF[hp * 128:(hp + 1) * 128, t0:t0 + ln], VBb, 'VBb', ('VBF', hp))
                kko = PLAY['k_k'][0] + hp
                TS('dve', KKb[:, SL], KBb[:, SL], PAR[:, kko:kko + 1], None, ALU.mult, None, ['KBb', 'PAR'], ['KKb'])
                TT('dve', TA[:, SL], KKb[:, SL], KKb[:, SL], ALU.mult, ['KKb'], ['TA'])
                for (o, n) in PCS:
                    ps, kp = ps_next()
                    MM(ps[:, 0:n], BOF[:], TA[:, o:o + n], True, True, ['BOF', 'TA'], [kp])
                    ACTV(TBb[:, o:o + n], ps[:, 0:n], AF.Sqrt, [kp], ['TBb'])
                TS('dve', TBb[:, SL], TBb[:, SL], 1e-12, None, ALU.max, None, ['TBb'], ['TBb'])
                P.op('dve', lambda e: e.reciprocal(out=TBb[:, SL], in_=TBb[:, SL]), ['TBb'], ['TBb'])
                TT('dve', KKb[:, SL], KKb[:, SL], TBb[:, SL], ALU.mult, ['KKb', 'TBb'], ['KKb'])
                TS('dve', Ab[:, SL], KKb[:, SL], -1.0, None, ALU.mult, None, ['KKb'], ['Ab'])
                to_tm(Ab, 'Ab', 0, hp)
                to_tm(Rb, 'Rb', 1, hp)
                rko = PLAY['r_k'][0] + hp
                TT('dve', TA[:, SL], Rb[:, SL], KBb[:, SL], ALU.mult, ['Rb', 'KBb'], ['TA'])
                TS('dve', TA[:, SL], TA[:, SL], PAR[:, rko:rko + 1], None, ALU.mult, None, ['TA', 'PAR'], ['TA'])
                for (o, n) in PCS:
                    ps, kp = ps_next()
                    MM(ps[:, 0:n], BOF[:], TA[:, o:o + n], True, True, ['BOF', 'TA'], [kp])
                    TT('dve', TBb[:, o:o + n], ps[:, 0:n], VBb[:, o:o + n], ALU.mult, [kp, 'VBb'], ['TBb'])
                seq_store(lambda t0, ln: BON[hp, :, t0:t0 + ln], TBb, 'TBb', ('BON', hp))
                for (o, n) in PCS:
                    ps, kp = ps_next()
                    for kc in range(2):
                        MM(ps[:, 0:n], LORA[:, 2, kc, hp * 128:(hp + 1) * 128], SG[:, kc, o:o + n], kc == 0, kc == 1,
                           ['LORA', 'SG'], [kp])
                    CP('act', TA[:, o:o + n], ps[:, 0:n], [kp], ['TA'])
                seq_store(lambda t0, ln: GATE[hp, :, t0:t0 + ln], TA, 'TA', ('GATE', hp))
                for d in range(2):
                    w0o = PLAY['w0_%d' % d][0] + hp
                    a0o = PLAY['a0_%d' % d][0] + hp
                    kao = PLAY['k_a'][0] + hp
                    for (o, n) in PCS:
                        ps, kp = ps_next()
                        MM(ps[:, 0:n], LORA[0:96, 0, d, hp * 128:(hp + 1) * 128], TW[0:96, o:o + n], True, True,
                           ['LORA', 'TW'], [kp])
                        ACTV(WDb[:, o:o + n], ps[:, 0:n], AF.Sigmoid, [kp, 'PAR'], ['WDb'], bias=PAR[:, w0o:w0o + 1])
                        ps2, kp2 = ps_next()
                        MM(ps2[:, 0:n], LORA[0:96, 1, d, hp * 128:(hp + 1) * 128], XA[0:96, o:o + n], True, True,
                           ['LORA', 'XA'], [kp2])
                        ACTV(ALRb[:, o:o + n], ps2[:, 0:n], AF.Sigmoid, [kp2, 'PAR'], ['ALRb'], bias=PAR[:, a0o:a0o + 1])
                    ACTV(WDb[:, SL], WDb[:, SL], AF.Exp, ['WDb'], ['WDb'], scale=-math.exp(-0.5))
                    TS('dve', KDb[:, SL], ALRb[:, SL], PAR[:, kao:kao + 1], OMK[:, hp:hp + 1], ALU.mult, ALU.add,
                       ['ALRb', 'PAR', 'OMK'], ['KDb'])
                    TT('dve', KDb[:, SL], KDb[:, SL], KBb[:, SL], ALU.mult, ['KDb', 'KBb'], ['KDb'])
                    TT('dve', BDb[:, SL], KKb[:, SL], ALRb[:, SL], ALU.mult, ['KKb', 'ALRb'], ['BDb'])
                    to_tm(WDb, 'WDb', 2 + 4 * d, hp, hilo=True)
                    to_tm(BDb, 'BDb', 4 + 4 * d, hp)
                    to_tm(KDb, 'KDb', 5 + 4 * d, hp)
            P.barrier()
            cur.pop()

        chk(2)
        with ExitStack() as ph:
            cur.append(ph)
            ia = IDF[:]
            pstep = ia.ap[0][0]
            ia = IDB[:]
            pstep = ia.ap[0][0]
            SELM = sb("SELM", [128, 64, 128], BF16)
            for u_ in range(2):
                src_ = bass.AP(ia.tensor, ia.offset + 64 * u_, [[pstep, 128], [1, 64], [0, 64]])
                CP('dve', SELM[:, :, u_ * 64:(u_ + 1) * 64], src_, ['IDB'], ['SELM'])
            P.barrier()
            def tms_idx(slot, dr):
                return [0, 2 + 4 * dr, 3 + 4 * dr, 4 + 4 * dr, 5 + 4 * dr, 1][slot]

            def mk_tile(nm, si, dr, s0i, outi):
                return dict(nm=nm, off=SEQS[si][0], T=SEQS[si][1], dr=dr, s0i=s0i, outi=outi)
            lanes = [[mk_tile('Sf', 2, 0, 0, None)], [mk_tile('Sb', 2, 1, 1, None)],
                     [mk_tile('P0f', 0, 0, None, (0, 0)), mk_tile('P0b', 0, 1, None, (0, 1)),
                      mk_tile('P1f', 1, 0, None, (1, 0)), mk_tile('P1b', 1, 1, None, (1, 1))]]
            NL = len(lanes)
            LB = []
            lane_chunks = []
            for L, tiles in enumerate(lanes):
                nm = "L%d" % L
                lb = dict(nm=nm)
                lb['S'] = sb("S_" + nm, [128, 8, 64])
                lb['TMP'] = sb("TMP_" + nm, [128, 8, 64])
                lb['T2'] = sb("T2_" + nm, [128, 8, 64])
                lb['T3'] = sb("T3_" + nm, [128, 8, 64])
                lb['SA'] = sb("SA_" + nm, [128, 8])
                lb['WS'] = [sb("WS%d_%s" % (i, nm), [128, 8, 64]) for i in range(1)]
                lb['VV'] = [sb("VV%d_%s" % (i, nm), [128, 8, 64]) for i in range(1)]
                lb['YT'] = [sb("YT%d_%s" % (i, nm), [128, 8, 64]) for i in range(1)]
                lb['TMX'] = [sb("TMX%d_%s" % (i, nm), [128, 6, 512], BF16) for i in range(2)]
                LB.append(lb)
                chs = []
                for t in tiles:
                    for c in range(t['T'] // 64):
                        chs.append((t, c))
                assert len(chs) == 16
                lane_chunks.append(chs)

            def tok0_of(t, c):
                return t['off'] + (c * 64 if t['dr'] == 0 else t['T'] - 64 * (c + 1))

            def load_chunk(L, q):
                t, c = lane_chunks[L][q]
                lb = LB[L]
                b = q % 2
                tok0 = tok0_of(t, c)
                for ti in range(6):
                    idx = tms_idx(ti, t['dr'])
                    for u in range(2):
                        P.dma('sp' if u == 0 else 'act', lb['TMX'][b][u * 64:(u + 1) * 64, ti, :],
                              TMS[idx, tok0:tok0 + 64, u * 512:(u + 1) * 512], writes=[('TMX', L, b)])

            def load_vv(L, q):
                t, c = lane_chunks[L][q]
                lb = LB[L]
                tok0 = tok0_of(t, c)
                for u in range(2):
                    src = bass.AP(VBF.tensor, VBF.offset + (u * 512) * NT + tok0, [[NT, 64], [64 * NT, 8], [1, 64]])
                    P.dma('act', lb['VV'][0][u * 64:(u + 1) * 64, :, :], src, writes=[('VV', L, 0)])

            def emit_back(bk):
                (L, t, c, i, b, sel, pR, kR) = bk
                lb = LB[L]
                sk, tk = 'S_%d' % L, 'TMP_%d' % L
                TT('dve', lb['TMP'][:], lb['S'][:], pR, ALU.mult, [sk, kR], [tk])
                RED('dve', lb['YT'][0][:, :, sel], lb['TMP'][:], [tk], [('YT', L, 0)])
                if i == 63:
                    tok0 = tok0_of(t, c)
                    for u in range(2):
                        dst = bass.AP(YF.tensor, YF.offset + t['dr'] * 1024 * NT + (u * 512) * NT + tok0,
                                      [[NT, 64], [64 * NT, 8], [1, 64]])
                        P.dma('sp', dst, lb['YT'][0][u * 64:(u + 1) * 64, :, :], reads=[('YT', L, 0)])
                    if c == t['T'] // 64 - 1 and t['outi'] is not None:
                        pi, di = t['outi']
                        P.dma('sp', nst_d[pi, di], lb['S'][:].rearrange("p h k -> p (h k)"), reads=[sk])

            pend = [None] * NL
            scn = [0]
            for g in range(1024):
                q, i = divmod(g, 64)
                for L in range(NL):
                    t, c = lane_chunks[L][q]
                    lb = LB[L]
                    b = q % 2
                    sk = 'S_%d' % L
                    if i == 0:
                        if q == 0:
                            load_chunk(L, 0)
                        if q + 1 < 16:
                            load_chunk(L, q + 1)
                        load_vv(L, q)
                        if c == 0:
                            if pend[L] is not None:
                                emit_back(pend[L])
                                pend[L] = None
                            if t['s0i'] is not None:
                                P.dma('sp', lb['S'][:].rearrange("p h k -> p (h k)"), s0_d[t['s0i']], writes=[sk])
                            else:
                                MEMSET('dve', lb['S'][:], 0.0, [sk])
                    sel = i if t['dr'] == 0 else 63 - i
                    lhsT = SELM[:, sel, :]
                    S = lb['S']
                    pss = []
                    for slots in ((0,), (1, 2), (3,), (4,), (5,)):
                        if slots == (5,):
                            ps, kp = PS[5 + L], 'ps%d' % (5 + L)
                        else:
                            j5 = scn[0] % 5
                            scn[0] += 1
                            ps, kp = PS[j5], 'ps%d' % j5
                        for si_, sl_ in enumerate(slots):
                            MM(ps[:], lhsT, lb['TMX'][b][:, sl_, :], si_ == 0, si_ == len(slots) - 1,
                               [('TMX', L, b)], [kp])
                        pss.append((ps[:].rearrange("p (h k) -> p h k", k=64), kp))
                    (pA, kA), (pW, kW), (pB, kB), (pK, kK), (pR, kR) = pss
                    TMP, T2, T3, SA = lb['TMP'], lb['T2'], lb['T3'], lb['SA']
                    WS = lb['WS'][0]
                    wsk = ('WS', L, 0)
                    tk, t2k, t3k, sak = 'TMP_%d' % L, 'T2_%d' % L, 'T3_%d' % L, 'SA_%d' % L
                    CP('act', WS[:], pW, [kW], [wsk])
                    TT('dve', TMP[:], S[:], pA, ALU.mult, [sk, kA], [tk])
                    RED('dve', SA[:], TMP[:], [tk], [sak])
                    TT('dve', T2[:], pB, SA[:].unsqueeze(2).to_broadcast([128, 8, 64]), ALU.mult, [kB, sak], [t2k])
                    TT('dve', T3[:], pK, lb['VV'][0][:, :, sel:sel + 1].to_broadcast([128, 8, 64]), ALU.mult,
                       [kK, ('VV', L, 0)], [t3k])
                    TT('pool', S[:], S[:], WS[:], ALU.mult, [sk, wsk], [sk])
                    TT('pool', S[:], S[:], T2[:], ALU.add, [sk, t2k], [sk])
                    TT('pool', S[:], S[:], T3[:], ALU.add, [sk, t3k], [sk])
                    pend[L] = (L, t, c, i, b, sel, pR, kR)
                    nx = (L + 1) % NL
                    if pend[nx] is not None:
                        emit_back(pend[nx])
                        pend[nx] = None
            for L in range(NL):
                if pend[L] is not None:
                    emit_back(pend[L])
                    pend[L] = None
            P.barrier()
            cur.pop()
        chk(3)
        with ExitStack() as ph:
            cur.append(ph)
            OA = sb("OA", [128, 8, NT], BF16)
            OB = sb("OB", [128, 8, NT], BF16)
            with ExitStack() as p2:
                cur.append(p2)
                Y0 = sb("Y0", [128, NT])
                Y1 = sb("Y1", [128, NT])
                BNb = sb("BNb", [128, NT])
                GTb = sb("GTb", [128, NT])
                DDb = sb("DDb", [128, 512])
                SQb = sb("SQb", [128, 512])
                RSb = sb("RSb", [128, 512])
                for hp in range(8):
                    P.dma('sp', Y0[:], YF[0, hp * 128:(hp + 1) * 128, :], writes=['Y0'])
                    P.dma('act', Y1[:], YF[1, hp * 128:(hp + 1) * 128, :], writes=['Y1'])
                    P.dma('sp', BNb[:], BON[hp], writes=['BNb'])
                    P.dma('act', GTb[:], GATE[hp], writes=['GTb'])
                    TT('dve', Y0[:], Y0[:], Y1[:], ALU.add, ['Y0', 'Y1'], ['Y0'])
                    lgo = PLAY['lnx_g'][0] + hp
                    lbo = PLAY['lnx_b'][0] + hp
                    for tb in range(3):
                        ts_ = slice(tb * 512, (tb + 1) * 512)
                        ps, kp = ps_next()
                        MM(ps[:], BOF[:], Y0[:, ts_], True, True, ['BOF', 'Y0'], [kp])
                        STT('dve', DDb[:], ps[:], -1.0 / 64, Y0[:, ts_], ALU.mult, ALU.add, [kp, 'Y0'], ['DDb'])
                        TT('dve', SQb[:], DDb[:], DDb[:], ALU.mult, ['DDb'], ['SQb'])
                        ps2, kp2 = ps_next()
                        MM(ps2[:], BOF[:], SQb[:], True, True, ['BOF', 'SQb'], [kp2])
                        RSQ(RSb[:], ps2[:], 1.0 / 64, GN_EPS, [kp2], ['RSb'])
                        TT('dve', DDb[:], DDb[:], RSb[:], ALU.mult, ['DDb', 'RSb'], ['DDb'])
                        TS('dve', DDb[:], DDb[:], PAR[:, lgo:lgo + 1], PAR[:, lbo:lbo + 1], ALU.mult, ALU.add,
                           ['DDb', 'PAR'], ['DDb'])
                        TT('dve', DDb[:], DDb[:], BNb[:, ts_], ALU.add, ['DDb', 'BNb'], ['DDb'])
                        TT('dve', OB[:, hp, ts_], DDb[:], GTb[:, ts_], ALU.mult, ['DDb', 'GTb'], ['OB'])
                P.barrier()
                cur.pop()
            with ExitStack() as p2:
                cur.append(p2)
                Qh = sb("Qh", [128, NT], BF16)
                Kh = sb("Kh", [128, NT], BF16)
                VT = sb("VT", [128, 12, 128], BF16)
                CK = sb("CK", [128, 256], BF16)
                CVt = sb("CVt", [128, 2, 128], BF16)
                EX = Ring("EX", 3, [128, 512], BF16)
                R1 = sb("R1", [128, 512])
                O1 = sb("O1", [128, 512])
                O2 = sb("O2", [128, 512])
                SQa = sb("SQa", [128, 512])
                RSa = sb("RSa", [128, 512])
                for h in range(8):
                    P.dma('sp', Qh[:], QS[h], writes=['Qh'])
                    P.dma('act', Kh[:], QS[8 + h], writes=['Kh'])
                    P.dma('sp', VT[:], VS[h].rearrange("(j p) d -> p j d", p=128), writes=['VT'])
                    P.dma('pool', CK[:], cK_d[h], writes=['CK'])
                    P.dma('pool', CVt[:], cV_d[h].rearrange("(j p) d -> p j d", p=128), writes=['CVt'])
                    jobs = [(0, 256, [('n', 0), ('n', 1)]), (256, 256, [('n', 2), ('n', 3)]),
                            (512, 512, [('c', 0), ('c', 1)] + [('n', j) for j in range(4, 12)]),
                            (1024, 512, [('c', 0), ('c', 1)] + [('n', j) for j in range(4, 12)])]
                    for (q0, nq, kts) in jobs:
                        acc = []
                        for m in range(2):
                            psO, kO = PS[2 * m], 'ps%d' % (2 * m)
                            psD, kD = PS[2 * m + 1], 'ps%d' % (2 * m + 1)
                            for ki, (kind, j) in enumerate(kts):
                                if kind == 'n':
                                    ksrc = Kh[m * 64:(m + 1) * 64, j * 128:(j + 1) * 128]
                                    vsrc = VT[:, j, :]
                                    kr, vr = 'Kh', 'VT'
                                else:
                                    ksrc = CK[m * 64:(m + 1) * 64, j * 128:(j + 1) * 128]
                                    vsrc = CVt[:, j, :]
                                    kr, vr = 'CK', 'CVt'
                                psS, kS = ps_hi()
                                MM(psS[:, 0:nq], ksrc, Qh[m * 64:(m + 1) * 64, q0:q0 + nq], True, True, [kr, 'Qh'], [kS])
                                ex, ek = EX.next()
                                ACTV(ex[:, 0:nq], psS[:, 0:nq], AF.Exp, [kS], [ek], scale=0.125)
                                MM(psO[:, 0:nq], vsrc, ex[:, 0:nq], ki == 0, ki == len(kts) - 1, [vr, ek], [kO])
                                MM(psD[:, 0:nq], ONB[:], ex[:, 0:nq], ki == 0, ki == len(kts) - 1, ['ONB', ek], [kD])
                            acc.append((psO, kO, psD, kD))
                        (pO1, kO1, pD1, kD1), (pO2, kO2, pD2, kD2) = acc
                        n_ = slice(0, nq)
                        P.op('dve', lambda e, a=R1[:, n_], b=pD1[:, n_]: e.reciprocal(out=a, in_=b), [kD1], ['R1'])
                        TT('dve', O1[:, n_], pO1[:, n_], R1[:, n_], ALU.mult, [kO1, 'R1'], ['O1'])
                        P.op('dve', lambda e, a=R1[:, n_], b=pD2[:, n_]: e.reciprocal(out=a, in_=b), [kD2, 'O1'], ['R1'])
                        TT('dve', O2[:, n_], pO2[:, n_], R1[:, n_], ALU.mult, [kO2, 'R1'], ['O2'])
                        STT('dve', O1[:, n_], O2[:, n_], NEGLAM[:, 0:1], O1[:, n_], ALU.mult, ALU.add,
                            ['O2', 'O1', 'NEGLAM'], ['O1'])
                        TT('dve', SQa[:, n_], O1[:, n_], O1[:, n_], ALU.mult, ['O1'], ['SQa'])
                        psq, kq = ps_hi()
                        MM(psq[:, n_], ONF[:], SQa[:, n_], True, True, ['ONF', 'SQa'], [kq])
                        RSQ(RSa[:, n_], psq[:, n_], 1.0 / 128, LN_EPS, [kq], ['RSa'])
                        TT('dve', O1[:, n_], O1[:, n_], RSa[:, n_], ALU.mult, ['O1', 'RSa'], ['O1'])
                        TS('dve', OA[:, h, q0:q0 + nq], O1[:, n_], SUBG[:, 0:1], None, ALU.mult, None, ['O1', 'SUBG'], ['OA'])
                P.barrier()
                cur.pop()
            WB = Ring("WBo", 2, [128, KC, 128], BF16)
            for c in range(16):
                wb, wk = load_w(WB, wout_d[c])
                for tb in range(3):
                    ps, kp = ps_next()
                    for kc in range(KC):
                        src = OA if kc < 8 else OB
                        MM(ps[:], wb[:, kc, :], src[:, kc % 8, tb * 512:(tb + 1) * 512], kc == 0, kc == KC - 1,
                           [wk, 'OA', 'OB'], [kp])
                    resid_add(ps, kp, 0, 0, c, tb)
            P.barrier()
            cur.pop()

        chk(4)
        TBR = [[(0, 256, 1), (256, 256, 258)], [(0, 512, 515)], [(0, 512, 1027)]]

        def ffn(l):
            with ExitStack() as ph:
                cur.append(ph)
                H = sb("Hf%d" % l, [128, KC, NT], BF16)
                with ExitStack() as p2:
                    cur.append(p2)
                    SQ = Ring("SQf%d" % l, 2, [128, 512])
                    TMPN = Ring("TMPNf%d" % l, 2, [128, 512])
                    RSTD = sb("RSTDf%d" % l, [128, 512])
                    norm_mod(H, l, 1, SQ, RSTD, TMPN)
                    P.barrier()
                    cur.pop()
                G = 2
                WU = sb("WU%d" % l, [128, KC, 256], BF16)
                WDG = sb("WDG%d" % l, [128, G, D], BF16)
                H2G = sb("H2G%d" % l, [128, G, NPAD], BF16)
                UP = [sb("UP%d_%d" % (l, i), [128, NPAD]) for i in range(2)]
                CVb = [sb("CVb%d_%d" % (l, i), [128, NPAD]) for i in range(2)]
                for i in range(2):
                    MEMSET('dve', UP[i][:], 0.0, ['UP%d' % i])
                fo = PLAY['fdw%d' % l][0]
                SL = slice(1, 1539)
                c = 0
                while c < NFF:
                    gn = min(G, NFF - c)
                    for g in range(gn):
                        cc = c + g
                        P.dma('pool', WU[:], wup_d[l, cc], writes=['WU'])
                        P.dma('pool', WDG[:, g, :], wdn_d[l, cc], writes=[('WDG', g)])
                        for gv in range(2):
                            for tb in range(3):
                                ps, kp = ps_next()
                                for kc in range(KC):
                                    MM(ps[:], WU[:, kc, gv * 128:(gv + 1) * 128], H[:, kc, tb * 512:(tb + 1) * 512],
                                       kc == 0, kc == KC - 1, ['WU', ('H', tb)], [kp])
                                for (pc, n, po) in TBR[tb]:
                                    CP('act', UP[gv][:, po:po + n], ps[:, pc:pc + n], [kp], ['UP%d' % gv])
                            to = fo + (cc * 2 + gv) * 3
                            TS('dve', CVb[gv][:, SL], UP[gv][:, SL], PAR[:, to + 1:to + 2], None, ALU.mult, None,
                               ['UP%d' % gv, 'PAR'], ['CV%d' % gv])
                            STT('dve', CVb[gv][:, SL], UP[gv][:, 0:1538], PAR[:, to:to + 1], CVb[gv][:, SL], ALU.mult, ALU.add,
                                ['UP%d' % gv, 'PAR', 'CV%d' % gv], ['CV%d' % gv])
                            STT('dve', CVb[gv][:, SL], UP[gv][:, 2:1540], PAR[:, to + 2:to + 3], CVb[gv][:, SL], ALU.mult, ALU.add,
                                ['UP%d' % gv, 'PAR', 'CV%d' % gv], ['CV%d' % gv])
                        ACTV(CVb[0][:, SL], CVb[0][:, SL], AF.Silu, ['CV0'], ['CV0'])
                        TT('dve', H2G[:, g, SL], CVb[0][:, SL], CVb[1][:, SL], ALU.mult, ['CV0', 'CV1'], [('H2G', g)])
                    for ko in range(KC):
                        for tb in range(3):
                            ps, kp = ps_next()
                            for (pc, n, po) in TBR[tb]:
                                for g in range(gn):
                                    MM(ps[:, pc:pc + n], WDG[:, g, ko * 128:(ko + 1) * 128], H2G[:, g, po:po + n],
                                       g == 0, g == gn - 1, [('WDG', g), ('H2G', g)], [kp])
                            resid_add(ps, kp, l, 1, ko, tb)
                    c += gn
                P.barrier()
                cur.pop()

        ffn(0)

        chk(5)
        NC_ = NT + 60
        CO = [15, 286, 557]
        TBC = [[(0, 256, 15), (256, 256, 286)], [(0, 512, 557)], [(0, 512, 1069)]]
        with ExitStack() as ph:
            cur.append(ph)
            H = sb("Hc", [128, KC, NT], BF16)
            with ExitStack() as p2:
                cur.append(p2)
                SQ = Ring("SQc", 2, [128, 512])
                TMPN = Ring("TMPNc", 2, [128, 512])
                RSTD = sb("RSTDc", [128, 512])
                norm_mod(H, 1, 0, SQ, RSTD, TMPN)
                P.barrier()
                cur.pop()
            WBa = Ring("WBa", 2, [128, KC, 128], BF16)
            UA = sb("UA", [128, NC_])
            UBs = sb("UBs", [128, NC_])
            ACC = sb("ACC", [128, NC_])
            SQc = sb("SQc2", [128, NC_])
            MEAN = sb("MEAN", [128, 3, 512])
            RSC = sb("RSC", [128, 3, 512])
            MEMSET('dve', UA[:], 0.0, ['UA'])
            MEMSET('dve', UBs[:], 0.0, ['UBs'])
            wdo = PLAY['w_dw'][0]
            bdo = PLAY['b_dw'][0]
            CS = slice(15, 1581)
            for c in range(16):
                wa, wak = load_w(WBa, wpw1_d[c])
                wg, wgk = load_w(WBa, wpw1_d[16 + c])
                for tb in range(3):
                    psA, kA = PS[6], 'ps6'
                    psB, kB = PS[7], 'ps7'
                    for kc in range(KC):
                        MM(psA[:], wa[:, kc, :], H[:, kc, tb * 512:(tb + 1) * 512], kc == 0, kc == KC - 1, [wak, ('H', tb)], [kA])
                    for kc in range(KC):
                        MM(psB[:], wg[:, kc, :], H[:, kc, tb * 512:(tb + 1) * 512], kc == 0, kc == KC - 1, [wgk, ('H', tb)], [kB])
                    for (pc, n, po) in TBC[tb]:
                        ACTV(UBs[:, po:po + n], psB[:, pc:pc + n], AF.Sigmoid, [kB], ['UBs'])
                        TT('dve', UA[:, po:po + n], psA[:, pc:pc + n], UBs[:, po:po + n], ALU.mult, [kA, 'UBs'], ['UA'])
                TS('dve', ACC[:, CS], UA[:, 0:1566], PAR[:, wdo + c * 31:wdo + c * 31 + 1], PAR[:, bdo + c:bdo + c + 1],
                   ALU.mult, ALU.add, ['UA', 'PAR'], ['ACC'])
                for j in range(1, 31):
                    STT('dve', ACC[:, CS], UA[:, j:j + 1566], PAR[:, wdo + c * 31 + j:wdo + c * 31 + j + 1], ACC[:, CS],
                        ALU.mult, ALU.add, ['UA', 'PAR', 'ACC'], ['ACC'])
                TT('dve', SQc[:, CS], ACC[:, CS], ACC[:, CS], ALU.mult, ['ACC'], ['SQc'])
                for tb in range(3):
                    for (pc, n, po) in TBC[tb]:
                        MM(PS[tb][:, pc:pc + n], ONF[:], ACC[:, po:po + n], c == 0, c == 15, ['ONF', 'ACC'], ['ps%d' % tb])
                        MM(PS[3 + tb][:, pc:pc + n], ONF[:], SQc[:, po:po + n], c == 0, c == 15, ['ONF', 'SQc'], ['ps%d' % (3 + tb)])
                for (t0, ln, po) in [(0, 256, 15), (256, 256, 286), (512, 1024, 557)]:
                    P.dma('sp', CONV[c, :, t0:t0 + ln], ACC[:, po:po + ln], reads=['ACC'], writes=[('CONV', c)])
            for tb in range(3):
                TS('dve', MEAN[:, tb, :], PS[tb][:], 1.0 / D, None, ALU.mult, None, ['ps%d' % tb], ['MEAN'])
                TT('dve', RSC[:, tb, :], MEAN[:, tb, :], MEAN[:, tb, :], ALU.mult, ['MEAN'], ['RSC'])
                STT('dve', RSC[:, tb, :], PS[3 + tb][:], 1.0 / D, RSC[:, tb, :], ALU.mult, ALU.subtract,
                    ['ps%d' % (3 + tb), 'RSC'], ['RSC'])
                RSQ(RSC[:, tb, :], RSC[:, tb, :], 1.0, LN_EPS, ['RSC'], ['RSC'])
            CL = Ring("CL", 1, [128, NT])
            MV = MEAN[:].rearrange("p a b -> p (a b)")
            RV = RSC[:].rearrange("p a b -> p (a b)")
            cgo = PLAY['cln_g'][0]
            cbo = PLAY['cln_b'][0]
            for c in range(16):
                cl, ck = CL.next()
                P.dma('sp', cl[:], CONV[c], reads=[('CONV', c)], writes=[ck])
                TT('dve', cl[:], cl[:], MV, ALU.subtract, [ck, 'MEAN'], [ck])
                TT('dve', cl[:], cl[:], RV, ALU.mult, [ck, 'RSC'], [ck])
                TS('dve', cl[:], cl[:], PAR[:, cgo + c:cgo + c + 1], PAR[:, cbo + c:cbo + c + 1], ALU.mult, ALU.add,
                   [ck, 'PAR'], [ck])
                ACTV(H[:, c, :], cl[:], AF.Silu, [ck], [('H', 0), ('H', 1), ('H', 2)])
            for c in range(16):
                wb, wk = load_w(WBa, wpw2_d[c])
                for tb in range(3):
                    ps, kp = PS[6 + (tb % 2)], 'ps%d' % (6 + (tb % 2))
                    for kc in range(KC):
                        MM(ps[:], wb[:, kc, :], H[:, kc, tb * 512:(tb + 1) * 512], kc == 0, kc == KC - 1, [wk, ('H', tb)], [kp])
                    resid_add(ps, kp, 1, 0, c, tb)
            P.barrier()
            cur.pop()

        chk(6)
        ffn(1)
        chk(7)

        with ExitStack() as ph:
            cur.append(ph)
            SQ = Ring("SQz", 2, [128, 512])
            YO = Ring("YO", 3, [128, 512])
            RSTD = sb("RSTDz", [128, 512])
            fo_ = PLAY['fng'][0]
            for tb in range(3):
                psN, kN = ps_next()
                for kc in range(KC):
                    sq, sk = SQ.next()
                    ACTV(sq[:], X[:, kc, tb * 512:(tb + 1) * 512], AF.Square, [('X', tb)], [sk])
                    MM(psN[:], ONF[:], sq[:], kc == 0, kc == KC - 1, [sk, 'ONF'], [kN])
                RSQ(RSTD[:], psN[:], 1.0 / D, RMS_EPS, [kN], ['RSTDz'])
                for kc in range(KC):
                    yo, yk = YO.next()
                    TT('dve', yo[:], X[:, kc, tb * 512:(tb + 1) * 512], RSTD[:], ALU.mult, [('X', tb), 'RSTDz'], [yk])
                    TS('dve', yo[:], yo[:], PAR[:, fo_ + kc:fo_ + kc + 1], None, ALU.mult, None, [yk, 'PAR'], [yk])
                    P.dma('sp', y_d[kc * 128:(kc + 1) * 128, tb * 512:(tb + 1) * 512], yo[:], reads=[yk])
            cur.pop()
        if debug:
            print('instr counts', {k: len(v) for k, v in P.prog.items()}, 'sems', P.nsem)
        P.finish()
    return nc


def _fm(v, n=None):
    v = np.asarray(v, np.float32).reshape(-1)
    return np.ascontiguousarray(v.reshape(-1, 128).T)


def _arr_w(W, cols=None):
    K, N = W.shape
    if cols is None:
        Wc = W.reshape(K // 128, 128, N // 128, 128)
        return np.ascontiguousarray(Wc.transpose(2, 1, 0, 3))
    out = np.zeros((len(cols), 128, K // 128, 128), np.float32)
    for c, cl in enumerate(cols):
        cl = np.asarray(cl)
        m = cl >= 0
        blk = np.zeros((K, 128), np.float32)
        blk[:, m] = W[:, cl[m]]
        out[c] = blk.reshape(K // 128, 128, 128).transpose(1, 0, 2)
    return out


def prep_shared(inp):
    f = lambda k: np.asarray(inp[k], np.float32)
    sh = {}
    pars = np.zeros((128, NPAR), np.float32)

    def put(name, a):
        o, w = PLAY[name]
        a = np.asarray(a, np.float32)
        assert a.shape == (128, w), (name, a.shape, w)
        pars[:, o:o + w] = a
    b_ada = f('b_ada')
    put('b_ada0', _fm(b_ada[0]))
    put('b_ada1', _fm(b_ada[1]))
    ng = f('norm_g')
    for l in range(2):
        for wh in range(2):
            put('ng%d%d' % (l, wh), _fm(ng[l, wh]))
    put('fng', _fm(f('final_norm_g')))
    mu = f('shift_mu')[0]
    for i in range(2):
        m = np.zeros((128, 28), np.float32)
        m[:, 0:24] = _fm(mu[i, 0:3072])
        m[:96, 24] = mu[i, 3072:3168]
        m[:96, 25] = mu[i, 3168:3264]
        m[:, 26:28] = _fm(mu[i, 3264:3520])
        put('mu%d' % i, m)
    for d in range(2):
        put('w0_%d' % d, _fm(f('w0')[0, d]))
        put('a0_%d' % d, _fm(f('a0')[0, d]))
    put('k_k', _fm(f('k_k')[0]))
    put('k_a', _fm(f('k_a')[0]))
    put('r_k', _fm(f('r_k')[0].reshape(-1)))
    put('lnx_g', _fm(f('lnx_g')[0]))
    put('lnx_b', _fm(f('lnx_b')[0]))
    put('subln', f('subln_g')[0].reshape(128, 1))
    put('dl', np.broadcast_to(f('diff_lambda')[0].reshape(1, 256), (128, 256)))
    put('b_dw', _fm(f('b_dw')[0]))
    put('cln_g', _fm(f('cln_g')[0]))
    put('cln_b', _fm(f('cln_b')[0]))
    wdw = f('w_dw')[0]
    put('w_dw', np.ascontiguousarray(wdw.reshape(31, 16, 128).transpose(2, 1, 0)).reshape(128, 496))
    fd = f('w_ffn_dw')
    for l in range(2):
        a = fd[l].reshape(3, 2, NFF, 128).transpose(3, 2, 1, 0)
        put('fdw%d' % l, np.ascontiguousarray(a).reshape(128, 258))
    sh['pars'] = pars
    sh['ident'] = np.eye(128, dtype=np.float32)
    bo = np.zeros((128, 128), np.float32)
    bo[:64, :64] = 1
    bo[64:, 64:] = 1
    sh['bones'] = bo
    rt = np.zeros((128, 128), np.float32)
    ang = np.zeros((128, 1024), np.float64)
    tt = np.arange(1024)
    row = (tt // 64).astype(np.float64)
    col = (tt % 64).astype(np.float64)
    inv = (10000.0 ** (-np.arange(16, dtype=np.float32) / 16)).astype(np.float32).astype(np.float64)
    for m in range(2):
        for d in range(64):
            half, r = divmod(d, 32)
            if r < 16:
                partner, sgn, fi = d + 16, -1.0, r
            else:
                partner, sgn, fi = d - 16, 1.0, r - 16
            rt[m * 64 + partner, m * 64 + d] = sgn
            pos = row if half == 0 else col
            ang[m * 64 + d] = (pos.astype(np.float32) * np.float32(inv[fi])).astype(np.float64)
    sh['rt'] = rt
    sh['cos'] = np.cos(ang).astype(np.float32)
    sh['sin'] = np.sin(ang).astype(np.float32)
    wa = f('w_ada')
    sh['wada'] = np.ascontiguousarray(wa.reshape(2, KC, 128, 24, 512).transpose(0, 3, 2, 1, 4))
    sh['win'] = _arr_w(f('w_in')[0], win_cols())
    lora = np.zeros((128, 3, 2, 1024), np.float32)
    lora[:96, 0] = f('w2')[0].transpose(1, 0, 2)
    lora[:96, 1] = f('a2')[0].transpose(1, 0, 2)
    lora[:, 2] = f('g2')[0].reshape(2, 128, 1024).transpose(1, 0, 2)
    sh['lora'] = lora
    sh['wout'] = _arr_w(f('w_out')[0])
    sh['wpw1'] = _arr_w(f('w_pw1')[0])
    sh['wpw2'] = _arr_w(f('w_pw2')[0])
    wu = f('w_up')
    a = wu.reshape(2, KC, 128, 2, NFF, 128).transpose(0, 4, 2, 1, 3, 5)
    sh['wup'] = np.ascontiguousarray(a).reshape(2, NFF, 128, KC, 256)
    sh['wdn'] = np.ascontiguousarray(f('w_down').reshape(2, NFF, 128, D))
    return sh


def prep_core(inp, i):
    f = lambda k: np.asarray(inp[k], np.float32)
    m = {}
    xp = f('x_prompt')[2 * i:2 * i + 2].reshape(512, D)
    xs = f('x_sample')[i]
    m['xin'] = np.ascontiguousarray(np.concatenate([xp, xs], 0).T)
    cf = np.stack([f('c')[i], f('c_ctx')], -1)
    m['cfm'] = np.ascontiguousarray(cf.reshape(KC, 128, 2).transpose(1, 0, 2))
    ck = f('cache_k')[i, 0]
    m['cK'] = np.ascontiguousarray(ck.transpose(0, 1, 3, 2)).reshape(8, 128, 256)
    m['cV'] = np.ascontiguousarray(f('cache_v')[i, 0])
    s0 = np.stack([f('state_wkv_fwd')[i, 0], f('state_wkv_bwd')[i, 0]], 0)
    m['s0'] = np.ascontiguousarray(s0.reshape(2, 2, 8, 64, 64).transpose(0, 1, 3, 2, 4)).reshape(2, 128, 512)
    return m


_NC_CACHE = {}


def kernel(**inputs):
    sh = prep_shared(inputs)
    in_maps = []
    for i in range(8):
        m = dict(sh)
        m.update(prep_core(inputs, i))
        in_maps.append(m)
    if 'nc' not in _NC_CACHE:
        _NC_CACHE['nc'] = build()
    nc = _NC_CACHE['nc']
    res = run_bass_kernel_spmd(nc, in_maps, core_ids=list(range(8)))
    yp = np.zeros((16, 256, D), np.float32)
    ys = np.zeros((8, 1024, D), np.float32)
    nk = np.zeros((16, 1, 8, 2, 256, 64), np.float32)
    nv = np.zeros((16, 1, 8, 256, 128), np.float32)
    sf = np.zeros((16, 1, 16, 64, 64), np.float32)
    sbw = np.zeros((16, 1, 16, 64, 64), np.float32)
    for i in range(8):
        r = res.results[i]
        y = np.asarray(r['y'], np.float32).T
        yp[2 * i:2 * i + 2] = y[:512].reshape(2, 256, D)
        ys[i] = y[512:]
        k = np.asarray(r['nk'], np.float32).reshape(2, 8, 2, 64, 256)
        nk[2 * i:2 * i + 2, 0] = k.transpose(0, 1, 2, 4, 3)
        nv[2 * i:2 * i + 2, 0] = np.asarray(r['nv'], np.float32)
        s = np.asarray(r['nst'], np.float32).reshape(2, 2, 2, 64, 8, 64)
        s = s.transpose(0, 1, 2, 4, 3, 5).reshape(2, 2, 16, 64, 64)
        sf[2 * i:2 * i + 2, 0] = s[:, 0]
        sbw[2 * i:2 * i + 2, 0] = s[:, 1]
    return (yp, ys, nk, nv, sf, sbw)
```

```python
import math
import numpy as np
import concourse.bass as bass
import concourse.mybir as mybir
from concourse.bass_utils import run_bass_kernel_spmd
from contextlib import ExitStack

F32 = mybir.dt.float32
BF16 = mybir.dt.bfloat16
ALU = mybir.AluOpType
AF = mybir.ActivationFunctionType
AX = mybir.AxisListType

D = 2048
KC = 16
NT = 1536
NPAD = NT + 4
SEQS = [(0, 256), (256, 256), (512, 1024)]
POFF = [1, 258, 515]
D_FF = 5504
NFF = 43
LAM_INIT = 0.8 - 0.6 * math.exp(0.0)
RMS_EPS = 1e-6
LN_EPS = 1e-5
GN_EPS = 64e-5
N_WIN = 52


class Prog:
    NS = 8

    def __init__(self, nc, stack):
        self.nc = nc
        self.stack = stack
        self.ce = ['pe', 'dve', 'act', 'pool']
        self.engs = ['pe', 'dve', 'act', 'pool', 'sp']
        self.prog = {e: [] for e in self.engs}
        self.sems = {}
        self.cnt = {}
        self.waited = {e: {} for e in self.engs}
        self.lastw = {}
        self.readers = {}
        self.ndma = {e: 0 for e in self.engs}
        self.epoch = 0
        self.nsem = 0
        self.dead = False
        for e in self.ce:
            self._mk(('c', e, 0))

    def _mk(self, key):
        self.sems[key] = self.stack.enter_context(self.nc.semaphore("s%d" % self.nsem))
        self.nsem += 1
        self.cnt[key] = 0

    def _deps(self, reads, writes):
        deps = {}
        for r in reads:
            lw = self.lastw.get(r)
            if lw is not None and deps.get(lw[0], 0) < lw[1]:
                deps[lw[0]] = lw[1]
            if isinstance(r, str) and r.startswith('ps'):
                for sk, v in self.readers.get(r, {}).items():
                    if deps.get(sk, 0) < v:
                        deps[sk] = v
        for w in writes:
            lw = self.lastw.get(w)
            if lw is not None and deps.get(lw[0], 0) < lw[1]:
                deps[lw[0]] = lw[1]
            for sk, v in self.readers.get(w, {}).items():
                if deps.get(sk, 0) < v:
                    deps[sk] = v
        return deps

    def _waits(self, eng, deps, skip=None):
        waits = []
        wd = self.waited[eng]
        for sk, v in deps.items():
            if sk == skip:
                continue
            if wd.get(sk, 0) < v:
                waits.append((self.sems[sk], v))
                wd[sk] = v
        return waits

    def _record(self, sk, val, reads, writes):
        for r in reads:
            d = self.readers.setdefault(r, {})
            if d.get(sk, 0) < val:
                d[sk] = val
        for w in writes:
            self.lastw[w] = (sk, val)
            self.readers[w] = {}

    def op(self, eng, fn, reads=(), writes=()):
        if self.dead:
            return
        own = ('c', eng, self.epoch)
        deps = self._deps(reads, writes)
        waits = self._waits(eng, deps, skip=own if eng == 'pe' else None)
        self.cnt[own] += 1
        val = self.cnt[own]
        sem = self.sems[own]

        def emit(e):
            for s, v in waits:
                e.wait_ge(s, v)
            fn(e).then_inc(sem, 1)
        self.prog[eng].append(emit)
        self._record(own, val, reads, writes)

    def dma(self, eng, out, in_, reads=(), writes=(), **kw):
        if self.dead:
            return
        i = self.ndma[eng]
        self.ndma[eng] += 1
        sk = ('d', eng, i % self.NS)
        if sk not in self.sems:
            self._mk(sk)
        target = 16 * (i // self.NS + 1)
        deps = self._deps(reads, writes)
        if target > 16 and deps.get(sk, 0) < target - 16:
            deps[sk] = target - 16
        waits = self._waits(eng, deps)
        self.cnt[sk] = target
        sem = self.sems[sk]

        def emit(e):
            for s, v in waits:
                e.wait_ge(s, v)
            e.dma_start(out=out, in_=in_, **kw).then_inc(sem, 16)
        self.prog[eng].append(emit)
        self._record(sk, target, reads, writes)

    def _finals(self):
        return {sk: c for sk, c in self.cnt.items()
                if c > 0 and (sk[0] == 'd' or sk[2] == self.epoch)}

    def barrier(self):
        if self.dead:
            return
        finals = self._finals()
        for eng in self.engs:
            waits = self._waits(eng, finals, skip=('c', eng, self.epoch))
            if waits:
                def emit(e, waits=waits):
                    for s, v in waits:
                        e.wait_ge(s, v)
                self.prog[eng].append(emit)
        self.epoch += 1
        for e in self.ce:
            self._mk(('c', e, self.epoch))
        self.lastw = {}
        self.readers = {}

    def finish(self):
        finals = self._finals()
        for eng in self.engs:
            waits = self._waits(eng, finals, skip=('c', eng, self.epoch))

            def emit(e, waits=waits):
                for s, v in waits:
                    e.wait_ge(s, v)
            self.prog[eng].append(emit)
        prog = self.prog
        with self.nc.Block() as block:
            @block.tensor
            def _(e):
                for f in prog['pe']:
                    f(e)

            @block.vector
            def _(e):
                for f in prog['dve']:
                    f(e)

            @block.scalar
            def _(e):
                for f in prog['act']:
                    f(e)

            @block.gpsimd
            def _(e):
                for f in prog['pool']:
                    f(e)

            @block.sync
            def _(e):
                for f in prog['sp']:
                    f(e)


def par_layout():
    ents = [('b_ada0', 96), ('b_ada1', 96), ('ng00', 16), ('ng01', 16), ('ng10', 16), ('ng11', 16),
            ('fng', 16), ('mu0', 28), ('mu1', 28), ('w0_0', 8), ('w0_1', 8), ('a0_0', 8), ('a0_1', 8),
            ('k_k', 8), ('k_a', 8), ('r_k', 8), ('lnx_g', 8), ('lnx_b', 8), ('subln', 1), ('dl', 256),
            ('b_dw', 16), ('cln_g', 16), ('cln_b', 16), ('w_dw', 496), ('fdw0', 258), ('fdw1', 258)]
    lay = {}
    o = 0
    for n, w in ents:
        lay[n] = (o, w)
        o += w
    return lay, o


PLAY, NPAR = par_layout()


def win_cols():
    ch = []
    for h in range(8):
        ch.append(list(range(h * 128, h * 128 + 128)))
    for h in range(8):
        ch.append(list(range(1024 + h * 128, 1024 + h * 128 + 128)))
    for h in range(8):
        ch.append(list(range(2048 + h * 128, 2048 + h * 128 + 128)))
    ch.append(list(range(6144, 6240)) + [-1] * 32)
    ch.append(list(range(6240, 6336)) + [-1] * 32)
    ch.append(list(range(6336, 6464)))
    ch.append(list(range(6464, 6592)))
    for hp in range(8):
        for base in (3072, 4096, 5120):
            ch.append(list(range(base + hp * 128, base + hp * 128 + 128)))
    return ch


def rest_idx(c):
    if c == 24:
        return 24
    if c == 25:
        return 25
    if c in (26, 27):
        return c
    hp, t = divmod(c - 28, 3)
    return t * 8 + hp


def build(debug=0, stop=99, lite=0, sub=None):
    nc = bass.Bass('TRN2', target_bir_lowering=False)

    def din(name, shape, dt=F32):
        return nc.dram_tensor(name, list(shape), dt, kind="ExternalInput").ap()

    def dout(name, shape, dt=F32):
        return nc.dram_tensor(name, list(shape), dt, kind="ExternalOutput").ap()

    def dscr(name, shape, dt=F32):
        return nc.dram_tensor(name, list(shape), dt, kind="ExternalOutput" if debug else "Internal").ap()

    xin = din("xin", [D, NT])
    cfm_d = din("cfm", [128, KC, 2])
    pars_d = din("pars", [128, NPAR])
    ident_d = din("ident", [128, 128])
    bones_d = din("bones", [128, 128])
    rt_d = din("rt", [128, 128])
    cos_d = din("cos", [128, 1024])
    sin_d = din("sin", [128, 1024])
    cK_d = din("cK", [8, 128, 256])
    cV_d = din("cV", [8, 256, 128])
    s0_d = din("s0", [2, 128, 512])
    wada_d = din("wada", [2, 24, 128, KC, 512] if not lite else [1, 1, 128, 1, 512])
    win_d = din("win", [N_WIN, 128, KC, 128])
    lora_d = din("lora", [128, 3, 2, 1024])
    wout_d = din("wout", [16, 128, KC, 128] if not (lite and stop < 4) else [1, 1, 128, 1, 128])
    wpw1_d = din("wpw1", [32, 128, KC, 128] if not (lite and stop < 6) else [1, 1, 128, 1, 128])
    wpw2_d = din("wpw2", [16, 128, KC, 128] if not (lite and stop < 6) else [1, 1, 128, 1, 128])
    wup_d = din("wup", [2, NFF, 128, KC, 256] if not (lite and stop < 5) else [1, 1, 128, 1, 128])
    wdn_d = din("wdn", [2, NFF, 128, D] if not (lite and stop < 5) else [1, 1, 128, 1, 128])
    if lite:
        class _Dm:
            def __getitem__(self, k):
                return self

            def __getattr__(self, n):
                return lambda *a, **k: self
        if stop < 4:
            wout_d = _Dm()
        if stop < 5:
            wup_d = wdn_d = _Dm()
        if stop < 6:
            wpw1_d = wpw2_d = _Dm()
    y_d = dout("y", [D, NT])
    nk_d = dout("nk", [2, 8, 128, 256])
    nv_d = dout("nv", [2, 8, 256, 128])
    nst_d = dout("nst", [2, 2, 128, 512])
    QS = dscr("QS", [16, 128, NT], BF16)
    VS = dscr("VS", [8, NT, 128], BF16)
    TMS = dscr("TMS", [10, NT, 1024], BF16)
    VBF = dscr("VBF", [1024, NT])
    BON = dscr("BON", [8, 128, NT])
    GATE = dscr("GATE", [8, 128, NT])
    YF = dscr("YF", [2, 1024, NT])
    CONV = dscr("CONV", [16, 128, NT])
    dbg = {}

    with ExitStack() as st:
        P = Prog(nc, st)

        cur = [st]

        def sb(name, shape, dt=F32):
            return cur[-1].enter_context(nc.sbuf_tensor(name, list(shape), dt))

        PS = [st.enter_context(nc.psum_tensor("ps%d" % i, [128, 512], F32)) for i in range(8)]
        psn = [0]

        psh = [0]

        def ps_hi():
            i = 4 + psh[0] % 4
            psh[0] += 1
            return PS[i], 'ps%d' % i

        def ps_next():
            i = psn[0] % 8
            psn[0] += 1
            return PS[i], 'ps%d' % i

        class Ring:
            def __init__(self, name, n, shape, dt=F32):
                self.t = [sb("%s%d" % (name, i), shape, dt) for i in range(n)]
                self.k = ["%s%d" % (name, i) for i in range(n)]
                self.i = 0

            def next(self):
                j = self.i % len(self.t)
                self.i += 1
                return self.t[j], self.k[j]

        def TT(eng, out, in0, in1, op, r, w):
            P.op(eng, lambda e: e.tensor_tensor(out=out, in0=in0, in1=in1, op=op), r, w)

        def TS(eng, out, in0, s1, s2, op0, op1, r, w):
            if s2 is None:
                P.op(eng, lambda e: e.tensor_single_scalar(out=out, in_=in0, scalar=s1, op=op0), r, w)
            else:
                P.op(eng, lambda e: e.tensor_scalar(out=out, in0=in0, scalar1=s1, scalar2=s2, op0=op0, op1=op1), r, w)

        def STT(eng, out, in0, scalar, in1, op0, op1, r, w):
            P.op(eng, lambda e: e.scalar_tensor_tensor(out=out, in0=in0, scalar=scalar, in1=in1, op0=op0, op1=op1), r, w)

        def ACTV(out, in_, func, r, w, bias=None, scale=None):
            kw = {}
            if bias is not None:
                kw['bias'] = bias
            if scale is not None:
                kw['scale'] = scale
            P.op('act', lambda e: e.activation(out=out, in_=in_, func=func, **kw), r, w)

        EPSI = {RMS_EPS: 0, LN_EPS: 1, GN_EPS: 2}

        def RSQ(out, in_, scale, eps, r, w):
            j = EPSI[eps]
            ACTV(out, in_, AF.Ln, list(r) + ['EPS'], w, bias=EPS[:, j:j + 1], scale=scale)
            ACTV(out, out, AF.Exp, w, w, scale=-0.5)

        def CP(eng, out, in_, r, w):
            if eng == 'act':
                P.op('act', lambda e: e.copy(out=out, in_=in_), r, w)
            elif 'PSum' in type(in_.tensor).__name__:
                P.op(eng, lambda e: e.tensor_single_scalar(out=out, in_=in_, scalar=1.0, op=ALU.mult), r, w)
            else:
                P.op(eng, lambda e: e.tensor_copy(out=out, in_=in_), r, w)

        def MM(out, lhsT, rhs, start, stop, r, w):
            P.op('pe', lambda e: e.matmul(out, lhsT=lhsT, rhs=rhs, start=start, stop=stop), r, w)

        def RED(eng, out, in_, r, w):
            P.op(eng, lambda e: e.tensor_reduce(out=out, in_=in_, axis=AX.X, op=ALU.add), r, w)

        def MEMSET(eng, ap, val, w):
            P.op(eng, lambda e: e.memset(ap, val), (), w)

        def chk(k):
            if stop <= k and not P.dead:
                if debug:
                    dx = dout("dbgX", [128, KC, NT])
                    P.dma('sp', dx, X[:], reads=[('X', 0), ('X', 1), ('X', 2)])
                P.dead = True

        X = sb("X", [128, KC, NT])
        PAR = sb("PAR", [128, NPAR])
        IDF = sb("IDF", [128, 128])
        IDB = sb("IDB", [128, 128], BF16)
        ONF = sb("ONF", [128, 128])
        ONB = sb("ONB", [128, 128], BF16)
        BOF = sb("BOF", [128, 128])
        MOD = sb("MOD", [128, 2, 96, 2])
        GG = sb("GG", [128, 2, 2, KC, 2])
        NEGLAM = sb("NEGLAM", [128, 1])
        SUBG = sb("SUBG", [128, 1])
        CM = sb("CM", [128, 28])
        EPS = sb("EPS", [128, 4])

        def par(name, j0=0, n=None):
            o, w = PLAY[name]
            if n is None:
                n = w - j0
            return PAR[:, o + j0:o + j0 + n]

        P.dma('sp', PAR[:], pars_d, writes=['PAR'])
        P.dma('sp', IDF[:], ident_d, writes=['IDF'])
        P.dma('pool', IDB[:], ident_d, writes=['IDB'])
        P.dma('sp', BOF[:], bones_d, writes=['BOF'])
        MEMSET('dve', ONF[:], 1.0, ['ONF'])
        for eps_, j_ in EPSI.items():
            MEMSET('dve', EPS[:, j_:j_ + 1], eps_, ['EPS'])
        MEMSET('dve', ONB[:], 1.0, ['ONB'])
        for kc in range(KC):
            P.dma('sp', X[:, kc, :], xin[kc * 128:(kc + 1) * 128, :], writes=[('X', 0), ('X', 1), ('X', 2)])

        with ExitStack() as ph:
            cur.append(ph)
            sbp = sb
            SC = sbp("SC", [128, KC, 2])
            WA = [sbp("WA%d" % i, [128, KC, 512]) for i in range(2)]
            P.dma('sp', SC[:], cfm_d, writes=['SC'])
            ACTV(SC[:], SC[:], AF.Silu, ['SC'], ['SC'])
            psM, kM = PS[0], 'ps0'
            for l in range(2):
                for jb in range(24 if not lite else 0):
                    wa = WA[jb % 2]
                    wk = 'WA%d' % (jb % 2)
                    P.dma('sp' if jb % 2 == 0 else 'act', wa[:], wada_d[l, jb], writes=[wk])
                    for jj in range(4):
                        j = jb * 4 + jj
                        for kc in range(KC):
                            MM(psM[:, 2 * j:2 * j + 2], wa[:, kc, jj * 128:(jj + 1) * 128], SC[:, kc, :],
                               kc == 0, kc == KC - 1, [wk, 'SC'], [kM])
                bo = PLAY['b_ada%d' % l][0]
                if lite:
                    MEMSET('dve', MOD[:, l], 0.05, ['MOD'])
                else:
                    TT('dve', MOD[:, l], psM[:, 0:192].rearrange("p (j s) -> p j s", s=2),
                       PAR[:, bo:bo + 96].unsqueeze(2).to_broadcast([128, 96, 2]), ALU.add, [kM, 'PAR'], ['MOD'])
                for wh in range(2):
                    go = PLAY['ng%d%d' % (l, wh)][0]
                    STT('dve', GG[:, l, wh], MOD[:, l, (1 + 3 * wh) * 16:(2 + 3 * wh) * 16, :], 1.0,
                        PAR[:, go:go + 16].unsqueeze(2).to_broadcast([128, 16, 2]), ALU.add, ALU.mult,
                        ['MOD', 'PAR'], ['GG'])
            DT = sbp("DT", [128, 2, 64])
            DS = sbp("DS", [128, 2])
            dlo = PLAY['dl'][0]
            dlv = PAR[:, dlo:dlo + 256].rearrange("p (a b k) -> p a b k", a=2, b=2)
            TT('dve', DT[:], dlv[:, :, 0, :], dlv[:, :, 1, :], ALU.mult, ['PAR'], ['DT'])
            RED('dve', DS[:], DT[:], ['DT'], ['DS'])
            ACTV(DS[:], DS[:], AF.Exp, ['DS'], ['DS'])
            TT('dve', NEGLAM[:], DS[:, 1:2], DS[:, 0:1], ALU.subtract, ['DS'], ['NEGLAM'])
            TS('dve', NEGLAM[:], NEGLAM[:], -LAM_INIT, None, ALU.add, None, ['NEGLAM'], ['NEGLAM'])
            TS('dve', SUBG[:], par('subln'), 1.0 - LAM_INIT, None, ALU.mult, None, ['PAR'], ['SUBG'])
            TT('dve', CM[:], par('mu0'), par('mu1'), ALU.add, ['PAR'], ['CM'])
            TS('dve', CM[:], CM[:], -1.0, 1.0, ALU.mult, ALU.add, ['CM'], ['CM'])
            P.barrier()
            cur.pop()

        def norm_mod(H, l, wh, SQ, RSTD, TMPN):
            for tb in range(3):
                s = 1 if tb == 0 else 0
                psN, kN = ps_next()
                for kc in range(KC):
                    sq, sk = SQ.next()
                    ACTV(sq[:], X[:, kc, tb * 512:(tb + 1) * 512], AF.Square, [('X', tb)], [sk])
                    MM(psN[:], ONF[:], sq[:], kc == 0, kc == KC - 1, [sk, 'ONF'], [kN])
                RSQ(RSTD[:], psN[:], 1.0 / D, RMS_EPS, [kN], ['RSTD'])
                for kc in range(KC):
                    tm, tk = TMPN.next()
                    TT('dve', tm[:], X[:, kc, tb * 512:(tb + 1) * 512], RSTD[:], ALU.mult, [('X', tb), 'RSTD'], [tk])
                    ACTV(H[:, kc, tb * 512:(tb + 1) * 512], tm[:], AF.Identity, [tk, 'GG', 'MOD'], [('H', tb)],
                         bias=MOD[:, l, (3 * wh) * 16 + kc, s:s + 1], scale=GG[:, l, wh, kc, s:s + 1])

        def resid_add(ps, kps, l, wh, kc, tb):
            s = 1 if tb == 0 else 0
            xs = X[:, kc, tb * 512:(tb + 1) * 512]
            STT('dve', xs, ps[:], MOD[:, l, (2 + 3 * wh) * 16 + kc, s:s + 1], xs, ALU.mult, ALU.add,
                [kps, 'MOD', ('X', tb)], [('X', tb)])


        def pad_off(t):
            return t + 1 if t < 256 else (t + 2 if t < 512 else t + 3)
        PCS = [(1, 385), (386, 385), (771, 384), (1155, 384)]
        SEQP = [(0, 256, 1), (256, 256, 258), (512, 1024, 515)]
        RS = dscr("RS", [28, 128, NPAD])

        def load_w(WBr, src, eng='pool'):
            wb, wk = WBr.next()
            P.dma(eng, wb[:], src, writes=[wk])
            return wb, wk

        with ExitStack() as ph:
            cur.append(ph)
            H = sb("H", [128, KC, NT], BF16)
            SQ = Ring("SQ", 2, [128, 512])
            TMPN = Ring("TMPN", 2, [128, 512])
            RSTD = sb("RSTD", [128, 512])
            norm_mod(H, 0, 0, SQ, RSTD, TMPN)
            WB = Ring("WB", 2, [128, KC, 128], BF16)
            COS = sb("COS", [128, 1024])
            SIN = sb("SIN", [128, 1024])
            RTF = sb("RTF", [128, 128])
            P.dma('sp', COS[:], cos_d, writes=['COS'])
            P.dma('sp', SIN[:], sin_d, writes=['SIN'])
            P.dma('sp', RTF[:], rt_d, writes=['RTF'])
            RAWP = sb("RAWP", [128, NPAD])
            OUTP = sb("OUTP", [128, NPAD])
            MEMSET('dve', RAWP[:], 0.0, ['RAWP'])
            QB = Ring("QB", 2, [128, 512], BF16)
            XS = Ring("XS", 2, [128, 512])
            T1 = Ring("T1", 2, [128, 512])
            T2 = Ring("T2", 2, [128, 512])
            HK = [('H', 0), ('H', 1), ('H', 2)]

            def proj_tb(wb, wk, tb):
                ps, kp = ps_next()
                for kc in range(KC):
                    MM(ps[:], wb[:, kc, :], H[:, kc, tb * 512:(tb + 1) * 512], kc == 0, kc == KC - 1,
                       [wk, ('H', tb)], [kp])
                return ps, kp

            SUB = sub if sub is not None else 'qvr'
            for c in range(16 if 'q' in SUB else 0):
                wb, wk = load_w(WB, win_d[c])
                if 'L' in SUB:
                    continue
                for tb in range(3):
                    ps, kp = proj_tb(wb, wk, tb)
                    if 'M' in SUB:
                        continue
                    if 'E' in SUB and tb > 0:
                        continue
                    qb, qk = QB.next()
                    if tb == 0:
                        CP('act', qb[:], ps[:], [kp], [qk])
                        if c >= 8 and 'A' not in SUB:
                            xs, xk = XS.next()
                            CP('dve', xs[:], ps[:], [kp, qk] if 'S' in SUB else [kp], [xk])
                            for pi in range(2 if 'C' not in SUB else 0):
                                P.dma('sp', nk_d[pi, c - 8], xs[:, pi * 256:(pi + 1) * 256], reads=[xk])
                    else:
                        xs, xk = XS.next()
                        CP('act', xs[:], ps[:], [kp], [xk])
                        ps2, kp2 = ps_next()
                        MM(ps2[:], RTF[:], xs[:], True, True, ['RTF', xk], [kp2])
                        t1, k1 = T1.next()
                        t2, k2 = T2.next()
                        TT('dve', t1[:], xs[:], COS[:, (tb - 1) * 512:tb * 512], ALU.mult, [xk, 'COS'], [k1])
                        TT('dve', t2[:], ps2[:], SIN[:, (tb - 1) * 512:tb * 512], ALU.mult, [kp2, 'SIN'], [k2])
                        TT('dve', qb[:], t1[:], t2[:], ALU.add, [k1, k2], [qk])
                    if 'B' not in SUB:
                        P.dma('sp', QS[c, :, tb * 512:(tb + 1) * 512], qb[:], reads=[qk], writes=[('QS', c)])
            for h in range(8 if 'v' in SUB else 0):
                wb, wk = load_w(WB, win_d[16 + h])
                for g in range(3):
                    ps, kp = ps_next()
                    for j in range(4):
                        tt = g * 4 + j
                        for kc in range(KC):
                            MM(ps[:, j * 128:(j + 1) * 128], H[:, kc, tt * 128:(tt + 1) * 128], wb[:, kc, :],
                               kc == 0, kc == KC - 1, [wk, ('H', g)], [kp])
                    qb, qk = QB.next()
                    CP('act', qb[:], ps[:], [kp], [qk])
                    P.dma('sp', VS[h, g * 512:(g + 1) * 512, :].rearrange("(j p) d -> p j d", p=128),
                          qb[:].rearrange("p (j d) -> p j d", d=128), reads=[qk], writes=[('VS', h)])
                    if g == 0:
                        xs, xk = XS.next()
                        CP('dve', xs[:], ps[:], [kp], [xk])
                        for pi in range(2):
                            P.dma('sp', nv_d[pi, h].rearrange("(j p) d -> p j d", p=128),
                                  xs[:, pi * 256:(pi + 1) * 256].rearrange("p (j d) -> p j d", d=128), reads=[xk])
            for c in range(24, N_WIN if 'r' in SUB else 24):
                ri = rest_idx(c)
                wb, wk = load_w(WB, win_d[c])
                for tb in range(3):
                    ps, kp = proj_tb(wb, wk, tb)
                    if tb == 0:
                        CP('act', RAWP[:, 1:257], ps[:, 0:256], [kp], ['RAWP'])
                        CP('act', RAWP[:, 258:514], ps[:, 256:512], [kp], ['RAWP'])
                    else:
                        o = 515 + (tb - 1) * 512
                        CP('act', RAWP[:, o:o + 512], ps[:], [kp], ['RAWP'])
                m0o = PLAY['mu0'][0] + ri
                m1o = PLAY['mu1'][0] + ri
                TS('dve', OUTP[:, 1:1539], RAWP[:, 1:1539], CM[:, ri:ri + 1], None, ALU.mult, None, ['RAWP', 'CM'], ['OUTP'])
                STT('dve', OUTP[:, 1:1539], RAWP[:, 0:1538], PAR[:, m0o:m0o + 1], OUTP[:, 1:1539], ALU.mult, ALU.add,
                    ['RAWP', 'PAR', 'OUTP'], ['OUTP'])
                STT('dve', OUTP[:, 1:1539], RAWP[:, 2:1540], PAR[:, m1o:m1o + 1], OUTP[:, 1:1539], ALU.mult, ALU.add,
                    ['RAWP', 'PAR', 'OUTP'], ['OUTP'])
                P.dma('sp', RS[ri], OUTP[:], reads=['OUTP'], writes=[('RS', ri)])
            P.barrier()
            cur.pop()

        chk(1)
        with ExitStack() as ph:
            cur.append(ph)
            LORA = sb("LORA", [128, 3, 2, 1024], BF16)
            P.dma('pool', LORA[:], lora_d, writes=['LORA'])
            TWr = sb("TWr", [128, NPAD])
            TW = sb("TW", [128, NPAD], BF16)
            XA = sb("XA", [128, NPAD], BF16)
            SG = sb("SG", [128, 2, NPAD], BF16)
            OMK = sb("OMK", [128, 8])
            TS('dve', OMK[:], par('k_a'), -1.0, 1.0, ALU.mult, ALU.add, ['PAR'], ['OMK'])
            P.dma('sp', TWr[:], RS[24], writes=['TWr'])
            ACTV(TW[:], TWr[:], AF.Tanh, ['TWr'], ['TW'])
            P.dma('sp', TWr[:], RS[25], reads=[], writes=['TWr'])
            CP('act', XA[:], TWr[:], ['TWr'], ['XA'])
            for i in range(2):
                P.dma('sp', TWr[:], RS[26 + i], writes=['TWr'])
                ACTV(SG[:, i, :], TWr[:], AF.Sigmoid, ['TWr'], ['SG'])
            Rb = sb("Rb", [128, NPAD])
            KBb = sb("KBb", [128, NPAD])
            VBb = sb("VBb", [128, NPAD])
            KKb = sb("KKb", [128, NPAD])
            Ab = sb("Ab", [128, NPAD])
            TA = sb("TA", [128, NPAD])
            TBb = sb("TBb", [128, NPAD])
            WDb = sb("WDb", [128, NPAD])
            ALRb = sb("ALRb", [128, NPAD])
            KDb = sb("KDb", [128, NPAD])
            BDb = sb("BDb", [128, NPAD])
            TMB = Ring("TMB", 4, [128, 512], BF16)
            SL = slice(1, 1539)

            def to_tm(arr, akey, idx, hp, hilo=False):
                for g in range(3):
                    ps, kp = ps_next()
                    for j in range(4):
                        o = pad_off((g * 4 + j) * 128)
                        MM(ps[:, j * 128:(j + 1) * 128], arr[:, o:o + 128], IDF[:], True, True, [akey, 'IDF'], [kp])
                    tmb, tk = TMB.next()
                    CP('act', tmb[:], ps[:], [kp], [tk])
                    dst = TMS[idx, g * 512:(g + 1) * 512, hp * 128:(hp + 1) * 128].rearrange("(j p) c -> p j c", p=128)
                    P.dma('sp', dst, tmb[:].rearrange("p (j c) -> p j c", c=128), reads=[tk], writes=[('TMS', idx, hp)])
                    if hilo:
                        tml, tlk = TMB.next()
                        TT('dve', tml[:], ps[:], tmb[:], ALU.subtract, [kp, tk], [tlk])
                        dst2 = TMS[idx + 1, g * 512:(g + 1) * 512, hp * 128:(hp + 1) * 128].rearrange("(j p) c -> p j c", p=128)
                        P.dma('sp', dst2, tml[:].rearrange("p (j c) -> p j c", c=128), reads=[tlk], writes=[('TMS', idx + 1, hp)])

            def seq_store(dst_fn, arr, akey, wkey):
                for (t0, ln, po) in SEQP:
                    P.dma('sp', dst_fn(t0, ln), arr[:, po:po + ln], reads=[akey], writes=[wkey])

            for hp in range(8):
                P.dma('sp', Rb[:], RS[hp], writes=['Rb'])
                P.dma('sp', KBb[:], RS[8 + hp], writes=['KBb'])
                P.dma('sp', VBb[:], RS[16 + hp], writes=['VBb'])
                seq_store(lambda t0, ln: VBF[hp * 128:(hp + 1) * 128, t0:t0 + ln], VBb, 'VBb', ('VBF', hp))
                kko = PLAY['k_k'][0] + hp
                TS('dve', KKb[:, SL], KBb[:, SL], PAR[:, kko:kko + 1], None, ALU.mult, None, ['KBb', 'PAR'], ['KKb'])
                TT('dve', TA[:, SL], KKb[:, SL], KKb[:, SL], ALU.mult, ['KKb'], ['TA'])
                for (o, n) in PCS:
                    ps, kp = ps_next()
                    MM(ps[:, 0:n], BOF[:], TA[:, o:o + n], True, True, ['BOF', 'TA'], [kp])
                    ACTV(TBb[:, o:o + n], ps[:, 0:n], AF.Sqrt, [kp], ['TBb'])
                TS('dve', TBb[:, SL], TBb[:, SL], 1e-12, None, ALU.max, None, ['TBb'], ['TBb'])
                P.op('dve', lambda e: e.reciprocal(out=TBb[:, SL], in_=TBb[:, SL]), ['TBb'], ['TBb'])
                TT('dve', KKb[:, SL], KKb[:, SL], TBb[:, SL], ALU.mult, ['KKb', 'TBb'], ['KKb'])
                TS('dve', Ab[:, SL], KKb[:, SL], -1.0, None, ALU.mult, None, ['KKb'], ['Ab'])
                to_tm(Ab, 'Ab', 0, hp)
                to_tm(Rb, 'Rb', 1, hp)
                rko = PLAY['r_k'][0] + hp
                TT('dve', TA[:, SL], Rb[:, SL], KBb[:, SL], ALU.mult, ['Rb', 'KBb'], ['TA'])
                TS('dve', TA[:, SL], TA[:, SL], PAR[:, rko:rko + 1], None, ALU.mult, None, ['TA', 'PAR'], ['TA'])
                for (o, n) in PCS:
                    ps, kp = ps_next()
                    MM(ps[:, 0:n], BOF[:], TA[:, o:o + n], True, True, ['BOF', 'TA'], [kp])
                    TT('dve', TBb[:, o:o + n], ps[:, 0:n], VBb[:, o:o + n], ALU.mult, [kp, 'VBb'], ['TBb'])
                seq_store(lambda t0, ln: BON[hp, :, t0:t0 + ln], TBb, 'TBb', ('BON', hp))
                for (o, n) in PCS:
                    ps, kp = ps_next()
                    for kc in range(2):
                        MM(ps[:, 0:n], LORA[:, 2, kc, hp * 128:(hp + 1) * 128], SG[:, kc, o:o + n], kc == 0, kc == 1,
                           ['LORA', 'SG'], [kp])
                    CP('act', TA[:, o:o + n], ps[:, 0:n], [kp], ['TA'])
                seq_store(lambda t0, ln: GATE[hp, :, t0:t0 + ln], TA, 'TA', ('GATE', hp))
                for d in range(2):
                    w0o = PLAY['w0_%d' % d][0] + hp
                    a0o = PLAY['a0_%d' % d][0] + hp
                    kao = PLAY['k_a'][0] + hp
                    for (o, n) in PCS:
                        ps, kp = ps_next()
                        MM(ps[:, 0:n], LORA[0:96, 0, d, hp * 128:(hp + 1) * 128], TW[0:96, o:o + n], True, True,
                           ['LORA', 'TW'], [kp])
                        ACTV(WDb[:, o:o + n], ps[:, 0:n], AF.Sigmoid, [kp, 'PAR'], ['WDb'], bias=PAR[:, w0o:w0o + 1])
                        ps2, kp2 = ps_next()
                        MM(ps2[:, 0:n], LORA[0:96, 1, d, hp * 128:(hp + 1) * 128], XA[0:96, o:o + n], True, True,
                           ['LORA', 'XA'], [kp2])
                        ACTV(ALRb[:, o:o + n], ps2[:, 0:n], AF.Sigmoid, [kp2, 'PAR'], ['ALRb'], bias=PAR[:, a0o:a0o + 1])
                    ACTV(WDb[:, SL], WDb[:, SL], AF.Exp, ['WDb'], ['WDb'], scale=-math.exp(-0.5))
                    TS('dve', KDb[:, SL], ALRb[:, SL], PAR[:, kao:kao + 1], OMK[:, hp:hp + 1], ALU.mult, ALU.add,
                       ['ALRb', 'PAR', 'OMK'], ['KDb'])
                    TT('dve', KDb[:, SL], KDb[:, SL], KBb[:, SL], ALU.mult, ['KDb', 'KBb'], ['KDb'])
                    TT('dve', BDb[:, SL], KKb[:, SL], ALRb[:, SL], ALU.mult, ['KKb', 'ALRb'], ['BDb'])
                    to_tm(WDb, 'WDb', 2 + 4 * d, hp, hilo=True)
                    to_tm(BDb, 'BDb', 4 + 4 * d, hp)
                    to_tm(KDb, 'KDb', 5 + 4 * d, hp)
            P.barrier()
            cur.pop()

        chk(2)
        with ExitStack() as ph:
            cur.append(ph)
            ia = IDF[:]
            pstep = ia.ap[0][0]
            ia = IDB[:]
            pstep = ia.ap[0][0]
            SELM = sb("SELM", [128, 64, 128], BF16)
            for u_ in range(2):
                src_ = bass.AP(ia.tensor, ia.offset + 64 * u_, [[pstep, 128], [1, 64], [0, 64]])
                CP('dve', SELM[:, :, u_ * 64:(u_ + 1) * 64], src_, ['IDB'], ['SELM'])
            P.barrier()
            groups = [[('Sf', 2, 0, 0, None, 'dve'), ('Sb', 2, 1, 1, None, 'pool')],
                      [('P0f', 0, 0, None, (0, 0), 'dve'), ('P0b', 0, 1, None, (0, 1), 'pool')],
                      [('P1f', 1, 0, None, (1, 0), 'dve'), ('P1b', 1, 1, None, (1, 1), 'pool')]]
            def tms_idx(slot, dr):
                return [0, 2 + 4 * dr, 3 + 4 * dr, 4 + 4 * dr, 5 + 4 * dr, 1][slot]
            for grp in groups:
                with ExitStack() as gs:
                    cur.append(gs)
                    tl = []
                    for (nm, si, dr, s0i, outi, teng) in grp:
                        t = dict(nm=nm, dr=dr, outi=outi, eng=teng)
                        t['WS'] = [sb("WS%d_%s" % (i, nm), [128, 8, 64]) for i in range(2)]
                        t['T3'] = sb("T3_" + nm, [128, 8, 64])
                        t['off'], t['T'] = SEQS[si]
                        t['S'] = sb("S_" + nm, [128, 8, 64])
                        t['TMP'] = sb("TMP_" + nm, [128, 8, 64])
                        t['T2'] = sb("T2_" + nm, [128, 8, 64])
                        t['SA'] = sb("SA_" + nm, [128, 8])
                        t['VV'] = [sb("VV%d_%s" % (i, nm), [128, 8, 64]) for i in range(2)]
                        t['YT'] = [sb("YT%d_%s" % (i, nm), [128, 8, 64]) for i in range(2)]
                        t['TMX'] = [sb("TMX%d_%s" % (i, nm), [128, 6, 512], BF16) for i in range(2)]
                        if s0i is not None:
                            P.dma('sp', t['S'][:].rearrange("p h k -> p (h k)"), s0_d[s0i], writes=['S_' + nm])
                        else:
                            MEMSET('dve', t['S'][:], 0.0, ['S_' + nm])
                        tl.append(t)
                    nch = tl[0]['T'] // 64
                    for c in range(nch):
                        for t in tl:
                            nm = t['nm']
                            b = c % 2
                            tok0 = t['off'] + (c * 64 if t['dr'] == 0 else t['T'] - 64 * (c + 1))
                            for ti in range(6):
                                idx = tms_idx(ti, t['dr'])
                                for u in range(2):
                                    bT = c % 2
                                    P.dma('sp' if u == 0 else 'act', t['TMX'][bT][u * 64:(u + 1) * 64, ti, :],
                                          TMS[idx, tok0:tok0 + 64, u * 512:(u + 1) * 512],
                                          writes=[('TMX', nm, bT)])
                            for u in range(2):
                                src = bass.AP(VBF.tensor, VBF.offset + (u * 512) * NT + tok0,
                                              [[NT, 64], [64 * NT, 8], [1, 64]])
                                P.dma('sp', t['VV'][b][u * 64:(u + 1) * 64, :, :], src,
                                      reads=[('VBF', hq) for hq in range(8)], writes=[('VV', nm, b)])
                        def emit_back(bk):
                            (t_, S_, sk_, TMP_, tk_, pR_, kR_, b_, sel_) = bk
                            TT('dve', TMP_[:], S_[:], pR_, ALU.mult, [sk_, kR_], [tk_])
                            RED('dve', t_['YT'][b_][:, :, sel_], TMP_[:], [tk_], [('YT', t_['nm'], b_)])

                        pend = None
                        for i in range(64):
                            for tix, t in enumerate(tl):
                                nm = t['nm']
                                b = c % 2
                                sel = i if t['dr'] == 0 else 63 - i
                                lhsT = SELM[:, sel, :]
                                S = t['S']
                                sk = 'S_' + nm
                                pss = []
                                for slots in ((0,), (1, 2), (3,), (4,), (5,)):
                                    ps, kp = ps_next()
                                    for si_, sl_ in enumerate(slots):
                                        MM(ps[:], lhsT, t['TMX'][b][:, sl_, :], si_ == 0, si_ == len(slots) - 1,
                                           [('TMX', nm, b)], [kp])
                                    pss.append((ps[:].rearrange("p (h k) -> p h k", k=64), kp))
                                (pA, kA), (pW, kW), (pB, kB), (pK, kK), (pR, kR) = pss
                                TMP = t['TMP'][tix % 2] if isinstance(t['TMP'], list) else t['TMP']
                                T2 = t['T2']
                                T3 = t['T3']
                                SA = t['SA']
                                WS = t['WS'][i % 2]
                                wsk = ('WS', nm, i % 2)
                                tk, t2k, t3k, sak = 'TMP_' + nm, 'T2_' + nm, 'T3_' + nm, 'SA_' + nm
                                CP('act', WS[:], pW, [kW], [wsk])
                                TT('dve', TMP[:], S[:], pA, ALU.mult, [sk, kA], [tk])
                                RED('dve', SA[:], TMP[:], [tk], [sak])
                                TT('dve', T2[:], pB, SA[:].unsqueeze(2).to_broadcast([128, 8, 64]), ALU.mult, [kB, sak], [t2k])
                                TT('dve', T3[:], pK, t['VV'][b][:, :, sel:sel + 1].to_broadcast([128, 8, 64]), ALU.mult,
                                   [kK, ('VV', nm, b)], [t3k])
                                TT('pool', S[:], S[:], WS[:], ALU.mult, [sk, wsk], [sk])
                                TT('pool', S[:], S[:], T2[:], ALU.add, [sk, t2k], [sk])
                                TT('pool', S[:], S[:], T3[:], ALU.add, [sk, t3k], [sk])
                                bk = (t, S, sk, TMP, tk, pR, kR, b, sel)
                                if tix == 0:
                                    if pend is not None:
                                        emit_back(pend)
                                        pend = None
                                    backA = bk
                                else:
                                    emit_back(backA)
                                    pend = bk
                        if pend is not None:
                            emit_back(pend)
                            pend = None
                        for t in tl:
                            nm = t['nm']
                            b = c % 2
                            tok0 = t['off'] + (c * 64 if t['dr'] == 0 else t['T'] - 64 * (c + 1))
                            for u in range(2):
                                dst = bass.AP(YF.tensor, YF.offset + t['dr'] * 1024 * NT + (u * 512) * NT + tok0,
                                              [[NT, 64], [64 * NT, 8], [1, 64]])
                                P.dma('sp', dst, t['YT'][b][u * 64:(u + 1) * 64, :, :], reads=[('YT', nm, b)],
                                      writes=[('YF', t['dr'])])
                    for t in tl:
                        if t['outi'] is not None:
                            pi, di = t['outi']
                            P.dma('sp', nst_d[pi, di], t['S'][:].rearrange("p h k -> p (h k)"), reads=['S_' + t['nm']])
                    P.barrier()
                    cur.pop()
            cur.pop()

        chk(3)
        with ExitStack() as ph:
            cur.append(ph)
            OA = sb("OA", [128, 8, NT], BF16)
            OB = sb("OB", [128, 8, NT], BF16)
            with ExitStack() as p2:
                cur.append(p2)
                Y0 = sb("Y0", [128, NT])
                Y1 = sb("Y1", [128, NT])
                BNb = sb("BNb", [128, NT])
                GTb = sb("GTb", [128, NT])
                DDb = sb("DDb", [128, 512])
                SQb = sb("SQb", [128, 512])
                RSb = sb("RSb", [128, 512])
                for hp in range(8):
                    P.dma('sp', Y0[:], YF[0, hp * 128:(hp + 1) * 128, :], writes=['Y0'])
                    P.dma('act', Y1[:], YF[1, hp * 128:(hp + 1) * 128, :], writes=['Y1'])
                    P.dma('sp', BNb[:], BON[hp], writes=['BNb'])
                    P.dma('act', GTb[:], GATE[hp], writes=['GTb'])
                    TT('dve', Y0[:], Y0[:], Y1[:], ALU.add, ['Y0', 'Y1'], ['Y0'])
                    lgo = PLAY['lnx_g'][0] + hp
                    lbo = PLAY['lnx_b'][0] + hp
                    for tb in range(3):
                        ts_ = slice(tb * 512, (tb + 1) * 512)
                        ps, kp = ps_next()
                        MM(ps[:], BOF[:], Y0[:, ts_], True, True, ['BOF', 'Y0'], [kp])
                        STT('dve', DDb[:], ps[:], -1.0 / 64, Y0[:, ts_], ALU.mult, ALU.add, [kp, 'Y0'], ['DDb'])
                        TT('dve', SQb[:], DDb[:], DDb[:], ALU.mult, ['DDb'], ['SQb'])
                        ps2, kp2 = ps_next()
                        MM(ps2[:], BOF[:], SQb[:], True, True, ['BOF', 'SQb'], [kp2])
                        RSQ(RSb[:], ps2[:], 1.0 / 64, GN_EPS, [kp2], ['RSb'])
                        TT('dve', DDb[:], DDb[:], RSb[:], ALU.mult, ['DDb', 'RSb'], ['DDb'])
                        TS('dve', DDb[:], DDb[:], PAR[:, lgo:lgo + 1], PAR[:, lbo:lbo + 1], ALU.mult, ALU.add,
                           ['DDb', 'PAR'], ['DDb'])
                        TT('dve', DDb[:], DDb[:], BNb[:, ts_], ALU.add, ['DDb', 'BNb'], ['DDb'])
                        TT('dve', OB[:, hp, ts_], DDb[:], GTb[:, ts_], ALU.mult, ['DDb', 'GTb'], ['OB'])
                P.barrier()
                cur.pop()
            with ExitStack() as p2:
                cur.append(p2)
                Qh = sb("Qh", [128, NT], BF16)
                Kh = sb("Kh", [128, NT], BF16)
                VT = sb("VT", [128, 12, 128], BF16)
                CK = sb("CK", [128, 256], BF16)
                CVt = sb("CVt", [128, 2, 128], BF16)
                EX = Ring("EX", 3, [128, 512], BF16)
                R1 = sb("R1", [128, 512])
                O1 = sb("O1", [128, 512])
                O2 = sb("O2", [128, 512])
                SQa = sb("SQa", [128, 512])
                RSa = sb("RSa", [128, 512])
                for h in range(8):
                    P.dma('sp', Qh[:], QS[h], writes=['Qh'])
                    P.dma('act', Kh[:], QS[8 + h], writes=['Kh'])
                    P.dma('sp', VT[:], VS[h].rearrange("(j p) d -> p j d", p=128), writes=['VT'])
                    P.dma('pool', CK[:], cK_d[h], writes=['CK'])
                    P.dma('pool', CVt[:], cV_d[h].rearrange("(j p) d -> p j d", p=128), writes=['CVt'])
                    jobs = [(0, 256, [('n', 0), ('n', 1)]), (256, 256, [('n', 2), ('n', 3)]),
                            (512, 512, [('c', 0), ('c', 1)] + [('n', j) for j in range(4, 12)]),
                            (1024, 512, [('c', 0), ('c', 1)] + [('n', j) for j in range(4, 12)])]
                    for (q0, nq, kts) in jobs:
                        acc = []
                        for m in range(2):
                            psO, kO = PS[2 * m], 'ps%d' % (2 * m)
                            psD, kD = PS[2 * m + 1], 'ps%d' % (2 * m + 1)
                            for ki, (kind, j) in enumerate(kts):
                                if kind == 'n':
                                    ksrc = Kh[m * 64:(m + 1) * 64, j * 128:(j + 1) * 128]
                                    vsrc = VT[:, j, :]
                                    kr, vr = 'Kh', 'VT'
                                else:
                                    ksrc = CK[m * 64:(m + 1) * 64, j * 128:(j + 1) * 128]
                                    vsrc = CVt[:, j, :]
                                    kr, vr = 'CK', 'CVt'
                                psS, kS = ps_hi()
                                MM(psS[:, 0:nq], ksrc, Qh[m * 64:(m + 1) * 64, q0:q0 + nq], True, True, [kr, 'Qh'], [kS])
                                ex, ek = EX.next()
                                ACTV(ex[:, 0:nq], psS[:, 0:nq], AF.Exp, [kS], [ek], scale=0.125)
                                MM(psO[:, 0:nq], vsrc, ex[:, 0:nq], ki == 0, ki == len(kts) - 1, [vr, ek], [kO])
                                MM(psD[:, 0:nq], ONB[:], ex[:, 0:nq], ki == 0, ki == len(kts) - 1, ['ONB', ek], [kD])
                            acc.append((psO, kO, psD, kD))
                        (pO1, kO1, pD1, kD1), (pO2, kO2, pD2, kD2) = acc
                        n_ = slice(0, nq)
                        P.op('dve', lambda e, a=R1[:, n_], b=pD1[:, n_]: e.reciprocal(out=a, in_=b), [kD1], ['R1'])
                        TT('dve', O1[:, n_], pO1[:, n_], R1[:, n_], ALU.mult, [kO1, 'R1'], ['O1'])
                        P.op('dve', lambda e, a=R1[:, n_], b=pD2[:, n_]: e.reciprocal(out=a, in_=b), [kD2, 'O1'], ['R1'])
                        TT('dve', O2[:, n_], pO2[:, n_], R1[:, n_], ALU.mult, [kO2, 'R1'], ['O2'])
                        STT('dve', O1[:, n_], O2[:, n_], NEGLAM[:, 0:1], O1[:, n_], ALU.mult, ALU.add,
                            ['O2', 'O1', 'NEGLAM'], ['O1'])
                        TT('dve', SQa[:, n_], O1[:, n_], O1[:, n_], ALU.mult, ['O1'], ['SQa'])
                        psq, kq = ps_hi()
                        MM(psq[:, n_], ONF[:], SQa[:, n_], True, True, ['ONF', 'SQa'], [kq])
                        RSQ(RSa[:, n_], psq[:, n_], 1.0 / 128, LN_EPS, [kq], ['RSa'])
                        TT('dve', O1[:, n_], O1[:, n_], RSa[:, n_], ALU.mult, ['O1', 'RSa'], ['O1'])
                        TS('dve', OA[:, h, q0:q0 + nq], O1[:, n_], SUBG[:, 0:1], None, ALU.mult, None, ['O1', 'SUBG'], ['OA'])
                P.barrier()
                cur.pop()
            WB = Ring("WBo", 2, [128, KC, 128], BF16)
            for c in range(16):
                wb, wk = load_w(WB, wout_d[c])
                for tb in range(3):
                    ps, kp = ps_next()
                    for kc in range(KC):
                        src = OA if kc < 8 else OB
                        MM(ps[:], wb[:, kc, :], src[:, kc % 8, tb * 512:(tb + 1) * 512], kc == 0, kc == KC - 1,
                           [wk, 'OA', 'OB'], [kp])
                    resid_add(ps, kp, 0, 0, c, tb)
            P.barrier()
            cur.pop()

        chk(4)
        TBR = [[(0, 256, 1), (256, 256, 258)], [(0, 512, 515)], [(0, 512, 1027)]]

        def ffn(l):
            with ExitStack() as ph:
                cur.append(ph)
                H = sb("Hf%d" % l, [128, KC, NT], BF16)
                with ExitStack() as p2:
                    cur.append(p2)
                    SQ = Ring("SQf%d" % l, 2, [128, 512])
                    TMPN = Ring("TMPNf%d" % l, 2, [128, 512])
                    RSTD = sb("RSTDf%d" % l, [128, 512])
                    norm_mod(H, l, 1, SQ, RSTD, TMPN)
                    P.barrier()
                    cur.pop()
                G = 2
                WUr = Ring("WU%d_" % l, 2, [128, KC, 256], BF16)
                WDG = sb("WDG%d" % l, [128, G, D], BF16)
                H2G = sb("H2G%d" % l, [128, G, NPAD], BF16)
                UP = [sb("UP%d_%d" % (l, i), [128, NPAD]) for i in range(2)]
                CVg = sb("CVg%d" % l, [128, NPAD])
                for i in range(2):
                    MEMSET('dve', UP[i][:], 0.0, ['UP%d' % i])
                fo = PLAY['fdw%d' % l][0]
                SL = slice(1, 1539)
                c = 0
                while c < NFF:
                    gn = min(G, NFF - c)
                    for g in range(gn):
                        cc = c + g
                        WU, wuk = WUr.next()
                        P.dma('pool', WU[:], wup_d[l, cc], writes=[wuk])
                        P.dma('pool', WDG[:, g, :], wdn_d[l, cc], writes=[('WDG', g)])
                        for gv in range(2):
                            for tb in range(3):
                                ps, kp = ps_next()
                                for kc in range(KC):
                                    MM(ps[:], WU[:, kc, gv * 128:(gv + 1) * 128], H[:, kc, tb * 512:(tb + 1) * 512],
                                       kc == 0, kc == KC - 1, [wuk, ('H', tb)], [kp])
                                for (pc, n, po) in TBR[tb]:
                                    CP('act', UP[gv][:, po:po + n], ps[:, pc:pc + n], [kp], ['UP%d' % gv])
                            to = fo + (cc * 2 + gv) * 3
                            dst, dk = (CVg, 'CVg') if gv == 0 else (UP[0], 'UP0')
                            TS('dve', dst[:, SL], UP[gv][:, SL], PAR[:, to + 1:to + 2], None, ALU.mult, None,
                               ['UP%d' % gv, 'PAR'], [dk])
                            STT('dve', dst[:, SL], UP[gv][:, 0:1538], PAR[:, to:to + 1], dst[:, SL], ALU.mult, ALU.add,
                                ['UP%d' % gv, 'PAR', dk], [dk])
                            STT('dve', dst[:, SL], UP[gv][:, 2:1540], PAR[:, to + 2:to + 3], dst[:, SL], ALU.mult, ALU.add,
                                ['UP%d' % gv, 'PAR', dk], [dk])
                            if gv == 0:
                                ACTV(CVg[:, SL], CVg[:, SL], AF.Silu, ['CVg'], ['CVg'])
                        TT('dve', H2G[:, g, SL], CVg[:, SL], UP[0][:, SL], ALU.mult, ['CVg', 'UP0'], [('H2G', g)])
                        MEMSET('dve', UP[0][:, 257:258], 0.0, ['UP0'])
                        MEMSET('dve', UP[0][:, 514:515], 0.0, ['UP0'])
                    for ko in range(KC):
                        for tb in range(3):
                            ps, kp = ps_next()
                            for (pc, n, po) in TBR[tb]:
                                for g in range(gn):
                                    MM(ps[:, pc:pc + n], WDG[:, g, ko * 128:(ko + 1) * 128], H2G[:, g, po:po + n],
                                       g == 0, g == gn - 1, [('WDG', g), ('H2G', g)], [kp])
                            resid_add(ps, kp, l, 1, ko, tb)
                    c += gn
                P.barrier()
                cur.pop()

        ffn(0)

        chk(5)
        NC_ = NT + 60
        CO = [15, 286, 557]
        TBC = [[(0, 256, 15), (256, 256, 286)], [(0, 512, 557)], [(0, 512, 1069)]]
        with ExitStack() as ph:
            cur.append(ph)
            H = sb("Hc", [128, KC, NT], BF16)
            with ExitStack() as p2:
                cur.append(p2)
                SQ = Ring("SQc", 2, [128, 512])
                TMPN = Ring("TMPNc", 2, [128, 512])
                RSTD = sb("RSTDc", [128, 512])
                norm_mod(H, 1, 0, SQ, RSTD, TMPN)
                P.barrier()
                cur.pop()
            WBa = Ring("WBa", 2, [128, KC, 128], BF16)
            UAr = [sb("UA%d" % i, [128, NC_]) for i in range(2)]
            ACC = sb("ACC", [128, NC_])
            SQc = sb("SQc2", [128, NC_])
            MEAN = sb("MEAN", [128, 3, 512])
            RSC = sb("RSC", [128, 3, 512])
            for i in range(2):
                MEMSET('dve', UAr[i][:], 0.0, ['UA%d' % i])
            wdo = PLAY['w_dw'][0]
            bdo = PLAY['b_dw'][0]
            CS = slice(15, 1581)
            for c in range(16):
                wa, wak = load_w(WBa, wpw1_d[c])
                wg, wgk = load_w(WBa, wpw1_d[16 + c])
                UA, uak = UAr[c % 2], 'UA%d' % (c % 2)
                for tb in range(3):
                    psA, kA = PS[6], 'ps6'
                    psB, kB = PS[7], 'ps7'
                    for kc in range(KC):
                        MM(psA[:], wa[:, kc, :], H[:, kc, tb * 512:(tb + 1) * 512], kc == 0, kc == KC - 1, [wak, ('H', tb)], [kA])
                    for kc in range(KC):
                        MM(psB[:], wg[:, kc, :], H[:, kc, tb * 512:(tb + 1) * 512], kc == 0, kc == KC - 1, [wgk, ('H', tb)], [kB])
                    for (pc, n, po) in TBC[tb]:
                        ACTV(UA[:, po:po + n], psB[:, pc:pc + n], AF.Sigmoid, [kB], [uak])
                        TT('dve', UA[:, po:po + n], psA[:, pc:pc + n], UA[:, po:po + n], ALU.mult, [kA, uak], [uak])
                TS('dve', ACC[:, CS], UA[:, 0:1566], PAR[:, wdo + c * 31:wdo + c * 31 + 1], PAR[:, bdo + c:bdo + c + 1],
                   ALU.mult, ALU.add, [uak, 'PAR'], ['ACC'])
                for j in range(1, 31):
                    STT('dve', ACC[:, CS], UA[:, j:j + 1566], PAR[:, wdo + c * 31 + j:wdo + c * 31 + j + 1], ACC[:, CS],
                        ALU.mult, ALU.add, [uak, 'PAR', 'ACC'], ['ACC'])
                TT('dve', SQc[:, CS], ACC[:, CS], ACC[:, CS], ALU.mult, ['ACC'], ['SQc'])
                for tb in range(3):
                    for (pc, n, po) in TBC[tb]:
                        MM(PS[tb][:, pc:pc + n], ONF[:], ACC[:, po:po + n], c == 0, c == 15, ['ONF', 'ACC'], ['ps%d' % tb])
                        MM(PS[3 + tb][:, pc:pc + n], ONF[:], SQc[:, po:po + n], c == 0, c == 15, ['ONF', 'SQc'], ['ps%d' % (3 + tb)])
                for (t0, ln, po) in [(0, 256, 15), (256, 256, 286), (512, 1024, 557)]:
                    P.dma('sp', CONV[c, :, t0:t0 + ln], ACC[:, po:po + ln], reads=['ACC'], writes=[('CONV', c)])
            for tb in range(3):
                TS('dve', MEAN[:, tb, :], PS[tb][:], 1.0 / D, None, ALU.mult, None, ['ps%d' % tb], ['MEAN'])
                TT('dve', RSC[:, tb, :], MEAN[:, tb, :], MEAN[:, tb, :], ALU.mult, ['MEAN'], ['RSC'])
                STT('dve', RSC[:, tb, :], PS[3 + tb][:], 1.0 / D, RSC[:, tb, :], ALU.mult, ALU.subtract,
                    ['ps%d' % (3 + tb), 'RSC'], ['RSC'])
                RSQ(RSC[:, tb, :], RSC[:, tb, :], 1.0, LN_EPS, ['RSC'], ['RSC'])
            CL = Ring("CL", 1, [128, NT])
            MV = MEAN[:].rearrange("p a b -> p (a b)")
            RV = RSC[:].rearrange("p a b -> p (a b)")
            cgo = PLAY['cln_g'][0]
            cbo = PLAY['cln_b'][0]
            for c in range(16):
                cl, ck = CL.next()
                P.dma('sp', cl[:], CONV[c], reads=[('CONV', c)], writes=[ck])
                TT('dve', cl[:], cl[:], MV, ALU.subtract, [ck, 'MEAN'], [ck])
                TT('dve', cl[:], cl[:], RV, ALU.mult, [ck, 'RSC'], [ck])
                TS('dve', cl[:], cl[:], PAR[:, cgo + c:cgo + c + 1], PAR[:, cbo + c:cbo + c + 1], ALU.mult, ALU.add,
                   [ck, 'PAR'], [ck])
                ACTV(H[:, c, :], cl[:], AF.Silu, [ck], [('H', 0), ('H', 1), ('H', 2)])
            for c in range(16):
                wb, wk = load_w(WBa, wpw2_d[c])
                for tb in range(3):
                    ps, kp = PS[6 + (tb % 2)], 'ps%d' % (6 + (tb % 2))
                    for kc in range(KC):
                        MM(ps[:], wb[:, kc, :], H[:, kc, tb * 512:(tb + 1) * 512], kc == 0, kc == KC - 1, [wk, ('H', tb)], [kp])
                    resid_add(ps, kp, 1, 0, c, tb)
            P.barrier()
            cur.pop()

        chk(6)
        ffn(1)
        chk(7)

        with ExitStack() as ph:
            cur.append(ph)
            SQ = Ring("SQz", 2, [128, 512])
            YO = Ring("YO", 3, [128, 512])
            RSTD = sb("RSTDz", [128, 512])
            fo_ = PLAY['fng'][0]
            for tb in range(3):
                psN, kN = ps_next()
                for kc in range(KC):
                    sq, sk = SQ.next()
                    ACTV(sq[:], X[:, kc, tb * 512:(tb + 1) * 512], AF.Square, [('X', tb)], [sk])
                    MM(psN[:], ONF[:], sq[:], kc == 0, kc == KC - 1, [sk, 'ONF'], [kN])
                RSQ(RSTD[:], psN[:], 1.0 / D, RMS_EPS, [kN], ['RSTDz'])
                for kc in range(KC):
                    yo, yk = YO.next()
                    TT('dve', yo[:], X[:, kc, tb * 512:(tb + 1) * 512], RSTD[:], ALU.mult, [('X', tb), 'RSTDz'], [yk])
                    TS('dve', yo[:], yo[:], PAR[:, fo_ + kc:fo_ + kc + 1], None, ALU.mult, None, [yk, 'PAR'], [yk])
                    P.dma('sp', y_d[kc * 128:(kc + 1) * 128, tb * 512:(tb + 1) * 512], yo[:], reads=[yk])
            cur.pop()
        if debug:
            print('instr counts', {k: len(v) for k, v in P.prog.items()}, 'sems', P.nsem)
        P.finish()
    return nc


def _fm(v, n=None):
    v = np.asarray(v, np.float32).reshape(-1)
    return np.ascontiguousarray(v.reshape(-1, 128).T)


def _arr_w(W, cols=None):
    K, N = W.shape
    if cols is None:
        Wc = W.reshape(K // 128, 128, N // 128, 128)
        return np.ascontiguousarray(Wc.transpose(2, 1, 0, 3))
    out = np.zeros((len(cols), 128, K // 128, 128), np.float32)
    for c, cl in enumerate(cols):
        cl = np.asarray(cl)
        m = cl >= 0
        blk = np.zeros((K, 128), np.float32)
        blk[:, m] = W[:, cl[m]]
        out[c] = blk.reshape(K // 128, 128, 128).transpose(1, 0, 2)
    return out


def prep_shared(inp):
    f = lambda k: np.asarray(inp[k], np.float32)
    sh = {}
    pars = np.zeros((128, NPAR), np.float32)

    def put(name, a):
        o, w = PLAY[name]
        a = np.asarray(a, np.float32)
        assert a.shape == (128, w), (name, a.shape, w)
        pars[:, o:o + w] = a
    b_ada = f('b_ada')
    put('b_ada0', _fm(b_ada[0]))
    put('b_ada1', _fm(b_ada[1]))
    ng = f('norm_g')
    for l in range(2):
        for wh in range(2):
            put('ng%d%d' % (l, wh), _fm(ng[l, wh]))
    put('fng', _fm(f('final_norm_g')))
    mu = f('shift_mu')[0]
    for i in range(2):
        m = np.zeros((128, 28), np.float32)
        m[:, 0:24] = _fm(mu[i, 0:3072])
        m[:96, 24] = mu[i, 3072:3168]
        m[:96, 25] = mu[i, 3168:3264]
        m[:, 26:28] = _fm(mu[i, 3264:3520])
        put('mu%d' % i, m)
    for d in range(2):
        put('w0_%d' % d, _fm(f('w0')[0, d]))
        put('a0_%d' % d, _fm(f('a0')[0, d]))
    put('k_k', _fm(f('k_k')[0]))
    put('k_a', _fm(f('k_a')[0]))
    put('r_k', _fm(f('r_k')[0].reshape(-1)))
    put('lnx_g', _fm(f('lnx_g')[0]))
    put('lnx_b', _fm(f('lnx_b')[0]))
    put('subln', f('subln_g')[0].reshape(128, 1))
    put('dl', np.broadcast_to(f('diff_lambda')[0].reshape(1, 256), (128, 256)))
    put('b_dw', _fm(f('b_dw')[0]))
    put('cln_g', _fm(f('cln_g')[0]))
    put('cln_b', _fm(f('cln_b')[0]))
    wdw = f('w_dw')[0]
    put('w_dw', np.ascontiguousarray(wdw.reshape(31, 16, 128).transpose(2, 1, 0)).reshape(128, 496))
    fd = f('w_ffn_dw')
    for l in range(2):
        a = fd[l].reshape(3, 2, NFF, 128).transpose(3, 2, 1, 0)
        put('fdw%d' % l, np.ascontiguousarray(a).reshape(128, 258))
    sh['pars'] = pars
    sh['ident'] = np.eye(128, dtype=np.float32)
    bo = np.zeros((128, 128), np.float32)
    bo[:64, :64] = 1
    bo[64:, 64:] = 1
    sh['bones'] = bo
    rt = np.zeros((128, 128), np.float32)
    ang = np.zeros((128, 1024), np.float64)
    tt = np.arange(1024)
    row = (tt // 64).astype(np.float64)
    col = (tt % 64).astype(np.float64)
    inv = (10000.0 ** (-np.arange(16, dtype=np.float32) / 16)).astype(np.float32).astype(np.float64)
    for m in range(2):
        for d in range(64):
            half, r = divmod(d, 32)
            if r < 16:
                partner, sgn, fi = d + 16, -1.0, r
            else:
                partner, sgn, fi = d - 16, 1.0, r - 16
            rt[m * 64 + partner, m * 64 + d] = sgn
            pos = row if half == 0 else col
            ang[m * 64 + d] = (pos.astype(np.float32) * np.float32(inv[fi])).astype(np.float64)
    sh['rt'] = rt
    sh['cos'] = np.cos(ang).astype(np.float32)
    sh['sin'] = np.sin(ang).astype(np.float32)
    wa = f('w_ada')
    sh['wada'] = np.ascontiguousarray(wa.reshape(2, KC, 128, 24, 512).transpose(0, 3, 2, 1, 4))
    sh['win'] = _arr_w(f('w_in')[0], win_cols())
    lora = np.zeros((128, 3, 2, 1024), np.float32)
    lora[:96, 0] = f('w2')[0].transpose(1, 0, 2)
    lora[:96, 1] = f('a2')[0].transpose(1, 0, 2)
    lora[:, 2] = f('g2')[0].reshape(2, 128, 1024).transpose(1, 0, 2)
    sh['lora'] = lora
    sh['wout'] = _arr_w(f('w_out')[0])
    sh['wpw1'] = _arr_w(f('w_pw1')[0])
    sh['wpw2'] = _arr_w(f('w_pw2')[0])
    wu = f('w_up')
    a = wu.reshape(2, KC, 128, 2, NFF, 128).transpose(0, 4, 2, 1, 3, 5)
    sh['wup'] = np.ascontiguousarray(a).reshape(2, NFF, 128, KC, 256)
    sh['wdn'] = np.ascontiguousarray(f('w_down').reshape(2, NFF, 128, D))
    return sh


def prep_core(inp, i):
    f = lambda k: np.asarray(inp[k], np.float32)
    m = {}
    xp = f('x_prompt')[2 * i:2 * i + 2].reshape(512, D)
    xs = f('x_sample')[i]
    m['xin'] = np.ascontiguousarray(np.concatenate([xp, xs], 0).T)
    cf = np.stack([f('c')[i], f('c_ctx')], -1)
    m['cfm'] = np.ascontiguousarray(cf.reshape(KC, 128, 2).transpose(1, 0, 2))
    ck = f('cache_k')[i, 0]
    m['cK'] = np.ascontiguousarray(ck.transpose(0, 1, 3, 2)).reshape(8, 128, 256)
    m['cV'] = np.ascontiguousarray(f('cache_v')[i, 0])
    s0 = np.stack([f('state_wkv_fwd')[i, 0], f('state_wkv_bwd')[i, 0]], 0)
    m['s0'] = np.ascontiguousarray(s0.reshape(2, 2, 8, 64, 64).transpose(0, 1, 3, 2, 4)).reshape(2, 128, 512)
    return m


_NC_CACHE = {}


def kernel(**inputs):
    sh = prep_shared(inputs)
    in_maps = []
    for i in range(8):
        m = dict(sh)
        m.update(prep_core(inputs, i))
        in_maps.append(m)
    if 'nc' not in _NC_CACHE:
        _NC_CACHE['nc'] = build()
    nc = _NC_CACHE['nc']
    res = run_bass_kernel_spmd(nc, in_maps, core_ids=list(range(8)))
    yp = np.zeros((16, 256, D), np.float32)
    ys = np.zeros((8, 1024, D), np.float32)
    nk = np.zeros((16, 1, 8, 2, 256, 64), np.float32)
    nv = np.zeros((16, 1, 8, 256, 128), np.float32)
    sf = np.zeros((16, 1, 16, 64, 64), np.float32)
    sbw = np.zeros((16, 1, 16, 64, 64), np.float32)
    for i in range(8):
        r = res.results[i]
        y = np.asarray(r['y'], np.float32).T
        yp[2 * i:2 * i + 2] = y[:512].reshape(2, 256, D)
        ys[i] = y[512:]
        k = np.asarray(r['nk'], np.float32).reshape(2, 8, 2, 64, 256)
        nk[2 * i:2 * i + 2, 0] = k.transpose(0, 1, 2, 4, 3)
        nv[2 * i:2 * i + 2, 0] = np.asarray(r['nv'], np.float32)
        s = np.asarray(r['nst'], np.float32).reshape(2, 2, 2, 64, 8, 64)
        s = s.transpose(0, 1, 2, 4, 3, 5).reshape(2, 2, 16, 64, 64)
        sf[2 * i:2 * i + 2, 0] = s[:, 0]
        sbw[2 * i:2 * i + 2, 0] = s[:, 1]
    return (yp, ys, nk, nv, sf, sbw)
```

```python
import math
import numpy as np
import concourse.bass as bass
import concourse.mybir as mybir
from concourse.bass_utils import run_bass_kernel_spmd
from contextlib import ExitStack

F32 = mybir.dt.float32
BF16 = mybir.dt.bfloat16
ALU = mybir.AluOpType
AF = mybir.ActivationFunctionType
AX = mybir.AxisListType

D = 2048
KC = 16
NT = 1536
NPAD = NT + 4
SEQS = [(0, 256), (256, 256), (512, 1024)]
POFF = [1, 258, 515]
D_FF = 5504
NFF = 43
LAM_INIT = 0.8 - 0.6 * math.exp(0.0)
RMS_EPS = 1e-6
LN_EPS = 1e-5
GN_EPS = 64e-5
N_WIN = 52


class Prog:
    NS = 8

    def __init__(self, nc, stack):
        self.nc = nc
        self.stack = stack
        self.ce = ['pe', 'dve', 'act', 'pool']
        self.engs = ['pe', 'dve', 'act', 'pool', 'sp']
        self.prog = {e: [] for e in self.engs}
        self.sems = {}
        self.cnt = {}
        self.waited = {e: {} for e in self.engs}
        self.lastw = {}
        self.readers = {}
        self.ndma = {e: 0 for e in self.engs}
        self.epoch = 0
        self.nsem = 0
        self.dead = False
        for e in self.ce:
            self._mk(('c', e, 0))

    def _mk(self, key):
        self.sems[key] = self.stack.enter_context(self.nc.semaphore("s%d" % self.nsem))
        self.nsem += 1
        self.cnt[key] = 0

    def _deps(self, reads, writes):
        deps = {}
        for r in reads:
            lw = self.lastw.get(r)
            if lw is not None and deps.get(lw[0], 0) < lw[1]:
                deps[lw[0]] = lw[1]
            if isinstance(r, str) and r.startswith('ps'):
                for sk, v in self.readers.get(r, {}).items():
                    if deps.get(sk, 0) < v:
                        deps[sk] = v
        for w in writes:
            lw = self.lastw.get(w)
            if lw is not None and deps.get(lw[0], 0) < lw[1]:
                deps[lw[0]] = lw[1]
            for sk, v in self.readers.get(w, {}).items():
                if deps.get(sk, 0) < v:
                    deps[sk] = v
        return deps

    def _waits(self, eng, deps, skip=None):
        waits = []
        wd = self.waited[eng]
        for sk, v in deps.items():
            if sk == skip:
                continue
            if wd.get(sk, 0) < v:
                waits.append((self.sems[sk], v))
                wd[sk] = v
        return waits

    def _record(self, sk, val, reads, writes):
        for r in reads:
            d = self.readers.setdefault(r, {})
            if d.get(sk, 0) < val:
                d[sk] = val
        for w in writes:
            self.lastw[w] = (sk, val)
            self.readers[w] = {}

    def op(self, eng, fn, reads=(), writes=()):
        if self.dead:
            return
        own = ('c', eng, self.epoch)
        deps = self._deps(reads, writes)
        waits = self._waits(eng, deps, skip=own if eng == 'pe' else None)
        self.cnt[own] += 1
        val = self.cnt[own]
        sem = self.sems[own]

        def emit(e):
            for s, v in waits:
                e.wait_ge(s, v)
            fn(e).then_inc(sem, 1)
        self.prog[eng].append(emit)
        self._record(own, val, reads, writes)

    def dma(self, eng, out, in_, reads=(), writes=(), **kw):
        if self.dead:
            return
        i = self.ndma[eng]
        self.ndma[eng] += 1
        sk = ('d', eng, i % self.NS)
        if sk not in self.sems:
            self._mk(sk)
        target = 16 * (i // self.NS + 1)
        deps = self._deps(reads, writes)
        if target > 16 and deps.get(sk, 0) < target - 16:
            deps[sk] = target - 16
        waits = self._waits(eng, deps)
        self.cnt[sk] = target
        sem = self.sems[sk]

        def emit(e):
            for s, v in waits:
                e.wait_ge(s, v)
            e.dma_start(out=out, in_=in_, **kw).then_inc(sem, 16)
        self.prog[eng].append(emit)
        self._record(sk, target, reads, writes)

    def _finals(self):
        return {sk: c for sk, c in self.cnt.items()
                if c > 0 and (sk[0] == 'd' or sk[2] == self.epoch)}

    def barrier(self):
        if self.dead:
            return
        finals = self._finals()
        for eng in self.engs:
            waits = self._waits(eng, finals, skip=('c', eng, self.epoch))
            if waits:
                def emit(e, waits=waits):
                    for s, v in waits:
                        e.wait_ge(s, v)
                self.prog[eng].append(emit)
        self.epoch += 1
        for e in self.ce:
            self._mk(('c', e, self.epoch))
        self.lastw = {}
        self.readers = {}

    def finish(self):
        finals = self._finals()
        for eng in self.engs:
            waits = self._waits(eng, finals, skip=('c', eng, self.epoch))

            def emit(e, waits=waits):
                for s, v in waits:
                    e.wait_ge(s, v)
            self.prog[eng].append(emit)
        prog = self.prog
        with self.nc.Block() as block:
            @block.tensor
            def _(e):
                for f in prog['pe']:
                    f(e)

            @block.vector
            def _(e):
                for f in prog['dve']:
                    f(e)

            @block.scalar
            def _(e):
                for f in prog['act']:
                    f(e)

            @block.gpsimd
            def _(e):
                for f in prog['pool']:
                    f(e)

            @block.sync
            def _(e):
                for f in prog['sp']:
                    f(e)


def par_layout():
    ents = [('b_ada0', 96), ('b_ada1', 96), ('ng00', 16), ('ng01', 16), ('ng10', 16), ('ng11', 16),
            ('fng', 16), ('mu0', 28), ('mu1', 28), ('w0_0', 8), ('w0_1', 8), ('a0_0', 8), ('a0_1', 8),
            ('k_k', 8), ('k_a', 8), ('r_k', 8), ('lnx_g', 8), ('lnx_b', 8), ('subln', 1), ('dl', 256),
            ('b_dw', 16), ('cln_g', 16), ('cln_b', 16), ('w_dw', 496), ('fdw0', 258), ('fdw1', 258)]
    lay = {}
    o = 0
    for n, w in ents:
        lay[n] = (o, w)
        o += w
    return lay, o


PLAY, NPAR = par_layout()


def win_cols():
    ch = []
    for h in range(8):
        ch.append(list(range(h * 128, h * 128 + 128)))
    for h in range(8):
        ch.append(list(range(1024 + h * 128, 1024 + h * 128 + 128)))
    for h in range(8):
        ch.append(list(range(2048 + h * 128, 2048 + h * 128 + 128)))
    ch.append(list(range(6144, 6240)) + [-1] * 32)
    ch.append(list(range(6240, 6336)) + [-1] * 32)
    ch.append(list(range(6336, 6464)))
    ch.append(list(range(6464, 6592)))
    for hp in range(8):
        for base in (3072, 4096, 5120):
            ch.append(list(range(base + hp * 128, base + hp * 128 + 128)))
    return ch


def rest_idx(c):
    if c == 24:
        return 24
    if c == 25:
        return 25
    if c in (26, 27):
        return c
    hp, t = divmod(c - 28, 3)
    return t * 8 + hp


def build(debug=0, stop=99, lite=0, sub=None):
    nc = bass.Bass('TRN2', target_bir_lowering=False)

    def din(name, shape, dt=F32):
        return nc.dram_tensor(name, list(shape), dt, kind="ExternalInput").ap()

    def dout(name, shape, dt=F32):
        return nc.dram_tensor(name, list(shape), dt, kind="ExternalOutput").ap()

    def dscr(name, shape, dt=F32):
        return nc.dram_tensor(name, list(shape), dt, kind="ExternalOutput" if debug else "Internal").ap()

    xin = din("xin", [D, NT])
    cfm_d = din("cfm", [128, KC, 2])
    pars_d = din("pars", [128, NPAR])
    ident_d = din("ident", [128, 128])
    bones_d = din("bones", [128, 128])
    rt_d = din("rt", [128, 128])
    cos_d = din("cos", [128, 1024])
    sin_d = din("sin", [128, 1024])
    cK_d = din("cK", [8, 128, 256])
    cV_d = din("cV", [8, 256, 128])
    s0_d = din("s0", [2, 128, 512])
    wada_d = din("wada", [2, 24, 128, KC, 512] if not lite else [1, 1, 128, 1, 512])
    win_d = din("win", [N_WIN, 128, KC, 128])
    lora_d = din("lora", [128, 3, 2, 1024])
    wout_d = din("wout", [16, 128, KC, 128] if not (lite and stop < 4) else [1, 1, 128, 1, 128])
    wpw1_d = din("wpw1", [32, 128, KC, 128] if not (lite and stop < 6) else [1, 1, 128, 1, 128])
    wpw2_d = din("wpw2", [16, 128, KC, 128] if not (lite and stop < 6) else [1, 1, 128, 1, 128])
    wup_d = din("wup", [2, NFF, 128, KC, 256] if not (lite and stop < 5) else [1, 1, 128, 1, 128])
    wdn_d = din("wdn", [2, NFF, 128, D] if not (lite and stop < 5) else [1, 1, 128, 1, 128])
    if lite:
        class _Dm:
            def __getitem__(self, k):
                return self

            def __getattr__(self, n):
                return lambda *a, **k: self
        if stop < 4:
            wout_d = _Dm()
        if stop < 5:
            wup_d = wdn_d = _Dm()
        if stop < 6:
            wpw1_d = wpw2_d = _Dm()
    y_d = dout("y", [D, NT])
    nk_d = dout("nk", [2, 8, 128, 256])
    nv_d = dout("nv", [2, 8, 256, 128])
    nst_d = dout("nst", [2, 2, 128, 512])
    QS = dscr("QS", [16, 128, NT], BF16)
    VS = dscr("VS", [8, NT, 128], BF16)
    TMS = dscr("TMS", [10, NT, 1024], BF16)
    VBF = dscr("VBF", [1024, NT])
    BON = dscr("BON", [8, 128, NT])
    GATE = dscr("GATE", [8, 128, NT])
    YF = dscr("YF", [2, 1024, NT])
    CONV = dscr("CONV", [16, 128, NT])
    dbg = {}

    with ExitStack() as st:
        P = Prog(nc, st)

        cur = [st]

        def sb(name, shape, dt=F32):
            return cur[-1].enter_context(nc.sbuf_tensor(name, list(shape), dt))

        PS = [st.enter_context(nc.psum_tensor("ps%d" % i, [128, 512], F32)) for i in range(8)]
        psn = [0]

        psh = [0]

        def ps_hi():
            i = 4 + psh[0] % 4
            psh[0] += 1
            return PS[i], 'ps%d' % i

        def ps_next():
            i = psn[0] % 8
            psn[0] += 1
            return PS[i], 'ps%d' % i

        class Ring:
            def __init__(self, name, n, shape, dt=F32):
                self.t = [sb("%s%d" % (name, i), shape, dt) for i in range(n)]
                self.k = ["%s%d" % (name, i) for i in range(n)]
                self.i = 0

            def next(self):
                j = self.i % len(self.t)
                self.i += 1
                return self.t[j], self.k[j]

        def TT(eng, out, in0, in1, op, r, w):
            P.op(eng, lambda e: e.tensor_tensor(out=out, in0=in0, in1=in1, op=op), r, w)

        def TS(eng, out, in0, s1, s2, op0, op1, r, w):
            if s2 is None:
                P.op(eng, lambda e: e.tensor_single_scalar(out=out, in_=in0, scalar=s1, op=op0), r, w)
            else:
                P.op(eng, lambda e: e.tensor_scalar(out=out, in0=in0, scalar1=s1, scalar2=s2, op0=op0, op1=op1), r, w)

        def STT(eng, out, in0, scalar, in1, op0, op1, r, w):
            P.op(eng, lambda e: e.scalar_tensor_tensor(out=out, in0=in0, scalar=scalar, in1=in1, op0=op0, op1=op1), r, w)

        def ACTV(out, in_, func, r, w, bias=None, scale=None):
            kw = {}
            if bias is not None:
                kw['bias'] = bias
            if scale is not None:
                kw['scale'] = scale
            P.op('act', lambda e: e.activation(out=out, in_=in_, func=func, **kw), r, w)

        EPSI = {RMS_EPS: 0, LN_EPS: 1, GN_EPS: 2}

        def RSQ(out, in_, scale, eps, r, w):
            j = EPSI[eps]
            ACTV(out, in_, AF.Ln, list(r) + ['EPS'], w, bias=EPS[:, j:j + 1], scale=scale)
            ACTV(out, out, AF.Exp, w, w, scale=-0.5)

        def CP(eng, out, in_, r, w):
            if eng == 'act':
                P.op('act', lambda e: e.copy(out=out, in_=in_), r, w)
            elif 'PSum' in type(in_.tensor).__name__:
                P.op(eng, lambda e: e.tensor_single_scalar(out=out, in_=in_, scalar=1.0, op=ALU.mult), r, w)
            else:
                P.op(eng, lambda e: e.tensor_copy(out=out, in_=in_), r, w)

        def MM(out, lhsT, rhs, start, stop, r, w):
            P.op('pe', lambda e: e.matmul(out, lhsT=lhsT, rhs=rhs, start=start, stop=stop), r, w)

        def RED(eng, out, in_, r, w):
            P.op(eng, lambda e: e.tensor_reduce(out=out, in_=in_, axis=AX.X, op=ALU.add), r, w)

        def MEMSET(eng, ap, val, w):
            P.op(eng, lambda e: e.memset(ap, val), (), w)

        def chk(k):
            if stop <= k and not P.dead:
                if debug:
                    dx = dout("dbgX", [128, KC, NT])
                    P.dma('sp', dx, X[:], reads=[('X', 0), ('X', 1), ('X', 2)])
                P.dead = True

        X = sb("X", [128, KC, NT])
        PAR = sb("PAR", [128, NPAR])
        IDF = sb("IDF", [128, 128])
        IDB = sb("IDB", [128, 128], BF16)
        ONF = sb("ONF", [128, 128])
        ONB = sb("ONB", [128, 128], BF16)
        BOF = sb("BOF", [128, 128])
        MOD = sb("MOD", [128, 2, 96, 2])
        GG = sb("GG", [128, 2, 2, KC, 2])
        NEGLAM = sb("NEGLAM", [128, 1])
        SUBG = sb("SUBG", [128, 1])
        CM = sb("CM", [128, 28])
        EPS = sb("EPS", [128, 4])

        def par(name, j0=0, n=None):
            o, w = PLAY[name]
            if n is None:
                n = w - j0
            return PAR[:, o + j0:o + j0 + n]

        P.dma('sp', PAR[:], pars_d, writes=['PAR'])
        P.dma('sp', IDF[:], ident_d, writes=['IDF'])
        P.dma('pool', IDB[:], ident_d, writes=['IDB'])
        P.dma('sp', BOF[:], bones_d, writes=['BOF'])
        MEMSET('dve', ONF[:], 1.0, ['ONF'])
        for eps_, j_ in EPSI.items():
            MEMSET('dve', EPS[:, j_:j_ + 1], eps_, ['EPS'])
        MEMSET('dve', ONB[:], 1.0, ['ONB'])
        for kc in range(KC):
            P.dma('sp', X[:, kc, :], xin[kc * 128:(kc + 1) * 128, :], writes=[('X', 0), ('X', 1), ('X', 2)])

        with ExitStack() as ph:
            cur.append(ph)
            sbp = sb
            SC = sbp("SC", [128, KC, 2])
            WA = [sbp("WA%d" % i, [128, KC, 512]) for i in range(2)]
            P.dma('sp', SC[:], cfm_d, writes=['SC'])
            ACTV(SC[:], SC[:], AF.Silu, ['SC'], ['SC'])
            psM, kM = PS[0], 'ps0'
            for l in range(2):
                for jb in range(24 if not lite else 0):
                    wa = WA[jb % 2]
                    wk = 'WA%d' % (jb % 2)
                    P.dma('sp' if jb % 2 == 0 else 'act', wa[:], wada_d[l, jb], writes=[wk])
                    for jj in range(4):
                        j = jb * 4 + jj
                        for kc in range(KC):
                            MM(psM[:, 2 * j:2 * j + 2], wa[:, kc, jj * 128:(jj + 1) * 128], SC[:, kc, :],
                               kc == 0, kc == KC - 1, [wk, 'SC'], [kM])
                bo = PLAY['b_ada%d' % l][0]
                if lite:
                    MEMSET('dve', MOD[:, l], 0.05, ['MOD'])
                else:
                    TT('dve', MOD[:, l], psM[:, 0:192].rearrange("p (j s) -> p j s", s=2),
                       PAR[:, bo:bo + 96].unsqueeze(2).to_broadcast([128, 96, 2]), ALU.add, [kM, 'PAR'], ['MOD'])
                for wh in range(2):
                    go = PLAY['ng%d%d' % (l, wh)][0]
                    STT('dve', GG[:, l, wh], MOD[:, l, (1 + 3 * wh) * 16:(2 + 3 * wh) * 16, :], 1.0,
                        PAR[:, go:go + 16].unsqueeze(2).to_broadcast([128, 16, 2]), ALU.add, ALU.mult,
                        ['MOD', 'PAR'], ['GG'])
            DT = sbp("DT", [128, 2, 64])
            DS = sbp("DS", [128, 2])
            dlo = PLAY['dl'][0]
            dlv = PAR[:, dlo:dlo + 256].rearrange("p (a b k) -> p a b k", a=2, b=2)
            TT('dve', DT[:], dlv[:, :, 0, :], dlv[:, :, 1, :], ALU.mult, ['PAR'], ['DT'])
            RED('dve', DS[:], DT[:], ['DT'], ['DS'])
            ACTV(DS[:], DS[:], AF.Exp, ['DS'], ['DS'])
            TT('dve', NEGLAM[:], DS[:, 1:2], DS[:, 0:1], ALU.subtract, ['DS'], ['NEGLAM'])
            TS('dve', NEGLAM[:], NEGLAM[:], -LAM_INIT, None, ALU.add, None, ['NEGLAM'], ['NEGLAM'])
            TS('dve', SUBG[:], par('subln'), 1.0 - LAM_INIT, None, ALU.mult, None, ['PAR'], ['SUBG'])
            TT('dve', CM[:], par('mu0'), par('mu1'), ALU.add, ['PAR'], ['CM'])
            TS('dve', CM[:], CM[:], -1.0, 1.0, ALU.mult, ALU.add, ['CM'], ['CM'])
            P.barrier()
            cur.pop()

        def norm_mod(H, l, wh, SQ, RSTD, TMPN):
            for tb in range(3):
                s = 1 if tb == 0 else 0
                psN, kN = ps_next()
                for kc in range(KC):
                    sq, sk = SQ.next()
                    ACTV(sq[:], X[:, kc, tb * 512:(tb + 1) * 512], AF.Square, [('X', tb)], [sk])
                    MM(psN[:], ONF[:], sq[:], kc == 0, kc == KC - 1, [sk, 'ONF'], [kN])
                RSQ(RSTD[:], psN[:], 1.0 / D, RMS_EPS, [kN], ['RSTD'])
                for kc in range(KC):
                    tm, tk = TMPN.next()
                    TT('dve', tm[:], X[:, kc, tb * 512:(tb + 1) * 512], RSTD[:], ALU.mult, [('X', tb), 'RSTD'], [tk])
                    ACTV(H[:, kc, tb * 512:(tb + 1) * 512], tm[:], AF.Identity, [tk, 'GG', 'MOD'], [('H', tb)],
                         bias=MOD[:, l, (3 * wh) * 16 + kc, s:s + 1], scale=GG[:, l, wh, kc, s:s + 1])

        def resid_add(ps, kps, l, wh, kc, tb):
            s = 1 if tb == 0 else 0
            xs = X[:, kc, tb * 512:(tb + 1) * 512]
            STT('dve', xs, ps[:], MOD[:, l, (2 + 3 * wh) * 16 + kc, s:s + 1], xs, ALU.mult, ALU.add,
                [kps, 'MOD', ('X', tb)], [('X', tb)])


        def pad_off(t):
            return t + 1 if t < 256 else (t + 2 if t < 512 else t + 3)
        PCS = [(1, 385), (386, 385), (771, 384), (1155, 384)]
        SEQP = [(0, 256, 1), (256, 256, 258), (512, 1024, 515)]
        RS = dscr("RS", [28, 128, NPAD])

        def load_w(WBr, src, eng='pool'):
            wb, wk = WBr.next()
            P.dma(eng, wb[:], src, writes=[wk])
            return wb, wk

        with ExitStack() as ph:
            cur.append(ph)
            H = sb("H", [128, KC, NT], BF16)
            SQ = Ring("SQ", 2, [128, 512])
            TMPN = Ring("TMPN", 2, [128, 512])
            RSTD = sb("RSTD", [128, 512])
            norm_mod(H, 0, 0, SQ, RSTD, TMPN)
            WB = Ring("WB", 2, [128, KC, 128], BF16)
            COS = sb("COS", [128, 1024])
            SIN = sb("SIN", [128, 1024])
            RTF = sb("RTF", [128, 128])
            P.dma('sp', COS[:], cos_d, writes=['COS'])
            P.dma('sp', SIN[:], sin_d, writes=['SIN'])
            P.dma('sp', RTF[:], rt_d, writes=['RTF'])
            RAWP = sb("RAWP", [128, NPAD])
            OUTP = sb("OUTP", [128, NPAD])
            MEMSET('dve', RAWP[:], 0.0, ['RAWP'])
            QB = Ring("QB", 2, [128, 512], BF16)
            XS = Ring("XS", 2, [128, 512])
            T1 = Ring("T1", 2, [128, 512])
            T2 = Ring("T2", 2, [128, 512])
            HK = [('H', 0), ('H', 1), ('H', 2)]

            def proj_tb(wb, wk, tb):
                ps, kp = ps_next()
                for kc in range(KC):
                    MM(ps[:], wb[:, kc, :], H[:, kc, tb * 512:(tb + 1) * 512], kc == 0, kc == KC - 1,
                       [wk, ('H', tb)], [kp])
                return ps, kp

            SUB = sub if sub is not None else 'qvr'
            for c in range(16 if 'q' in SUB else 0):
                wb, wk = load_w(WB, win_d[c])
                if 'L' in SUB:
                    continue
                for tb in range(3):
                    ps, kp = proj_tb(wb, wk, tb)
                    if 'M' in SUB:
                        continue
                    if 'E' in SUB and tb > 0:
                        continue
                    qb, qk = QB.next()
                    if tb == 0:
                        CP('act', qb[:], ps[:], [kp], [qk])
                        if c >= 8 and 'A' not in SUB:
                            xs, xk = XS.next()
                            CP('dve', xs[:], ps[:], [kp, qk] if 'S' in SUB else [kp], [xk])
                            for pi in range(2 if 'C' not in SUB else 0):
                                P.dma('sp', nk_d[pi, c - 8], xs[:, pi * 256:(pi + 1) * 256], reads=[xk])
                    else:
                        xs, xk = XS.next()
                        CP('act', xs[:], ps[:], [kp], [xk])
                        ps2, kp2 = ps_next()
                        MM(ps2[:], RTF[:], xs[:], True, True, ['RTF', xk], [kp2])
                        t1, k1 = T1.next()
                        t2, k2 = T2.next()
                        TT('dve', t1[:], xs[:], COS[:, (tb - 1) * 512:tb * 512], ALU.mult, [xk, 'COS'], [k1])
                        TT('dve', t2[:], ps2[:], SIN[:, (tb - 1) * 512:tb * 512], ALU.mult, [kp2, 'SIN'], [k2])
                        TT('dve', qb[:], t1[:], t2[:], ALU.add, [k1, k2], [qk])
                    if 'B' not in SUB:
                        P.dma('sp', QS[c, :, tb * 512:(tb + 1) * 512], qb[:], reads=[qk], writes=[('QS', c)])
            for h in range(8 if 'v' in SUB else 0):
                wb, wk = load_w(WB, win_d[16 + h])
                for g in range(3):
                    ps, kp = ps_next()
                    for j in range(4):
                        tt = g * 4 + j
                        for kc in range(KC):
                            MM(ps[:, j * 128:(j + 1) * 128], H[:, kc, tt * 128:(tt + 1) * 128], wb[:, kc, :],
                               kc == 0, kc == KC - 1, [wk, ('H', g)], [kp])
                    qb, qk = QB.next()
                    CP('act', qb[:], ps[:], [kp], [qk])
                    P.dma('sp', VS[h, g * 512:(g + 1) * 512, :].rearrange("(j p) d -> p j d", p=128),
                          qb[:].rearrange("p (j d) -> p j d", d=128), reads=[qk], writes=[('VS', h)])
                    if g == 0:
                        xs, xk = XS.next()
                        CP('dve', xs[:], ps[:], [kp], [xk])
                        for pi in range(2):
                            P.dma('sp', nv_d[pi, h].rearrange("(j p) d -> p j d", p=128),
                                  xs[:, pi * 256:(pi + 1) * 256].rearrange("p (j d) -> p j d", d=128), reads=[xk])
            for c in range(24, N_WIN if 'r' in SUB else 24):
                ri = rest_idx(c)
                wb, wk = load_w(WB, win_d[c])
                for tb in range(3):
                    ps, kp = proj_tb(wb, wk, tb)
                    if tb == 0:
                        CP('act', RAWP[:, 1:257], ps[:, 0:256], [kp], ['RAWP'])
                        CP('act', RAWP[:, 258:514], ps[:, 256:512], [kp], ['RAWP'])
                    else:
                        o = 515 + (tb - 1) * 512
                        CP('act', RAWP[:, o:o + 512], ps[:], [kp], ['RAWP'])
                m0o = PLAY['mu0'][0] + ri
                m1o = PLAY['mu1'][0] + ri
                TS('dve', OUTP[:, 1:1539], RAWP[:, 1:1539], CM[:, ri:ri + 1], None, ALU.mult, None, ['RAWP', 'CM'], ['OUTP'])
                STT('dve', OUTP[:, 1:1539], RAWP[:, 0:1538], PAR[:, m0o:m0o + 1], OUTP[:, 1:1539], ALU.mult, ALU.add,
                    ['RAWP', 'PAR', 'OUTP'], ['OUTP'])
                STT('dve', OUTP[:, 1:1539], RAWP[:, 2:1540], PAR[:, m1o:m1o + 1], OUTP[:, 1:1539], ALU.mult, ALU.add,
                    ['RAWP', 'PAR', 'OUTP'], ['OUTP'])
                P.dma('sp', RS[ri], OUTP[:], reads=['OUTP'], writes=[('RS', ri)])
            P.barrier()
            cur.pop()

        chk(1)
        with ExitStack() as ph:
            cur.append(ph)
            LORA = sb("LORA", [128, 3, 2, 1024], BF16)
            P.dma('pool', LORA[:], lora_d, writes=['LORA'])
            TWr = sb("TWr", [128, NPAD])
            TW = sb("TW", [128, NPAD], BF16)
            XA = sb("XA", [128, NPAD], BF16)
            SG = sb("SG", [128, 2, NPAD], BF16)
            OMK = sb("OMK", [128, 8])
            TS('dve', OMK[:], par('k_a'), -1.0, 1.0, ALU.mult, ALU.add, ['PAR'], ['OMK'])
            P.dma('sp', TWr[:], RS[24], writes=['TWr'])
            ACTV(TW[:], TWr[:], AF.Tanh, ['TWr'], ['TW'])
            P.dma('sp', TWr[:], RS[25], reads=[], writes=['TWr'])
            CP('act', XA[:], TWr[:], ['TWr'], ['XA'])
            for i in range(2):
                P.dma('sp', TWr[:], RS[26 + i], writes=['TWr'])
                ACTV(SG[:, i, :], TWr[:], AF.Sigmoid, ['TWr'], ['SG'])
            Rb = sb("Rb", [128, NPAD])
            KBb = sb("KBb", [128, NPAD])
            VBb = sb("VBb", [128, NPAD])
            KKb = sb("KKb", [128, NPAD])
            Ab = sb("Ab", [128, NPAD])
            TA = sb("TA", [128, NPAD])
            TBb = sb("TBb", [128, NPAD])
            WDb = sb("WDb", [128, NPAD])
            ALRb = sb("ALRb", [128, NPAD])
            KDb = sb("KDb", [128, NPAD])
            BDb = sb("BDb", [128, NPAD])
            TMB = Ring("TMB", 4, [128, 512], BF16)
            SL = slice(1, 1539)

            def to_tm(arr, akey, idx, hp, hilo=False):
                for g in range(3):
                    ps, kp = ps_next()
                    for j in range(4):
                        o = pad_off((g * 4 + j) * 128)
                        MM(ps[:, j * 128:(j + 1) * 128], arr[:, o:o + 128], IDF[:], True, True, [akey, 'IDF'], [kp])
                    tmb, tk = TMB.next()
                    CP('act', tmb[:], ps[:], [kp], [tk])
                    dst = TMS[idx, g * 512:(g + 1) * 512, hp * 128:(hp + 1) * 128].rearrange("(j p) c -> p j c", p=128)
                    P.dma('sp', dst, tmb[:].rearrange("p (j c) -> p j c", c=128), reads=[tk], writes=[('TMS', idx, hp)])
                    if hilo:
                        tml, tlk = TMB.next()
                        TT('dve', tml[:], ps[:], tmb[:], ALU.subtract, [kp, tk], [tlk])
                        dst2 = TMS[idx + 1, g * 512:(g + 1) * 512, hp * 128:(hp + 1) * 128].rearrange("(j p) c -> p j c", p=128)
                        P.dma('sp', dst2, tml[:].rearrange("p (j c) -> p j c", c=128), reads=[tlk], writes=[('TMS', idx + 1, hp)])

            def seq_store(dst_fn, arr, akey, wkey):
                for (t0, ln, po) in SEQP:
                    P.dma('sp', dst_fn(t0, ln), arr[:, po:po + ln], reads=[akey], writes=[wkey])

            for hp in range(8):
                P.dma('sp', Rb[:], RS[hp], writes=['Rb'])
                P.dma('sp', KBb[:], RS[8 + hp], writes=['KBb'])
                P.dma('sp', VBb[:], RS[16 + hp], writes=['VBb'])
                seq_store(lambda t0, ln: VBF[hp * 128:(hp + 1) * 128, t0:t0 + ln], VBb, 'VBb', ('VBF', hp))
                kko = PLAY['k_k'][0] + hp
                TS('dve', KKb[:, SL], KBb[:, SL], PAR[:, kko:kko + 1], None, ALU.mult, None, ['KBb', 'PAR'], ['KKb'])
                TT('dve', TA[:, SL], KKb[:, SL], KKb[:, SL], ALU.mult, ['KKb'], ['TA'])
                for (o, n) in PCS:
                    ps, kp = ps_next()
                    MM(ps[:, 0:n], BOF[:], TA[:, o:o + n], True, True, ['BOF', 'TA'], [kp])
                    ACTV(TBb[:, o:o + n], ps[:, 0:n], AF.Sqrt, [kp], ['TBb'])
                TS('dve', TBb[:, SL], TBb[:, SL], 1e-12, None, ALU.max, None, ['TBb'], ['TBb'])
                P.op('dve', lambda e: e.reciprocal(out=TBb[:, SL], in_=TBb[:, SL]), ['TBb'], ['TBb'])
                TT('dve', KKb[:, SL], KKb[:, SL], TBb[:, SL], ALU.mult, ['KKb', 'TBb'], ['KKb'])
                TS('dve', Ab[:, SL], KKb[:, SL], -1.0, None, ALU.mult, None, ['KKb'], ['Ab'])
                to_tm(Ab, 'Ab', 0, hp)
                to_tm(Rb, 'Rb', 1, hp)
                rko = PLAY['r_k'][0] + hp
                TT('dve', TA[:, SL], Rb[:, SL], KBb[:, SL], ALU.mult, ['Rb', 'KBb'], ['TA'])
                TS('dve', TA[:, SL], TA[:, SL], PAR[:, rko:rko + 1], None, ALU.mult, None, ['TA', 'PAR'], ['TA'])
                for (o, n) in PCS:
                    ps, kp = ps_next()
                    MM(ps[:, 0:n], BOF[:], TA[:, o:o + n], True, True, ['BOF', 'TA'], [kp])
                    TT('dve', TBb[:, o:o + n], ps[:, 0:n], VBb[:, o:o + n], ALU.mult, [kp, 'VBb'], ['TBb'])
                seq_store(lambda t0, ln: BON[hp, :, t0:t0 + ln], TBb, 'TBb', ('BON', hp))
                for (o, n) in PCS:
                    ps, kp = ps_next()
                    for kc in range(2):
                        MM(ps[:, 0:n], LORA[:, 2, kc, hp * 128:(hp + 1) * 128], SG[:, kc, o:o + n], kc == 0, kc == 1,
                           ['LORA', 'SG'], [kp])
                    CP('act', TA[:, o:o + n], ps[:, 0:n], [kp], ['TA'])
                seq_store(lambda t0, ln: GATE[hp, :, t0:t0 + ln], TA, 'TA', ('GATE', hp))
                for d in range(2):
                    w0o = PLAY['w0_%d' % d][0] + hp
                    a0o = PLAY['a0_%d' % d][0] + hp
                    kao = PLAY['k_a'][0] + hp
                    for (o, n) in PCS:
                        ps, kp = ps_next()
                        MM(ps[:, 0:n], LORA[0:96, 0, d, hp * 128:(hp + 1) * 128], TW[0:96, o:o + n], True, True,
                           ['LORA', 'TW'], [kp])
                        ACTV(WDb[:, o:o + n], ps[:, 0:n], AF.Sigmoid, [kp, 'PAR'], ['WDb'], bias=PAR[:, w0o:w0o + 1])
                        ps2, kp2 = ps_next()
                        MM(ps2[:, 0:n], LORA[0:96, 1, d, hp * 128:(hp + 1) * 128], XA[0:96, o:o + n], True, True,
                           ['LORA', 'XA'], [kp2])
                        ACTV(ALRb[:, o:o + n], ps2[:, 0:n], AF.Sigmoid, [kp2, 'PAR'], ['ALRb'], bias=PAR[:, a0o:a0o + 1])
                    ACTV(WDb[:, SL], WDb[:, SL], AF.Exp, ['WDb'], ['WDb'], scale=-math.exp(-0.5))
                    TS('dve', KDb[:, SL], ALRb[:, SL], PAR[:, kao:kao + 1], OMK[:, hp:hp + 1], ALU.mult, ALU.add,
                       ['ALRb', 'PAR', 'OMK'], ['KDb'])
                    TT('dve', KDb[:, SL], KDb[:, SL], KBb[:, SL], ALU.mult, ['KDb', 'KBb'], ['KDb'])
                    TT('dve', BDb[:, SL], KKb[:, SL], ALRb[:, SL], ALU.mult, ['KKb', 'ALRb'], ['BDb'])
                    to_tm(WDb, 'WDb', 2 + 4 * d, hp, hilo=True)
                    to_tm(BDb, 'BDb', 4 + 4 * d, hp)
                    to_tm(KDb, 'KDb', 5 + 4 * d, hp)
            P.barrier()
            cur.pop()

        chk(2)
        with ExitStack() as ph:
            cur.append(ph)
            ia = IDF[:]
            pstep = ia.ap[0][0]
            ia = IDB[:]
            pstep = ia.ap[0][0]
            SELM = sb("SELM", [128, 64, 128], BF16)
            for u_ in range(2):
                src_ = bass.AP(ia.tensor, ia.offset + 64 * u_, [[pstep, 128], [1, 64], [0, 64]])
                CP('dve', SELM[:, :, u_ * 64:(u_ + 1) * 64], src_, ['IDB'], ['SELM'])
            P.barrier()
            groups = [[('Sf', 2, 0, 0, None, 'dve'), ('Sb', 2, 1, 1, None, 'pool')],
                      [('P0f', 0, 0, None, (0, 0), 'dve'), ('P0b', 0, 1, None, (0, 1), 'pool')],
                      [('P1f', 1, 0, None, (1, 0), 'dve'), ('P1b', 1, 1, None, (1, 1), 'pool')]]
            def tms_idx(slot, dr):
                return [0, 2 + 4 * dr, 3 + 4 * dr, 4 + 4 * dr, 5 + 4 * dr, 1][slot]
            for grp in groups:
                with ExitStack() as gs:
                    cur.append(gs)
                    tl = []
                    for (nm, si, dr, s0i, outi, teng) in grp:
                        t = dict(nm=nm, dr=dr, outi=outi, eng=teng)
                        t['WS'] = [sb("WS%d_%s" % (i, nm), [128, 8, 64]) for i in range(2)]
                        t['T3'] = sb("T3_" + nm, [128, 8, 64])
                        t['off'], t['T'] = SEQS[si]
                        t['S'] = sb("S_" + nm, [128, 8, 64])
                        t['TMP'] = sb("TMP_" + nm, [128, 8, 64])
                        t['T2'] = sb("T2_" + nm, [128, 8, 64])
                        t['SA'] = sb("SA_" + nm, [128, 8])
                        t['VV'] = [sb("VV%d_%s" % (i, nm), [128, 8, 64]) for i in range(2)]
                        t['YT'] = [sb("YT%d_%s" % (i, nm), [128, 8, 64]) for i in range(2)]
                        t['TMX'] = [sb("TMX%d_%s" % (i, nm), [128, 6, 512], BF16) for i in range(2)]
                        if s0i is not None:
                            P.dma('sp', t['S'][:].rearrange("p h k -> p (h k)"), s0_d[s0i], writes=['S_' + nm])
                        else:
                            MEMSET('dve', t['S'][:], 0.0, ['S_' + nm])
                        tl.append(t)
                    nch = tl[0]['T'] // 64
                    def load_chunk(cq):
                        for t in tl:
                            nm = t['nm']
                            b = cq % 2
                            tok0 = t['off'] + (cq * 64 if t['dr'] == 0 else t['T'] - 64 * (cq + 1))
                            for ti in range(6):
                                idx = tms_idx(ti, t['dr'])
                                for u in range(2):
                                    P.dma('sp' if u == 0 else 'act', t['TMX'][b][u * 64:(u + 1) * 64, ti, :],
                                          TMS[idx, tok0:tok0 + 64, u * 512:(u + 1) * 512],
                                          writes=[('TMX', nm, b)])
                            for u in range(2):
                                src = bass.AP(VBF.tensor, VBF.offset + (u * 512) * NT + tok0,
                                              [[NT, 64], [64 * NT, 8], [1, 64]])
                                P.dma('sp', t['VV'][b][u * 64:(u + 1) * 64, :, :], src, writes=[('VV', nm, b)])

                    for c in range(nch):
                        if c == 0:
                            load_chunk(0)
                        if c + 1 < nch:
                            load_chunk(c + 1)
                        def emit_back(bk):
                            (t_, S_, sk_, TMP_, tk_, pR_, kR_, b_, sel_) = bk
                            TT('dve', TMP_[:], S_[:], pR_, ALU.mult, [sk_, kR_], [tk_])
                            RED('dve', t_['YT'][b_][:, :, sel_], TMP_[:], [tk_], [('YT', t_['nm'], b_)])

                        pend = None
                        usc = [0]
                        for i in range(64):
                            for tix, t in enumerate(tl):
                                nm = t['nm']
                                b = c % 2
                                sel = i if t['dr'] == 0 else 63 - i
                                lhsT = SELM[:, sel, :]
                                S = t['S']
                                sk = 'S_' + nm
                                pss = []
                                for slots in ((0,), (1, 2), (3,), (4,), (5,)):
                                    ps, kp = ps_next()
                                    for si_, sl_ in enumerate(slots):
                                        MM(ps[:], lhsT, t['TMX'][b][:, sl_, :], si_ == 0, si_ == len(slots) - 1,
                                           [('TMX', nm, b)], [kp])
                                    pss.append((ps[:].rearrange("p (h k) -> p h k", k=64), kp))
                                (pA, kA), (pW, kW), (pB, kB), (pK, kK), (pR, kR) = pss
                                TMP = t['TMP'][tix % 2] if isinstance(t['TMP'], list) else t['TMP']
                                T2 = t['T2']
                                T3 = t['T3']
                                SA = t['SA']
                                WS = t['WS'][i % 2]
                                wsk = ('WS', nm, i % 2)
                                tk, t2k, t3k, sak = 'TMP_' + nm, 'T2_' + nm, 'T3_' + nm, 'SA_' + nm
                                CP('act', WS[:], pW, [kW], [wsk])
                                TT('dve', TMP[:], S[:], pA, ALU.mult, [sk, kA], [tk])
                                RED('dve', SA[:], TMP[:], [tk], [sak])
                                TT('dve', T2[:], pB, SA[:].unsqueeze(2).to_broadcast([128, 8, 64]), ALU.mult, [kB, sak], [t2k])
                                for hh in range(8):
                                    ACTV(T3[:, hh, :], pK[:, hh, :], AF.Identity, [kK, ('VV', nm, b)], [(t3k, hh)],
                                         scale=t['VV'][b][:, hh, sel:sel + 1])
                                TT('pool', S[:], S[:], WS[:], ALU.mult, [sk, wsk], [sk])
                                TT('pool', S[:], S[:], T2[:], ALU.add, [sk, t2k], [sk])
                                TT('pool', S[:], S[:], T3[:], ALU.add, [sk] + [(t3k, hh) for hh in range(8)], [sk])
                                bk = (t, S, sk, TMP, tk, pR, kR, b, sel)
                                if tix == 0:
                                    if pend is not None:
                                        emit_back(pend)
                                        pend = None
                                    backA = bk
                                else:
                                    emit_back(backA)
                                    pend = bk
                        if pend is not None:
                            emit_back(pend)
                            pend = None
                        for t in tl:
                            nm = t['nm']
                            b = c % 2
                            tok0 = t['off'] + (c * 64 if t['dr'] == 0 else t['T'] - 64 * (c + 1))
                            for u in range(2):
                                dst = bass.AP(YF.tensor, YF.offset + t['dr'] * 1024 * NT + (u * 512) * NT + tok0,
                                              [[NT, 64], [64 * NT, 8], [1, 64]])
                                P.dma('sp', dst, t['YT'][b][u * 64:(u + 1) * 64, :, :], reads=[('YT', nm, b)],
                                      writes=[('YF', t['dr'])])
                    for t in tl:
                        if t['outi'] is not None:
                            pi, di = t['outi']
                            P.dma('sp', nst_d[pi, di], t['S'][:].rearrange("p h k -> p (h k)"), reads=['S_' + t['nm']])
                    P.barrier()
                    cur.pop()
            cur.pop()

        chk(3)
        with ExitStack() as ph:
            cur.append(ph)
            OA = sb("OA", [128, 8, NT], BF16)
            OB = sb("OB", [128, 8, NT], BF16)
            with ExitStack() as p2:
                cur.append(p2)
                Y0 = sb("Y0", [128, NT])
                Y1 = sb("Y1", [128, NT])
                BNb = sb("BNb", [128, NT])
                GTb = sb("GTb", [128, NT])
                DDb = sb("DDb", [128, 512])
                SQb = sb("SQb", [128, 512])
                RSb = sb("RSb", [128, 512])
                for hp in range(8):
                    P.dma('sp', Y0[:], YF[0, hp * 128:(hp + 1) * 128, :], writes=['Y0'])
                    P.dma('act', Y1[:], YF[1, hp * 128:(hp + 1) * 128, :], writes=['Y1'])
                    P.dma('sp', BNb[:], BON[hp], writes=['BNb'])
                    P.dma('act', GTb[:], GATE[hp], writes=['GTb'])
                    TT('dve', Y0[:], Y0[:], Y1[:], ALU.add, ['Y0', 'Y1'], ['Y0'])
                    lgo = PLAY['lnx_g'][0] + hp
                    lbo = PLAY['lnx_b'][0] + hp
                    for tb in range(3):
                        ts_ = slice(tb * 512, (tb + 1) * 512)
                        ps, kp = ps_next()
                        MM(ps[:], BOF[:], Y0[:, ts_], True, True, ['BOF', 'Y0'], [kp])
                        STT('dve', DDb[:], ps[:], -1.0 / 64, Y0[:, ts_], ALU.mult, ALU.add, [kp, 'Y0'], ['DDb'])
                        TT('dve', SQb[:], DDb[:], DDb[:], ALU.mult, ['DDb'], ['SQb'])
                        ps2, kp2 = ps_next()
                        MM(ps2[:], BOF[:], SQb[:], True, True, ['BOF', 'SQb'], [kp2])
                        RSQ(RSb[:], ps2[:], 1.0 / 64, GN_EPS, [kp2], ['RSb'])
                        TT('dve', DDb[:], DDb[:], RSb[:], ALU.mult, ['DDb', 'RSb'], ['DDb'])
                        TS('dve', DDb[:], DDb[:], PAR[:, lgo:lgo + 1], PAR[:, lbo:lbo + 1], ALU.mult, ALU.add,
                           ['DDb', 'PAR'], ['DDb'])
                        TT('dve', DDb[:], DDb[:], BNb[:, ts_], ALU.add, ['DDb', 'BNb'], ['DDb'])
                        TT('dve', OB[:, hp, ts_], DDb[:], GTb[:, ts_], ALU.mult, ['DDb', 'GTb'], ['OB'])
                P.barrier()
                cur.pop()
            with ExitStack() as p2:
                cur.append(p2)
                QhR = [sb("Qh%d" % i, [128, NT], BF16) for i in range(2)]
                KhR = [sb("Kh%d" % i, [128, NT], BF16) for i in range(2)]
                VTR = [sb("VT%d" % i, [128, 12, 128], BF16) for i in range(2)]
                CKR = [sb("CK%d" % i, [128, 256], BF16) for i in range(2)]
                CVR = [sb("CVt%d" % i, [128, 2, 128], BF16) for i in range(2)]
                EX = Ring("EX", 3, [128, 512], BF16)
                R1 = sb("R1", [128, 512])
                O1 = sb("O1", [128, 512])
                O2 = sb("O2", [128, 512])
                SQa = sb("SQa", [128, 512])
                RSa = sb("RSa", [128, 512])
                for h in range(8):
                    hb = h % 2
                    Qh, Kh, VT, CK, CVt = QhR[hb], KhR[hb], VTR[hb], CKR[hb], CVR[hb]
                    kQh, kKh, kVT, kCK, kCV = 'Qh%d' % hb, 'Kh%d' % hb, 'VT%d' % hb, 'CK%d' % hb, 'CVt%d' % hb
                    P.dma('sp', Qh[:], QS[h], writes=[kQh])
                    P.dma('sp', Kh[:], QS[8 + h], writes=[kKh])
                    P.dma('sp', VT[:], VS[h].rearrange("(j p) d -> p j d", p=128), writes=[kVT])
                    P.dma('pool', CK[:], cK_d[h], writes=[kCK])
                    P.dma('pool', CVt[:], cV_d[h].rearrange("(j p) d -> p j d", p=128), writes=[kCV])
                    jobs = [(0, 256, [('n', 0), ('n', 1)]), (256, 256, [('n', 2), ('n', 3)]),
                            (512, 512, [('c', 0), ('c', 1)] + [('n', j) for j in range(4, 12)]),
                            (1024, 512, [('c', 0), ('c', 1)] + [('n', j) for j in range(4, 12)])]
                    for (q0, nq, kts) in jobs:
                        acc = []
                        for m in range(2):
                            psO, kO = PS[2 * m], 'ps%d' % (2 * m)
                            psD, kD = PS[2 * m + 1], 'ps%d' % (2 * m + 1)
                            for ki, (kind, j) in enumerate(kts):
                                if kind == 'n':
                                    ksrc = Kh[m * 64:(m + 1) * 64, j * 128:(j + 1) * 128]
                                    vsrc = VT[:, j, :]
                                    kr, vr = kKh, kVT
                                else:
                                    ksrc = CK[m * 64:(m + 1) * 64, j * 128:(j + 1) * 128]
                                    vsrc = CVt[:, j, :]
                                    kr, vr = kCK, kCV
                                psS, kS = ps_hi()
                                MM(psS[:, 0:nq], ksrc, Qh[m * 64:(m + 1) * 64, q0:q0 + nq], True, True, [kr, kQh], [kS])
                                ex, ek = EX.next()
                                ACTV(ex[:, 0:nq], psS[:, 0:nq], AF.Exp, [kS], [ek], scale=0.125)
                                MM(psO[:, 0:nq], vsrc, ex[:, 0:nq], ki == 0, ki == len(kts) - 1, [vr, ek], [kO])
                                MM(psD[:, 0:nq], ONB[:], ex[:, 0:nq], ki == 0, ki == len(kts) - 1, ['ONB', ek], [kD])
                            acc.append((psO, kO, psD, kD))
                        (pO1, kO1, pD1, kD1), (pO2, kO2, pD2, kD2) = acc
                        n_ = slice(0, nq)
                        P.op('dve', lambda e, a=R1[:, n_], b=pD1[:, n_]: e.reciprocal(out=a, in_=b), [kD1], ['R1'])
                        TT('dve', O1[:, n_], pO1[:, n_], R1[:, n_], ALU.mult, [kO1, 'R1'], ['O1'])
                        P.op('dve', lambda e, a=R1[:, n_], b=pD2[:, n_]: e.reciprocal(out=a, in_=b), [kD2, 'O1'], ['R1'])
                        TT('dve', O2[:, n_], pO2[:, n_], R1[:, n_], ALU.mult, [kO2, 'R1'], ['O2'])
                        STT('dve', O1[:, n_], O2[:, n_], NEGLAM[:, 0:1], O1[:, n_], ALU.mult, ALU.add,
                            ['O2', 'O1', 'NEGLAM'], ['O1'])
                        TT('dve', SQa[:, n_], O1[:, n_], O1[:, n_], ALU.mult, ['O1'], ['SQa'])
                        psq, kq = ps_hi()
                        MM(psq[:, n_], ONF[:], SQa[:, n_], True, True, ['ONF', 'SQa'], [kq])
                        RSQ(RSa[:, n_], psq[:, n_], 1.0 / 128, LN_EPS, [kq], ['RSa'])
                        TT('dve', O1[:, n_], O1[:, n_], RSa[:, n_], ALU.mult, ['O1', 'RSa'], ['O1'])
                        TS('dve', OA[:, h, q0:q0 + nq], O1[:, n_], SUBG[:, 0:1], None, ALU.mult, None, ['O1', 'SUBG'], ['OA'])
                P.barrier()
                cur.pop()
            WB = Ring("WBo", 2, [128, KC, 128], BF16)
            for c in range(16):
                wb, wk = load_w(WB, wout_d[c])
                for tb in range(3):
                    ps, kp = ps_next()
                    for kc in range(KC):
                        src = OA if kc < 8 else OB
                        MM(ps[:], wb[:, kc, :], src[:, kc % 8, tb * 512:(tb + 1) * 512], kc == 0, kc == KC - 1,
                           [wk, 'OA', 'OB'], [kp])
                    resid_add(ps, kp, 0, 0, c, tb)
            P.barrier()
            cur.pop()

        chk(4)
        TBR = [[(0, 256, 1), (256, 256, 258)], [(0, 512, 515)], [(0, 512, 1027)]]

        def ffn(l):
            with ExitStack() as ph:
                cur.append(ph)
                H = sb("Hf%d" % l, [128, KC, NT], BF16)
                with ExitStack() as p2:
                    cur.append(p2)
                    SQ = Ring("SQf%d" % l, 2, [128, 512])
                    TMPN = Ring("TMPNf%d" % l, 2, [128, 512])
                    RSTD = sb("RSTDf%d" % l, [128, 512])
                    norm_mod(H, l, 1, SQ, RSTD, TMPN)
                    P.barrier()
                    cur.pop()
                G = 2
                WUr = Ring("WU%d_" % l, 2, [128, KC, 256], BF16)
                WDG = sb("WDG%d" % l, [128, G, D], BF16)
                H2G = sb("H2G%d" % l, [128, G, NPAD], BF16)
                UP = [sb("UP%d_%d" % (l, i), [128, NPAD]) for i in range(2)]
                CVg = sb("CVg%d" % l, [128, NPAD])
                for i in range(2):
                    MEMSET('dve', UP[i][:], 0.0, ['UP%d' % i])
                fo = PLAY['fdw%d' % l][0]
                SL = slice(1, 1539)
                c = 0
                while c < NFF:
                    gn = min(G, NFF - c)
                    for g in range(gn):
                        cc = c + g
                        WU, wuk = WUr.next()
                        P.dma('pool', WU[:], wup_d[l, cc], writes=[wuk])
                        P.dma('pool', WDG[:, g, :], wdn_d[l, cc], writes=[('WDG', g)])
                        for gv in range(2):
                            for tb in range(3):
                                ps, kp = ps_next()
                                for kc in range(KC):
                                    MM(ps[:], WU[:, kc, gv * 128:(gv + 1) * 128], H[:, kc, tb * 512:(tb + 1) * 512],
                                       kc == 0, kc == KC - 1, [wuk, ('H', tb)], [kp])
                                for (pc, n, po) in TBR[tb]:
                                    CP('act', UP[gv][:, po:po + n], ps[:, pc:pc + n], [kp], ['UP%d' % gv])
                            to = fo + (cc * 2 + gv) * 3
                            dst, dk = (CVg, 'CVg') if gv == 0 else (UP[0], 'UP0')
                            TS('dve', dst[:, SL], UP[gv][:, SL], PAR[:, to + 1:to + 2], None, ALU.mult, None,
                               ['UP%d' % gv, 'PAR'], [dk])
                            STT('dve', dst[:, SL], UP[gv][:, 0:1538], PAR[:, to:to + 1], dst[:, SL], ALU.mult, ALU.add,
                                ['UP%d' % gv, 'PAR', dk], [dk])
                            STT('dve', dst[:, SL], UP[gv][:, 2:1540], PAR[:, to + 2:to + 3], dst[:, SL], ALU.mult, ALU.add,
                                ['UP%d' % gv, 'PAR', dk], [dk])
                            if gv == 0:
                                ACTV(CVg[:, SL], CVg[:, SL], AF.Silu, ['CVg'], ['CVg'])
                        TT('dve', H2G[:, g, SL], CVg[:, SL], UP[0][:, SL], ALU.mult, ['CVg', 'UP0'], [('H2G', g)])
                        MEMSET('dve', UP[0][:, 257:258], 0.0, ['UP0'])
                        MEMSET('dve', UP[0][:, 514:515], 0.0, ['UP0'])
                    for ko in range(KC):
                        for tb in range(3):
                            ps, kp = ps_next()
                            for (pc, n, po) in TBR[tb]:
                                for g in range(gn):
                                    MM(ps[:, pc:pc + n], WDG[:, g, ko * 128:(ko + 1) * 128], H2G[:, g, po:po + n],
                                       g == 0, g == gn - 1, [('WDG', g), ('H2G', g)], [kp])
                            resid_add(ps, kp, l, 1, ko, tb)
                    c += gn
                P.barrier()
                cur.pop()

        ffn(0)

        chk(5)
        NC_ = NT + 60
        CO = [15, 286, 557]
        TBC = [[(0, 256, 15), (256, 256, 286)], [(0, 512, 557)], [(0, 512, 1069)]]
        with ExitStack() as ph:
            cur.append(ph)
            H = sb("Hc", [128, KC, NT], BF16)
            with ExitStack() as p2:
                cur.append(p2)
                SQ = Ring("SQc", 2, [128, 512])
                TMPN = Ring("TMPNc", 2, [128, 512])
                RSTD = sb("RSTDc", [128, 512])
                norm_mod(H, 1, 0, SQ, RSTD, TMPN)
                P.barrier()
                cur.pop()
            WBa = Ring("WBa", 2, [128, KC, 128], BF16)
            UAr = [sb("UA%d" % i, [128, NC_]) for i in range(2)]
            ACC = sb("ACC", [128, NC_])
            SQc = sb("SQc2", [128, NC_])
            MEAN = sb("MEAN", [128, 3, 512])
            RSC = sb("RSC", [128, 3, 512])
            for i in range(2):
                MEMSET('dve', UAr[i][:], 0.0, ['UA%d' % i])
            wdo = PLAY['w_dw'][0]
            bdo = PLAY['b_dw'][0]
            CS = slice(15, 1581)
            for c in range(16):
                wa, wak = load_w(WBa, wpw1_d[c])
                wg, wgk = load_w(WBa, wpw1_d[16 + c])
                UA, uak = UAr[c % 2], 'UA%d' % (c % 2)
                for tb in range(3):
                    psA, kA = PS[6], 'ps6'
                    psB, kB = PS[7], 'ps7'
                    for kc in range(KC):
                        MM(psA[:], wa[:, kc, :], H[:, kc, tb * 512:(tb + 1) * 512], kc == 0, kc == KC - 1, [wak, ('H', tb)], [kA])
                    for kc in range(KC):
                        MM(psB[:], wg[:, kc, :], H[:, kc, tb * 512:(tb + 1) * 512], kc == 0, kc == KC - 1, [wgk, ('H', tb)], [kB])
                    for (pc, n, po) in TBC[tb]:
                        ACTV(UA[:, po:po + n], psB[:, pc:pc + n], AF.Sigmoid, [kB], [uak])
                        TT('dve', UA[:, po:po + n], psA[:, pc:pc + n], UA[:, po:po + n], ALU.mult, [kA, uak], [uak])
                TS('dve', ACC[:, CS], UA[:, 0:1566], PAR[:, wdo + c * 31:wdo + c * 31 + 1], PAR[:, bdo + c:bdo + c + 1],
                   ALU.mult, ALU.add, [uak, 'PAR'], ['ACC'])
                for j in range(1, 31):
                    STT('dve', ACC[:, CS], UA[:, j:j + 1566], PAR[:, wdo + c * 31 + j:wdo + c * 31 + j + 1], ACC[:, CS],
                        ALU.mult, ALU.add, [uak, 'PAR', 'ACC'], ['ACC'])
                TT('dve', SQc[:, CS], ACC[:, CS], ACC[:, CS], ALU.mult, ['ACC'], ['SQc'])
                for tb in range(3):
                    for (pc, n, po) in TBC[tb]:
                        MM(PS[tb][:, pc:pc + n], ONF[:], ACC[:, po:po + n], c == 0, c == 15, ['ONF', 'ACC'], ['ps%d' % tb])
                        MM(PS[3 + tb][:, pc:pc + n], ONF[:], SQc[:, po:po + n], c == 0, c == 15, ['ONF', 'SQc'], ['ps%d' % (3 + tb)])
                for (t0, ln, po) in [(0, 256, 15), (256, 256, 286), (512, 1024, 557)]:
                    P.dma('sp', CONV[c, :, t0:t0 + ln], ACC[:, po:po + ln], reads=['ACC'], writes=[('CONV', c)])
            for tb in range(3):
                TS('dve', MEAN[:, tb, :], PS[tb][:], 1.0 / D, None, ALU.mult, None, ['ps%d' % tb], ['MEAN'])
                TT('dve', RSC[:, tb, :], MEAN[:, tb, :], MEAN[:, tb, :], ALU.mult, ['MEAN'], ['RSC'])
                STT('dve', RSC[:, tb, :], PS[3 + tb][:], 1.0 / D, RSC[:, tb, :], ALU.mult, ALU.subtract,
                    ['ps%d' % (3 + tb), 'RSC'], ['RSC'])
                RSQ(RSC[:, tb, :], RSC[:, tb, :], 1.0, LN_EPS, ['RSC'], ['RSC'])
            CL = Ring("CL", 1, [128, NT])
            MV = MEAN[:].rearrange("p a b -> p (a b)")
            RV = RSC[:].rearrange("p a b -> p (a b)")
            cgo = PLAY['cln_g'][0]
            cbo = PLAY['cln_b'][0]
            for c in range(16):
                cl, ck = CL.next()
                P.dma('sp', cl[:], CONV[c], reads=[('CONV', c)], writes=[ck])
                TT('dve', cl[:], cl[:], MV, ALU.subtract, [ck, 'MEAN'], [ck])
                TT('dve', cl[:], cl[:], RV, ALU.mult, [ck, 'RSC'], [ck])
                TS('dve', cl[:], cl[:], PAR[:, cgo + c:cgo + c + 1], PAR[:, cbo + c:cbo + c + 1], ALU.mult, ALU.add,
                   [ck, 'PAR'], [ck])
                ACTV(H[:, c, :], cl[:], AF.Silu, [ck], [('H', 0), ('H', 1), ('H', 2)])
            for c in range(16):
                wb, wk = load_w(WBa, wpw2_d[c])
                for tb in range(3):
                    ps, kp = PS[6 + (tb % 2)], 'ps%d' % (6 + (tb % 2))
                    for kc in range(KC):
                        MM(ps[:], wb[:, kc, :], H[:, kc, tb * 512:(tb + 1) * 512], kc == 0, kc == KC - 1, [wk, ('H', tb)], [kp])
                    resid_add(ps, kp, 1, 0, c, tb)
            P.barrier()
            cur.pop()

        chk(6)
        ffn(1)
        chk(7)

        with ExitStack() as ph:
            cur.append(ph)
            SQ = Ring("SQz", 2, [128, 512])
            YO = Ring("YO", 3, [128, 512])
            RSTD = sb("RSTDz", [128, 512])
            fo_ = PLAY['fng'][0]
            for tb in range(3):
                psN, kN = ps_next()
                for kc in range(KC):
                    sq, sk = SQ.next()
                    ACTV(sq[:], X[:, kc, tb * 512:(tb + 1) * 512], AF.Square, [('X', tb)], [sk])
                    MM(psN[:], ONF[:], sq[:], kc == 0, kc == KC - 1, [sk, 'ONF'], [kN])
                RSQ(RSTD[:], psN[:], 1.0 / D, RMS_EPS, [kN], ['RSTDz'])
                for kc in range(KC):
                    yo, yk = YO.next()
                    TT('dve', yo[:], X[:, kc, tb * 512:(tb + 1) * 512], RSTD[:], ALU.mult, [('X', tb), 'RSTDz'], [yk])
                    TS('dve', yo[:], yo[:], PAR[:, fo_ + kc:fo_ + kc + 1], None, ALU.mult, None, [yk, 'PAR'], [yk])
                    P.dma('sp', y_d[kc * 128:(kc + 1) * 128, tb * 512:(tb + 1) * 512], yo[:], reads=[yk])
            cur.pop()
        if debug:
            print('instr counts', {k: len(v) for k, v in P.prog.items()}, 'sems', P.nsem)
        P.finish()
    return nc


def _fm(v, n=None):
    v = np.asarray(v, np.float32).reshape(-1)
    return np.ascontiguousarray(v.reshape(-1, 128).T)


def _arr_w(W, cols=None):
    K, N = W.shape
    if cols is None:
        Wc = W.reshape(K // 128, 128, N // 128, 128)
        return np.ascontiguousarray(Wc.transpose(2, 1, 0, 3))
    out = np.zeros((len(cols), 128, K // 128, 128), np.float32)
    for c, cl in enumerate(cols):
        cl = np.asarray(cl)
        m = cl >= 0
        blk = np.zeros((K, 128), np.float32)
        blk[:, m] = W[:, cl[m]]
        out[c] = blk.reshape(K // 128, 128, 128).transpose(1, 0, 2)
    return out


def prep_shared(inp):
    f = lambda k: np.asarray(inp[k], np.float32)
    sh = {}
    pars = np.zeros((128, NPAR), np.float32)

    def put(name, a):
        o, w = PLAY[name]
        a = np.asarray(a, np.float32)
        assert a.shape == (128, w), (name, a.shape, w)
        pars[:, o:o + w] = a
    b_ada = f('b_ada')
    put('b_ada0', _fm(b_ada[0]))
    put('b_ada1', _fm(b_ada[1]))
    ng = f('norm_g')
    for l in range(2):
        for wh in range(2):
            put('ng%d%d' % (l, wh), _fm(ng[l, wh]))
    put('fng', _fm(f('final_norm_g')))
    mu = f('shift_mu')[0]
    for i in range(2):
        m = np.zeros((128, 28), np.float32)
        m[:, 0:24] = _fm(mu[i, 0:3072])
        m[:96, 24] = mu[i, 3072:3168]
        m[:96, 25] = mu[i, 3168:3264]
        m[:, 26:28] = _fm(mu[i, 3264:3520])
        put('mu%d' % i, m)
    for d in range(2):
        put('w0_%d' % d, _fm(f('w0')[0, d]))
        put('a0_%d' % d, _fm(f('a0')[0, d]))
    put('k_k', _fm(f('k_k')[0]))
    put('k_a', _fm(f('k_a')[0]))
    put('r_k', _fm(f('r_k')[0].reshape(-1)))
    put('lnx_g', _fm(f('lnx_g')[0]))
    put('lnx_b', _fm(f('lnx_b')[0]))
    put('subln', f('subln_g')[0].reshape(128, 1))
    put('dl', np.broadcast_to(f('diff_lambda')[0].reshape(1, 256), (128, 256)))
    put('b_dw', _fm(f('b_dw')[0]))
    put('cln_g', _fm(f('cln_g')[0]))
    put('cln_b', _fm(f('cln_b')[0]))
    wdw = f('w_dw')[0]
    put('w_dw', np.ascontiguousarray(wdw.reshape(31, 16, 128).transpose(2, 1, 0)).reshape(128, 496))
    fd = f('w_ffn_dw')
    for l in range(2):
        a = fd[l].reshape(3, 2, NFF, 128).transpose(3, 2, 1, 0)
        put('fdw%d' % l, np.ascontiguousarray(a).reshape(128, 258))
    sh['pars'] = pars
    sh['ident'] = np.eye(128, dtype=np.float32)
    bo = np.zeros((128, 128), np.float32)
    bo[:64, :64] = 1
    bo[64:, 64:] = 1
    sh['bones'] = bo
    rt = np.zeros((128, 128), np.float32)
    ang = np.zeros((128, 1024), np.float64)
    tt = np.arange(1024)
    row = (tt // 64).astype(np.float64)
    col = (tt % 64).astype(np.float64)
    inv = (10000.0 ** (-np.arange(16, dtype=np.float32) / 16)).astype(np.float32).astype(np.float64)
    for m in range(2):
        for d in range(64):
            half, r = divmod(d, 32)
            if r < 16:
                partner, sgn, fi = d + 16, -1.0, r
            else:
                partner, sgn, fi = d - 16, 1.0, r - 16
            rt[m * 64 + partner, m * 64 + d] = sgn
            pos = row if half == 0 else col
            ang[m * 64 + d] = (pos.astype(np.float32) * np.float32(inv[fi])).astype(np.float64)
    sh['rt'] = rt
    sh['cos'] = np.cos(ang).astype(np.float32)
    sh['sin'] = np.sin(ang).astype(np.float32)
    wa = f('w_ada')
    sh['wada'] = np.ascontiguousarray(wa.reshape(2, KC, 128, 24, 512).transpose(0, 3, 2, 1, 4))
    sh['win'] = _arr_w(f('w_in')[0], win_cols())
    lora = np.zeros((128, 3, 2, 1024), np.float32)
    lora[:96, 0] = f('w2')[0].transpose(1, 0, 2)
    lora[:96, 1] = f('a2')[0].transpose(1, 0, 2)
    lora[:, 2] = f('g2')[0].reshape(2, 128, 1024).transpose(1, 0, 2)
    sh['lora'] = lora
    sh['wout'] = _arr_w(f('w_out')[0])
    sh['wpw1'] = _arr_w(f('w_pw1')[0])
    sh['wpw2'] = _arr_w(f('w_pw2')[0])
    wu = f('w_up')
    a = wu.reshape(2, KC, 128, 2, NFF, 128).transpose(0, 4, 2, 1, 3, 5)
    sh['wup'] = np.ascontiguousarray(a).reshape(2, NFF, 128, KC, 256)
    sh['wdn'] = np.ascontiguousarray(f('w_down').reshape(2, NFF, 128, D))
    return sh


def prep_core(inp, i):
    f = lambda k: np.asarray(inp[k], np.float32)
    m = {}
    xp = f('x_prompt')[2 * i:2 * i + 2].reshape(512, D)
    xs = f('x_sample')[i]
    m['xin'] = np.ascontiguousarray(np.concatenate([xp, xs], 0).T)
    cf = np.stack([f('c')[i], f('c_ctx')], -1)
    m['cfm'] = np.ascontiguousarray(cf.reshape(KC, 128, 2).transpose(1, 0, 2))
    ck = f('cache_k')[i, 0]
    m['cK'] = np.ascontiguousarray(ck.transpose(0, 1, 3, 2)).reshape(8, 128, 256)
    m['cV'] = np.ascontiguousarray(f('cache_v')[i, 0])
    s0 = np.stack([f('state_wkv_fwd')[i, 0], f('state_wkv_bwd')[i, 0]], 0)
    m['s0'] = np.ascontiguousarray(s0.reshape(2, 2, 8, 64, 64).transpose(0, 1, 3, 2, 4)).reshape(2, 128, 512)
    return m


_NC_CACHE = {}


def kernel(**inputs):
    sh = prep_shared(inputs)
    in_maps = []
    for i in range(8):
        m = dict(sh)
        m.update(prep_core(inputs, i))
        in_maps.append(m)
    if 'nc' not in _NC_CACHE:
        _NC_CACHE['nc'] = build()
    nc = _NC_CACHE['nc']
    res = run_bass_kernel_spmd(nc, in_maps, core_ids=list(range(8)))
    yp = np.zeros((16, 256, D), np.float32)
    ys = np.zeros((8, 1024, D), np.float32)
    nk = np.zeros((16, 1, 8, 2, 256, 64), np.float32)
    nv = np.zeros((16, 1, 8, 256, 128), np.float32)
    sf = np.zeros((16, 1, 16, 64, 64), np.float32)
    sbw = np.zeros((16, 1, 16, 64, 64), np.float32)
    for i in range(8):
        r = res.results[i]
        y = np.asarray(r['y'], np.float32).T
        yp[2 * i:2 * i + 2] = y[:512].reshape(2, 256, D)
        ys[i] = y[512:]
        k = np.asarray(r['nk'], np.float32).reshape(2, 8, 2, 64, 256)
        nk[2 * i:2 * i + 2, 0] = k.transpose(0, 1, 2, 4, 3)
        nv[2 * i:2 * i + 2, 0] = np.asarray(r['nv'], np.float32)
        s = np.asarray(r['nst'], np.float32).reshape(2, 2, 2, 64, 8, 64)
        s = s.transpose(0, 1, 2, 4, 3, 5).reshape(2, 2, 16, 64, 64)
        sf[2 * i:2 * i + 2, 0] = s[:, 0]
        sbw[2 * i:2 * i + 2, 0] = s[:, 1]
    return (yp, ys, nk, nv, sf, sbw)
```

```python
import math
import numpy as np
import concourse.bass as bass
import concourse.mybir as mybir
from concourse.bass_utils import run_bass_kernel_spmd
from contextlib import ExitStack

F32 = mybir.dt.float32
BF16 = mybir.dt.bfloat16
ALU = mybir.AluOpType
AF = mybir.ActivationFunctionType
AX = mybir.AxisListType

D = 2048
KC = 16
NT = 1536
NPAD = NT + 4
SEQS = [(0, 256), (256, 256), (512, 1024)]
POFF = [1, 258, 515]
D_FF = 5504
NFF = 43
LAM_INIT = 0.8 - 0.6 * math.exp(0.0)
RMS_EPS = 1e-6
LN_EPS = 1e-5
GN_EPS = 64e-5
N_WIN = 52


class Prog:
    NS = 8

    def __init__(self, nc, stack):
        self.nc = nc
        self.stack = stack
        self.ce = ['pe', 'dve', 'act', 'pool']
        self.engs = ['pe', 'dve', 'act', 'pool', 'sp']
        self.prog = {e: [] for e in self.engs}
        self.sems = {}
        self.cnt = {}
        self.waited = {e: {} for e in self.engs}
        self.lastw = {}
        self.readers = {}
        self.ndma = {e: 0 for e in self.engs}
        self.epoch = 0
        self.nsem = 0
        self.dead = False
        for e in self.ce:
            self._mk(('c', e, 0))

    def _mk(self, key):
        self.sems[key] = self.stack.enter_context(self.nc.semaphore("s%d" % self.nsem))
        self.nsem += 1
        self.cnt[key] = 0

    def _deps(self, reads, writes):
        deps = {}
        for r in reads:
            lw = self.lastw.get(r)
            if lw is not None and deps.get(lw[0], 0) < lw[1]:
                deps[lw[0]] = lw[1]
            if isinstance(r, str) and r.startswith('ps'):
                for sk, v in self.readers.get(r, {}).items():
                    if deps.get(sk, 0) < v:
                        deps[sk] = v
        for w in writes:
            lw = self.lastw.get(w)
            if lw is not None and deps.get(lw[0], 0) < lw[1]:
                deps[lw[0]] = lw[1]
            for sk, v in self.readers.get(w, {}).items():
                if deps.get(sk, 0) < v:
                    deps[sk] = v
        return deps

    def _waits(self, eng, deps, skip=None):
        waits = []
        wd = self.waited[eng]
        for sk, v in deps.items():
            if sk == skip:
                continue
            if wd.get(sk, 0) < v:
                waits.append((self.sems[sk], v))
                wd[sk] = v
        return waits

    def _record(self, sk, val, reads, writes):
        for r in reads:
            d = self.readers.setdefault(r, {})
            if d.get(sk, 0) < val:
                d[sk] = val
        for w in writes:
            self.lastw[w] = (sk, val)
            self.readers[w] = {}

    def op(self, eng, fn, reads=(), writes=()):
        if self.dead:
            return
        own = ('c', eng, self.epoch)
        deps = self._deps(reads, writes)
        waits = self._waits(eng, deps, skip=own if eng == 'pe' else None)
        self.cnt[own] += 1
        val = self.cnt[own]
        sem = self.sems[own]

        def emit(e):
            for s, v in waits:
                e.wait_ge(s, v)
            fn(e).then_inc(sem, 1)
        self.prog[eng].append(emit)
        self._record(own, val, reads, writes)

    def dma(self, eng, out, in_, reads=(), writes=(), **kw):
        if self.dead:
            return
        i = self.ndma[eng]
        self.ndma[eng] += 1
        sk = ('d', eng, i % self.NS)
        if sk not in self.sems:
            self._mk(sk)
        target = 16 * (i // self.NS + 1)
        deps = self._deps(reads, writes)
        if target > 16 and deps.get(sk, 0) < target - 16:
            deps[sk] = target - 16
        waits = self._waits(eng, deps)
        self.cnt[sk] = target
        sem = self.sems[sk]

        def emit(e):
            for s, v in waits:
                e.wait_ge(s, v)
            e.dma_start(out=out, in_=in_, **kw).then_inc(sem, 16)
        self.prog[eng].append(emit)
        self._record(sk, target, reads, writes)

    def _finals(self):
        return {sk: c for sk, c in self.cnt.items()
                if c > 0 and (sk[0] == 'd' or sk[2] == self.epoch)}

    def barrier(self):
        if self.dead:
            return
        finals = self._finals()
        for eng in self.engs:
            waits = self._waits(eng, finals, skip=('c', eng, self.epoch))
            if waits:
                def emit(e, waits=waits):
                    for s, v in waits:
                        e.wait_ge(s, v)
                self.prog[eng].append(emit)
        self.epoch += 1
        for e in self.ce:
            self._mk(('c', e, self.epoch))
        self.lastw = {}
        self.readers = {}

    def finish(self):
        finals = self._finals()
        for eng in self.engs:
            waits = self._waits(eng, finals, skip=('c', eng, self.epoch))

            def emit(e, waits=waits):
                for s, v in waits:
                    e.wait_ge(s, v)
            self.prog[eng].append(emit)
        prog = self.prog
        with self.nc.Block() as block:
            @block.tensor
            def _(e):
                for f in prog['pe']:
                    f(e)

            @block.vector
            def _(e):
                for f in prog['dve']:
                    f(e)

            @block.scalar
            def _(e):
                for f in prog['act']:
                    f(e)

            @block.gpsimd
            def _(e):
                for f in prog['pool']:
                    f(e)

            @block.sync
            def _(e):
                for f in prog['sp']:
                    f(e)


def par_layout():
    ents = [('b_ada0', 96), ('b_ada1', 96), ('ng00', 16), ('ng01', 16), ('ng10', 16), ('ng11', 16),
            ('fng', 16), ('mu0', 28), ('mu1', 28), ('w0_0', 8), ('w0_1', 8), ('a0_0', 8), ('a0_1', 8),
            ('k_k', 8), ('k_a', 8), ('r_k', 8), ('lnx_g', 8), ('lnx_b', 8), ('subln', 1), ('dl', 256),
            ('b_dw', 16), ('cln_g', 16), ('cln_b', 16), ('w_dw', 496), ('fdw0', 258), ('fdw1', 258)]
    lay = {}
    o = 0
    for n, w in ents:
        lay[n] = (o, w)
        o += w
    return lay, o


PLAY, NPAR = par_layout()


def win_cols():
    ch = []
    for h in range(8):
        ch.append(list(range(h * 128, h * 128 + 128)))
    for h in range(8):
        ch.append(list(range(1024 + h * 128, 1024 + h * 128 + 128)))
    for h in range(8):
        ch.append(list(range(2048 + h * 128, 2048 + h * 128 + 128)))
    ch.append(list(range(6144, 6240)) + [-1] * 32)
    ch.append(list(range(6240, 6336)) + [-1] * 32)
    ch.append(list(range(6336, 6464)))
    ch.append(list(range(6464, 6592)))
    for hp in range(8):
        for base in (3072, 4096, 5120):
            ch.append(list(range(base + hp * 128, base + hp * 128 + 128)))
    return ch


def rest_idx(c):
    if c == 24:
        return 24
    if c == 25:
        return 25
    if c in (26, 27):
        return c
    hp, t = divmod(c - 28, 3)
    return t * 8 + hp


def build(debug=0, stop=99, lite=0, sub=None):
    nc = bass.Bass('TRN2', target_bir_lowering=False)

    def din(name, shape, dt=F32):
        return nc.dram_tensor(name, list(shape), dt, kind="ExternalInput").ap()

    def dout(name, shape, dt=F32):
        return nc.dram_tensor(name, list(shape), dt, kind="ExternalOutput").ap()

    def dscr(name, shape, dt=F32):
        return nc.dram_tensor(name, list(shape), dt, kind="ExternalOutput" if debug else "Internal").ap()

    xin = din("xin", [D, NT])
    cfm_d = din("cfm", [128, KC, 2])
    pars_d = din("pars", [128, NPAR])
    ident_d = din("ident", [128, 128])
    bones_d = din("bones", [128, 128])
    rt_d = din("rt", [128, 128])
    cos_d = din("cos", [128, 1024])
    sin_d = din("sin", [128, 1024])
    cK_d = din("cK", [8, 128, 256])
    cV_d = din("cV", [8, 256, 128])
    s0_d = din("s0", [2, 128, 512])
    wada_d = din("wada", [2, 24, 128, KC, 512] if not lite else [1, 1, 128, 1, 512])
    win_d = din("win", [N_WIN, 128, KC, 128])
    lora_d = din("lora", [128, 3, 2, 1024])
    wout_d = din("wout", [16, 128, KC, 128] if not (lite and stop < 4) else [1, 1, 128, 1, 128])
    wpw1_d = din("wpw1", [32, 128, KC, 128] if not (lite and stop < 6) else [1, 1, 128, 1, 128])
    wpw2_d = din("wpw2", [16, 128, KC, 128] if not (lite and stop < 6) else [1, 1, 128, 1, 128])
    wup_d = din("wup", [2, NFF, 128, KC, 256] if not (lite and stop < 5) else [1, 1, 128, 1, 128])
    wdn_d = din("wdn", [2, NFF, 128, D] if not (lite and stop < 5) else [1, 1, 128, 1, 128])
    if lite:
        class _Dm:
            def __getitem__(self, k):
                return self

            def __getattr__(self, n):
                return lambda *a, **k: self
        if stop < 4:
            wout_d = _Dm()
        if stop < 5:
            wup_d = wdn_d = _Dm()
        if stop < 6:
            wpw1_d = wpw2_d = _Dm()
    y_d = dout("y", [D, NT])
    nk_d = dout("nk", [2, 8, 128, 256])
    nv_d = dout("nv", [2, 8, 256, 128])
    nst_d = dout("nst", [2, 2, 128, 512])
    QS = dscr("QS", [16, 128, NT], BF16)
    VS = dscr("VS", [8, NT, 128], BF16)
    TMS = dscr("TMS", [10, NT, 1024], BF16)
    VBF = dscr("VBF", [1024, NT])
    BON = dscr("BON", [8, 128, NT])
    GATE = dscr("GATE", [8, 128, NT])
    YF = dscr("YF", [2, 1024, NT])
    CONV = dscr("CONV", [16, 128, NT])
    dbg = {}

    with ExitStack() as st:
        P = Prog(nc, st)

        cur = [st]

        def sb(name, shape, dt=F32):
            return cur[-1].enter_context(nc.sbuf_tensor(name, list(shape), dt))

        PS = [st.enter_context(nc.psum_tensor("ps%d" % i, [128, 512], F32)) for i in range(8)]
        psn = [0]

        psh = [0]

        def ps_hi():
            i = 4 + psh[0] % 4
            psh[0] += 1
            return PS[i], 'ps%d' % i

        psmod = [8]

        def ps_next():
            i = psn[0] % psmod[0]
            psn[0] += 1
            return PS[i], 'ps%d' % i

        class Ring:
            def __init__(self, name, n, shape, dt=F32):
                self.t = [sb("%s%d" % (name, i), shape, dt) for i in range(n)]
                self.k = ["%s%d" % (name, i) for i in range(n)]
                self.i = 0

            def next(self):
                j = self.i % len(self.t)
                self.i += 1
                return self.t[j], self.k[j]

        def TT(eng, out, in0, in1, op, r, w):
            P.op(eng, lambda e: e.tensor_tensor(out=out, in0=in0, in1=in1, op=op), r, w)

        def TS(eng, out, in0, s1, s2, op0, op1, r, w):
            if s2 is None:
                P.op(eng, lambda e: e.tensor_single_scalar(out=out, in_=in0, scalar=s1, op=op0), r, w)
            else:
                P.op(eng, lambda e: e.tensor_scalar(out=out, in0=in0, scalar1=s1, scalar2=s2, op0=op0, op1=op1), r, w)

        def STT(eng, out, in0, scalar, in1, op0, op1, r, w):
            P.op(eng, lambda e: e.scalar_tensor_tensor(out=out, in0=in0, scalar=scalar, in1=in1, op0=op0, op1=op1), r, w)

        def ACTV(out, in_, func, r, w, bias=None, scale=None):
            kw = {}
            if bias is not None:
                kw['bias'] = bias
            if scale is not None:
                kw['scale'] = scale
            P.op('act', lambda e: e.activation(out=out, in_=in_, func=func, **kw), r, w)

        EPSI = {RMS_EPS: 0, LN_EPS: 1, GN_EPS: 2}

        def RSQ(out, in_, scale, eps, r, w):
            j = EPSI[eps]
            ACTV(out, in_, AF.Ln, list(r) + ['EPS'], w, bias=EPS[:, j:j + 1], scale=scale)
            ACTV(out, out, AF.Exp, w, w, scale=-0.5)

        def CP(eng, out, in_, r, w):
            if eng == 'act':
                P.op('act', lambda e: e.copy(out=out, in_=in_), r, w)
            elif 'PSum' in type(in_.tensor).__name__:
                P.op(eng, lambda e: e.tensor_single_scalar(out=out, in_=in_, scalar=1.0, op=ALU.mult), r, w)
            else:
                P.op(eng, lambda e: e.tensor_copy(out=out, in_=in_), r, w)

        def MM(out, lhsT, rhs, start, stop, r, w):
            P.op('pe', lambda e: e.matmul(out, lhsT=lhsT, rhs=rhs, start=start, stop=stop), r, w)

        def RED(eng, out, in_, r, w):
            P.op(eng, lambda e: e.tensor_reduce(out=out, in_=in_, axis=AX.X, op=ALU.add), r, w)

        def MEMSET(eng, ap, val, w):
            P.op(eng, lambda e: e.memset(ap, val), (), w)

        def chk(k):
            if stop <= k and not P.dead:
                if debug:
                    dx = dout("dbgX", [128, KC, NT])
                    P.dma('sp', dx, X[:], reads=[('X', 0), ('X', 1), ('X', 2)])
                P.dead = True

        X = sb("X", [128, KC, NT])
        PAR = sb("PAR", [128, NPAR])
        IDF = sb("IDF", [128, 128])
        IDB = sb("IDB", [128, 128], BF16)
        ONF = sb("ONF", [128, 128])
        ONB = sb("ONB", [128, 128], BF16)
        BOF = sb("BOF", [128, 128])
        MOD = sb("MOD", [128, 2, 96, 2])
        GG = sb("GG", [128, 2, 2, KC, 2])
        NEGLAM = sb("NEGLAM", [128, 1])
        SUBG = sb("SUBG", [128, 1])
        CM = sb("CM", [128, 28])
        EPS = sb("EPS", [128, 4])
        SC = sb("SC", [128, KC, 2])

        def par(name, j0=0, n=None):
            o, w = PLAY[name]
            if n is None:
                n = w - j0
            return PAR[:, o + j0:o + j0 + n]

        P.dma('sp', PAR[:], pars_d, writes=['PAR'])
        P.dma('sp', IDF[:], ident_d, writes=['IDF'])
        P.dma('pool', IDB[:], ident_d, writes=['IDB'])
        P.dma('sp', BOF[:], bones_d, writes=['BOF'])
        MEMSET('dve', ONF[:], 1.0, ['ONF'])
        for eps_, j_ in EPSI.items():
            MEMSET('dve', EPS[:, j_:j_ + 1], eps_, ['EPS'])
        MEMSET('dve', ONB[:], 1.0, ['ONB'])
        for kc in range(KC):
            P.dma('sp', X[:, kc, :], xin[kc * 128:(kc + 1) * 128, :], writes=[('X', 0), ('X', 1), ('X', 2)])

        with ExitStack() as ph:
            cur.append(ph)
            sbp = sb
            WA = [sbp("WA%d" % i, [128, KC, 512]) for i in range(2)]
            P.dma('sp', SC[:], cfm_d, writes=['SC'])
            ACTV(SC[:], SC[:], AF.Silu, ['SC'], ['SC'])
            psM, kM = PS[0], 'ps0'
            for l in ([0, 1] if lite else [0]):
                for jb in range(24 if not lite else 0):
                    wa = WA[jb % 2]
                    wk = 'WA%d' % (jb % 2)
                    P.dma('sp' if jb % 2 == 0 else 'act', wa[:], wada_d[l, jb], writes=[wk])
                    for jj in range(4):
                        j = jb * 4 + jj
                        for kc in range(KC):
                            MM(psM[:, 2 * j:2 * j + 2], wa[:, kc, jj * 128:(jj + 1) * 128], SC[:, kc, :],
                               kc == 0, kc == KC - 1, [wk, 'SC'], [kM])
                bo = PLAY['b_ada%d' % l][0]
                if lite:
                    MEMSET('dve', MOD[:, l], 0.05, ['MOD'])
                else:
                    TT('dve', MOD[:, l], psM[:, 0:192].rearrange("p (j s) -> p j s", s=2),
                       PAR[:, bo:bo + 96].unsqueeze(2).to_broadcast([128, 96, 2]), ALU.add, [kM, 'PAR'], ['MOD'])
                for wh in range(2):
                    go = PLAY['ng%d%d' % (l, wh)][0]
                    STT('dve', GG[:, l, wh], MOD[:, l, (1 + 3 * wh) * 16:(2 + 3 * wh) * 16, :], 1.0,
                        PAR[:, go:go + 16].unsqueeze(2).to_broadcast([128, 16, 2]), ALU.add, ALU.mult,
                        ['MOD', 'PAR'], ['GG'])
            DT = sbp("DT", [128, 2, 64])
            DS = sbp("DS", [128, 2])
            dlo = PLAY['dl'][0]
            dlv = PAR[:, dlo:dlo + 256].rearrange("p (a b k) -> p a b k", a=2, b=2)
            TT('dve', DT[:], dlv[:, :, 0, :], dlv[:, :, 1, :], ALU.mult, ['PAR'], ['DT'])
            RED('dve', DS[:], DT[:], ['DT'], ['DS'])
            ACTV(DS[:], DS[:], AF.Exp, ['DS'], ['DS'])
            TT('dve', NEGLAM[:], DS[:, 1:2], DS[:, 0:1], ALU.subtract, ['DS'], ['NEGLAM'])
            TS('dve', NEGLAM[:], NEGLAM[:], -LAM_INIT, None, ALU.add, None, ['NEGLAM'], ['NEGLAM'])
            TS('dve', SUBG[:], par('subln'), 1.0 - LAM_INIT, None, ALU.mult, None, ['PAR'], ['SUBG'])
            TT('dve', CM[:], par('mu0'), par('mu1'), ALU.add, ['PAR'], ['CM'])
            TS('dve', CM[:], CM[:], -1.0, 1.0, ALU.mult, ALU.add, ['CM'], ['CM'])
            P.barrier()
            cur.pop()

        def norm_mod(H, l, wh, SQ, RSTD, TMPN):
            for tb in range(3):
                s = 1 if tb == 0 else 0
                psN, kN = ps_next()
                for kc in range(KC):
                    sq, sk = SQ.next()
                    ACTV(sq[:], X[:, kc, tb * 512:(tb + 1) * 512], AF.Square, [('X', tb)], [sk])
                    MM(psN[:], ONF[:], sq[:], kc == 0, kc == KC - 1, [sk, 'ONF'], [kN])
                RSQ(RSTD[:], psN[:], 1.0 / D, RMS_EPS, [kN], ['RSTD'])
                for kc in range(KC):
                    tm, tk = TMPN.next()
                    TT('dve', tm[:], X[:, kc, tb * 512:(tb + 1) * 512], RSTD[:], ALU.mult, [('X', tb), 'RSTD'], [tk])
                    ACTV(H[:, kc, tb * 512:(tb + 1) * 512], tm[:], AF.Identity, [tk, 'GG', 'MOD'], [('H', tb)],
                         bias=MOD[:, l, (3 * wh) * 16 + kc, s:s + 1], scale=GG[:, l, wh, kc, s:s + 1])

        def resid_add(ps, kps, l, wh, kc, tb):
            s = 1 if tb == 0 else 0
            xs = X[:, kc, tb * 512:(tb + 1) * 512]
            STT('dve', xs, ps[:], MOD[:, l, (2 + 3 * wh) * 16 + kc, s:s + 1], xs, ALU.mult, ALU.add,
                [kps, 'MOD', ('X', tb)], [('X', tb)])


        def pad_off(t):
            return t + 1 if t < 256 else (t + 2 if t < 512 else t + 3)
        PCS = [(1, 385), (386, 385), (771, 384), (1155, 384)]
        SEQP = [(0, 256, 1), (256, 256, 258), (512, 1024, 515)]
        RS = dscr("RS", [28, 128, NPAD])

        def load_w(WBr, src, eng='pool'):
            wb, wk = WBr.next()
            P.dma(eng, wb[:], src, writes=[wk])
            return wb, wk

        with ExitStack() as ph:
            cur.append(ph)
            H = sb("H", [128, KC, NT], BF16)
            SQ = Ring("SQ", 2, [128, 512])
            TMPN = Ring("TMPN", 2, [128, 512])
            RSTD = sb("RSTD", [128, 512])
            norm_mod(H, 0, 0, SQ, RSTD, TMPN)
            WB = Ring("WB", 2, [128, KC, 128], BF16)
            COS = sb("COS", [128, 1024])
            SIN = sb("SIN", [128, 1024])
            RTF = sb("RTF", [128, 128])
            P.dma('sp', COS[:], cos_d, writes=['COS'])
            P.dma('sp', SIN[:], sin_d, writes=['SIN'])
            P.dma('sp', RTF[:], rt_d, writes=['RTF'])
            RAWP = sb("RAWP", [128, NPAD])
            OUTP = sb("OUTP", [128, NPAD])
            MEMSET('dve', RAWP[:], 0.0, ['RAWP'])
            QB = Ring("QB", 2, [128, 512], BF16)
            XS = Ring("XS", 2, [128, 512])
            T1 = Ring("T1", 2, [128, 512])
            T2 = Ring("T2", 2, [128, 512])
            HK = [('H', 0), ('H', 1), ('H', 2)]

            def proj_tb(wb, wk, tb):
                ps, kp = ps_next()
                for kc in range(KC):
                    MM(ps[:], wb[:, kc, :], H[:, kc, tb * 512:(tb + 1) * 512], kc == 0, kc == KC - 1,
                       [wk, ('H', tb)], [kp])
                return ps, kp

            SUB = sub if sub is not None else 'qvr'
            for c in range(16 if 'q' in SUB else 0):
                wb, wk = load_w(WB, win_d[c])
                if 'L' in SUB:
                    continue
                for tb in range(3):
                    ps, kp = proj_tb(wb, wk, tb)
                    if 'M' in SUB:
                        continue
                    if 'E' in SUB and tb > 0:
                        continue
                    qb, qk = QB.next()
                    if tb == 0:
                        CP('act', qb[:], ps[:], [kp], [qk])
                        if c >= 8 and 'A' not in SUB:
                            xs, xk = XS.next()
                            CP('dve', xs[:], ps[:], [kp, qk] if 'S' in SUB else [kp], [xk])
                            for pi in range(2 if 'C' not in SUB else 0):
                                P.dma('sp', nk_d[pi, c - 8], xs[:, pi * 256:(pi + 1) * 256], reads=[xk])
                    else:
                        xs, xk = XS.next()
                        CP('act', xs[:], ps[:], [kp], [xk])
                        ps2, kp2 = ps_next()
                        MM(ps2[:], RTF[:], xs[:], True, True, ['RTF', xk], [kp2])
                        t1, k1 = T1.next()
                        t2, k2 = T2.next()
                        TT('dve', t1[:], xs[:], COS[:, (tb - 1) * 512:tb * 512], ALU.mult, [xk, 'COS'], [k1])
                        TT('dve', t2[:], ps2[:], SIN[:, (tb - 1) * 512:tb * 512], ALU.mult, [kp2, 'SIN'], [k2])
                        TT('dve', qb[:], t1[:], t2[:], ALU.add, [k1, k2], [qk])
                    if 'B' not in SUB:
                        P.dma('sp', QS[c, :, tb * 512:(tb + 1) * 512], qb[:], reads=[qk], writes=[('QS', c)])
            for h in range(8 if 'v' in SUB else 0):
                wb, wk = load_w(WB, win_d[16 + h])
                for g in range(3):
                    ps, kp = ps_next()
                    for j in range(4):
                        tt = g * 4 + j
                        for kc in range(KC):
                            MM(ps[:, j * 128:(j + 1) * 128], H[:, kc, tt * 128:(tt + 1) * 128], wb[:, kc, :],
                               kc == 0, kc == KC - 1, [wk, ('H', g)], [kp])
                    qb, qk = QB.next()
                    CP('act', qb[:], ps[:], [kp], [qk])
                    P.dma('sp', VS[h, g * 512:(g + 1) * 512, :].rearrange("(j p) d -> p j d", p=128),
                          qb[:].rearrange("p (j d) -> p j d", d=128), reads=[qk], writes=[('VS', h)])
                    if g == 0:
                        xs, xk = XS.next()
                        CP('dve', xs[:], ps[:], [kp], [xk])
                        for pi in range(2):
                            P.dma('sp', nv_d[pi, h].rearrange("(j p) d -> p j d", p=128),
                                  xs[:, pi * 256:(pi + 1) * 256].rearrange("p (j d) -> p j d", d=128), reads=[xk])
            for c in range(24, N_WIN if 'r' in SUB else 24):
                ri = rest_idx(c)
                wb, wk = load_w(WB, win_d[c])
                for tb in range(3):
                    ps, kp = proj_tb(wb, wk, tb)
                    if tb == 0:
                        CP('act', RAWP[:, 1:257], ps[:, 0:256], [kp], ['RAWP'])
                        CP('act', RAWP[:, 258:514], ps[:, 256:512], [kp], ['RAWP'])
                    else:
                        o = 515 + (tb - 1) * 512
                        CP('act', RAWP[:, o:o + 512], ps[:], [kp], ['RAWP'])
                m0o = PLAY['mu0'][0] + ri
                m1o = PLAY['mu1'][0] + ri
                TS('dve', OUTP[:, 1:1539], RAWP[:, 1:1539], CM[:, ri:ri + 1], None, ALU.mult, None, ['RAWP', 'CM'], ['OUTP'])
                STT('dve', OUTP[:, 1:1539], RAWP[:, 0:1538], PAR[:, m0o:m0o + 1], OUTP[:, 1:1539], ALU.mult, ALU.add,
                    ['RAWP', 'PAR', 'OUTP'], ['OUTP'])
                STT('dve', OUTP[:, 1:1539], RAWP[:, 2:1540], PAR[:, m1o:m1o + 1], OUTP[:, 1:1539], ALU.mult, ALU.add,
                    ['RAWP', 'PAR', 'OUTP'], ['OUTP'])
                P.dma('sp', RS[ri], OUTP[:], reads=['OUTP'], writes=[('RS', ri)])
            P.barrier()
            cur.pop()

        chk(1)
        with ExitStack() as ph:
            cur.append(ph)
            LORA = sb("LORA", [128, 3, 2, 1024], BF16)
            P.dma('pool', LORA[:], lora_d, writes=['LORA'])
            TWr = sb("TWr", [128, NPAD])
            TW = sb("TW", [128, NPAD], BF16)
            XA = sb("XA", [128, NPAD], BF16)
            SG = sb("SG", [128, 2, NPAD], BF16)
            OMK = sb("OMK", [128, 8])
            TS('dve', OMK[:], par('k_a'), -1.0, 1.0, ALU.mult, ALU.add, ['PAR'], ['OMK'])
            P.dma('sp', TWr[:], RS[24], writes=['TWr'])
            ACTV(TW[:], TWr[:], AF.Tanh, ['TWr'], ['TW'])
            P.dma('sp', TWr[:], RS[25], reads=[], writes=['TWr'])
            CP('act', XA[:], TWr[:], ['TWr'], ['XA'])
            for i in range(2):
                P.dma('sp', TWr[:], RS[26 + i], writes=['TWr'])
                ACTV(SG[:, i, :], TWr[:], AF.Sigmoid, ['TWr'], ['SG'])
            Rb = sb("Rb", [128, NPAD])
            KBb = sb("KBb", [128, NPAD])
            VBb = sb("VBb", [128, NPAD])
            KKb = sb("KKb", [128, NPAD])
            Ab = sb("Ab", [128, NPAD])
            TA = sb("TA", [128, NPAD])
            TBb = sb("TBb", [128, NPAD])
            WDb = sb("WDb", [128, NPAD])
            ALRb = sb("ALRb", [128, NPAD])
            KDb = sb("KDb", [128, NPAD])
            BDb = sb("BDb", [128, NPAD])
            TMB = Ring("TMB", 4, [128, 512], BF16)
            SL = slice(1, 1539)

            def to_tm(arr, akey, idx, hp, hilo=False):
                for g in range(3):
                    ps, kp = ps_next()
                    for j in range(4):
                        o = pad_off((g * 4 + j) * 128)
                        MM(ps[:, j * 128:(j + 1) * 128], arr[:, o:o + 128], IDF[:], True, True, [akey, 'IDF'], [kp])
                    tmb, tk = TMB.next()
                    CP('act', tmb[:], ps[:], [kp], [tk])
                    dst = TMS[idx, g * 512:(g + 1) * 512, hp * 128:(hp + 1) * 128].rearrange("(j p) c -> p j c", p=128)
                    P.dma('sp', dst, tmb[:].rearrange("p (j c) -> p j c", c=128), reads=[tk], writes=[('TMS', idx, hp)])
                    if hilo:
                        tml, tlk = TMB.next()
                        TT('dve', tml[:], ps[:], tmb[:], ALU.subtract, [kp, tk], [tlk])
                        dst2 = TMS[idx + 1, g * 512:(g + 1) * 512, hp * 128:(hp + 1) * 128].rearrange("(j p) c -> p j c", p=128)
                        P.dma('sp', dst2, tml[:].rearrange("p (j c) -> p j c", c=128), reads=[tlk], writes=[('TMS', idx + 1, hp)])

            def seq_store(dst_fn, arr, akey, wkey):
                for (t0, ln, po) in SEQP:
                    P.dma('sp', dst_fn(t0, ln), arr[:, po:po + ln], reads=[akey], writes=[wkey])

            for hp in range(8):
                P.dma('sp', Rb[:], RS[hp], writes=['Rb'])
                P.dma('sp', KBb[:], RS[8 + hp], writes=['KBb'])
                P.dma('sp', VBb[:], RS[16 + hp], writes=['VBb'])
                seq_store(lambda t0, ln: VBF[hp * 128:(hp + 1) * 128, t0:t0 + ln], VBb, 'VBb', ('VBF', hp))
                kko = PLAY['k_k'][0] + hp
                TS('dve', KKb[:, SL], KBb[:, SL], PAR[:, kko:kko + 1], None, ALU.mult, None, ['KBb', 'PAR'], ['KKb'])
                TT('dve', TA[:, SL], KKb[:, SL], KKb[:, SL], ALU.mult, ['KKb'], ['TA'])
                for (o, n) in PCS:
                    ps, kp = ps_next()
                    MM(ps[:, 0:n], BOF[:], TA[:, o:o + n], True, True, ['BOF', 'TA'], [kp])
                    ACTV(TBb[:, o:o + n], ps[:, 0:n], AF.Sqrt, [kp], ['TBb'])
                TS('dve', TBb[:, SL], TBb[:, SL], 1e-12, None, ALU.max, None, ['TBb'], ['TBb'])
                P.op('dve', lambda e: e.reciprocal(out=TBb[:, SL], in_=TBb[:, SL]), ['TBb'], ['TBb'])
                TT('dve', KKb[:, SL], KKb[:, SL], TBb[:, SL], ALU.mult, ['KKb', 'TBb'], ['KKb'])
                TS('dve', Ab[:, SL], KKb[:, SL], -1.0, None, ALU.mult, None, ['KKb'], ['Ab'])
                to_tm(Ab, 'Ab', 0, hp)
                to_tm(Rb, 'Rb', 1, hp)
                rko = PLAY['r_k'][0] + hp
                TT('dve', TA[:, SL], Rb[:, SL], KBb[:, SL], ALU.mult, ['Rb', 'KBb'], ['TA'])
                TS('dve', TA[:, SL], TA[:, SL], PAR[:, rko:rko + 1], None, ALU.mult, None, ['TA', 'PAR'], ['TA'])
                for (o, n) in PCS:
                    ps, kp = ps_next()
                    MM(ps[:, 0:n], BOF[:], TA[:, o:o + n], True, True, ['BOF', 'TA'], [kp])
                    TT('dve', TBb[:, o:o + n], ps[:, 0:n], VBb[:, o:o + n], ALU.mult, [kp, 'VBb'], ['TBb'])
                seq_store(lambda t0, ln: BON[hp, :, t0:t0 + ln], TBb, 'TBb', ('BON', hp))
                for (o, n) in PCS:
                    ps, kp = ps_next()
                    for kc in range(2):
                        MM(ps[:, 0:n], LORA[:, 2, kc, hp * 128:(hp + 1) * 128], SG[:, kc, o:o + n], kc == 0, kc == 1,
                           ['LORA', 'SG'], [kp])
                    CP('act', TA[:, o:o + n], ps[:, 0:n], [kp], ['TA'])
                seq_store(lambda t0, ln: GATE[hp, :, t0:t0 + ln], TA, 'TA', ('GATE', hp))
                for d in range(2):
                    w0o = PLAY['w0_%d' % d][0] + hp
                    a0o = PLAY['a0_%d' % d][0] + hp
                    kao = PLAY['k_a'][0] + hp
                    for (o, n) in PCS:
                        ps, kp = ps_next()
                        MM(ps[:, 0:n], LORA[0:96, 0, d, hp * 128:(hp + 1) * 128], TW[0:96, o:o + n], True, True,
                           ['LORA', 'TW'], [kp])
                        ACTV(WDb[:, o:o + n], ps[:, 0:n], AF.Sigmoid, [kp, 'PAR'], ['WDb'], bias=PAR[:, w0o:w0o + 1])
                        ps2, kp2 = ps_next()
                        MM(ps2[:, 0:n], LORA[0:96, 1, d, hp * 128:(hp + 1) * 128], XA[0:96, o:o + n], True, True,
                           ['LORA', 'XA'], [kp2])
                        ACTV(ALRb[:, o:o + n], ps2[:, 0:n], AF.Sigmoid, [kp2, 'PAR'], ['ALRb'], bias=PAR[:, a0o:a0o + 1])
                    ACTV(WDb[:, SL], WDb[:, SL], AF.Exp, ['WDb'], ['WDb'], scale=-math.exp(-0.5))
                    TS('dve', KDb[:, SL], ALRb[:, SL], PAR[:, kao:kao + 1], OMK[:, hp:hp + 1], ALU.mult, ALU.add,
                       ['ALRb', 'PAR', 'OMK'], ['KDb'])
                    TT('dve', KDb[:, SL], KDb[:, SL], KBb[:, SL], ALU.mult, ['KDb', 'KBb'], ['KDb'])
                    TT('dve', BDb[:, SL], KKb[:, SL], ALRb[:, SL], ALU.mult, ['KKb', 'ALRb'], ['BDb'])
                    to_tm(WDb, 'WDb', 2 + 4 * d, hp, hilo=True)
                    to_tm(BDb, 'BDb', 4 + 4 * d, hp)
                    to_tm(KDb, 'KDb', 5 + 4 * d, hp)
            P.barrier()
            cur.pop()

        chk(2)
        with ExitStack() as ph:
            cur.append(ph)
            ia = IDF[:]
            pstep = ia.ap[0][0]
            ia = IDB[:]
            pstep = ia.ap[0][0]
            SELM = sb("SELM", [128, 64, 128], BF16)
            for u_ in range(2):
                src_ = bass.AP(ia.tensor, ia.offset + 64 * u_, [[pstep, 128], [1, 64], [0, 64]])
                CP('dve', SELM[:, :, u_ * 64:(u_ + 1) * 64], src_, ['IDB'], ['SELM'])
            P.barrier()
            psmod[0] = 7
            WAs = Ring("WAs", 2, [128, KC, 128])

            def adaln1_gen():
                for j in range(96):
                    jb, jj = divmod(j, 4)
                    wa, wk = WAs.next()
                    P.dma('sp', wa[:], wada_d[1, jb][:, :, jj * 128:(jj + 1) * 128], writes=[wk])
                    for kc in range(KC):
                        MM(PS[7][:, 2 * j:2 * j + 2], wa[:, kc, :], SC[:, kc, :], kc == 0, kc == KC - 1,
                           [wk, 'SC'], ['ps7'])
                        yield
            agen = iter(()) if lite else adaln1_gen()
            groups = [[('Sf', 2, 0, 0, None, 'dve'), ('Sb', 2, 1, 1, None, 'pool')],
                      [('P0f', 0, 0, None, (0, 0), 'dve'), ('P0b', 0, 1, None, (0, 1), 'pool')],
                      [('P1f', 1, 0, None, (1, 0), 'dve'), ('P1b', 1, 1, None, (1, 1), 'pool')]]
            def tms_idx(slot, dr):
                return [0, 2 + 4 * dr, 3 + 4 * dr, 4 + 4 * dr, 5 + 4 * dr, 1][slot]
            for grp in groups:
                with ExitStack() as gs:
                    cur.append(gs)
                    tl = []
                    for (nm, si, dr, s0i, outi, teng) in grp:
                        t = dict(nm=nm, dr=dr, outi=outi, eng=teng)
                        t['WS'] = [sb("WS%d_%s" % (i, nm), [128, 8, 64]) for i in range(2)]
                        t['T3'] = sb("T3_" + nm, [128, 8, 64])
                        t['off'], t['T'] = SEQS[si]
                        t['S'] = sb("S_" + nm, [128, 8, 64])
                        t['TMP'] = sb("TMP_" + nm, [128, 8, 64])
                        t['T2'] = sb("T2_" + nm, [128, 8, 64])
                        t['SA'] = sb("SA_" + nm, [128, 8])
                        t['VV'] = [sb("VV%d_%s" % (i, nm), [128, 8, 64]) for i in range(2)]
                        t['YT'] = [sb("YT%d_%s" % (i, nm), [128, 8, 64]) for i in range(2)]
                        t['TMX'] = [sb("TMX%d_%s" % (i, nm), [128, 6, 512], BF16) for i in range(2)]
                        if s0i is not None:
                            P.dma('sp', t['S'][:].rearrange("p h k -> p (h k)"), s0_d[s0i], writes=['S_' + nm])
                        else:
                            MEMSET('dve', t['S'][:], 0.0, ['S_' + nm])
                        tl.append(t)
                    nch = tl[0]['T'] // 64
                    def load_chunk(cq):
                        for t in tl:
                            nm = t['nm']
                            b = cq % 2
                            tok0 = t['off'] + (cq * 64 if t['dr'] == 0 else t['T'] - 64 * (cq + 1))
                            for ti in range(6):
                                idx = tms_idx(ti, t['dr'])
                                for u in range(2):
                                    P.dma('sp' if u == 0 else 'act', t['TMX'][b][u * 64:(u + 1) * 64, ti, :],
                                          TMS[idx, tok0:tok0 + 64, u * 512:(u + 1) * 512],
                                          writes=[('TMX', nm, b)])
                            for u in range(2):
                                src = bass.AP(VBF.tensor, VBF.offset + (u * 512) * NT + tok0,
                                              [[NT, 64], [64 * NT, 8], [1, 64]])
                                P.dma('sp', t['VV'][b][u * 64:(u + 1) * 64, :, :], src, writes=[('VV', nm, b)])

                    for c in range(nch):
                        if c == 0:
                            load_chunk(0)
                        if c + 1 < nch:
                            load_chunk(c + 1)
                        def emit_back(bk):
                            (t_, S_, sk_, TMP_, tk_, pR_, kR_, b_, sel_) = bk
                            TT('dve', TMP_[:], S_[:], pR_, ALU.mult, [sk_, kR_], [tk_])
                            RED('dve', t_['YT'][b_][:, :, sel_], TMP_[:], [tk_], [('YT', t_['nm'], b_)])

                        pend = None
                        usc = [0]
                        for i in range(64):
                            for tix, t in enumerate(tl):
                                nm = t['nm']
                                b = c % 2
                                sel = i if t['dr'] == 0 else 63 - i
                                lhsT = SELM[:, sel, :]
                                S = t['S']
                                sk = 'S_' + nm
                                pss = []
                                for slots in ((0,), (1, 2), (3,), (4,), (5,)):
                                    ps, kp = ps_next()
                                    for si_, sl_ in enumerate(slots):
                                        MM(ps[:], lhsT, t['TMX'][b][:, sl_, :], si_ == 0, si_ == len(slots) - 1,
                                           [('TMX', nm, b)], [kp])
                                    pss.append((ps[:].rearrange("p (h k) -> p h k", k=64), kp))
                                (pA, kA), (pW, kW), (pB, kB), (pK, kK), (pR, kR) = pss
                                next(agen, None)
                                TMP = t['TMP'][tix % 2] if isinstance(t['TMP'], list) else t['TMP']
                                T2 = t['T2']
                                T3 = t['T3']
                                SA = t['SA']
                                WS = t['WS'][i % 2]
                                wsk = ('WS', nm, i % 2)
                                tk, t2k, t3k, sak = 'TMP_' + nm, 'T2_' + nm, 'T3_' + nm, 'SA_' + nm
                                CP('act', WS[:], pW, [kW], [wsk])
                                TT('dve', TMP[:], S[:], pA, ALU.mult, [sk, kA], [tk])
                                RED('dve', SA[:], TMP[:], [tk], [sak])
                                TT('dve', T2[:], pB, SA[:].unsqueeze(2).to_broadcast([128, 8, 64]), ALU.mult, [kB, sak], [t2k])
                                for hh in range(8):
                                    ACTV(T3[:, hh, :], pK[:, hh, :], AF.Identity, [kK, ('VV', nm, b)], [(t3k, hh)],
                                         scale=t['VV'][b][:, hh, sel:sel + 1])
                                TT('pool', S[:], S[:], WS[:], ALU.mult, [sk, wsk], [sk])
                                TT('pool', S[:], S[:], T2[:], ALU.add, [sk, t2k], [sk])
                                TT('pool', S[:], S[:], T3[:], ALU.add, [sk] + [(t3k, hh) for hh in range(8)], [sk])
                                bk = (t, S, sk, TMP, tk, pR, kR, b, sel)
                                if tix == 0:
                                    if pend is not None:
                                        emit_back(pend)
                                        pend = None
                                    backA = bk
                                else:
                                    emit_back(backA)
                                    pend = bk
                        if pend is not None:
                            emit_back(pend)
                            pend = None
                        for t in tl:
                            nm = t['nm']
                            b = c % 2
                            tok0 = t['off'] + (c * 64 if t['dr'] == 0 else t['T'] - 64 * (c + 1))
                            for u in range(2):
                                dst = bass.AP(YF.tensor, YF.offset + t['dr'] * 1024 * NT + (u * 512) * NT + tok0,
                                              [[NT, 64], [64 * NT, 8], [1, 64]])
                                P.dma('sp', dst, t['YT'][b][u * 64:(u + 1) * 64, :, :], reads=[('YT', nm, b)],
                                      writes=[('YF', t['dr'])])
                    for t in tl:
                        if t['outi'] is not None:
                            pi, di = t['outi']
                            P.dma('sp', nst_d[pi, di], t['S'][:].rearrange("p h k -> p (h k)"), reads=['S_' + t['nm']])
                    P.barrier()
                    cur.pop()
            for _ in agen:
                pass
            if not lite:
                bo1 = PLAY['b_ada1'][0]
                TT('dve', MOD[:, 1], PS[7][:, 0:192].rearrange("p (j s) -> p j s", s=2),
                   PAR[:, bo1:bo1 + 96].unsqueeze(2).to_broadcast([128, 96, 2]), ALU.add, ['ps7', 'PAR'], ['MOD'])
                for wh in range(2):
                    go = PLAY['ng1%d' % wh][0]
                    STT('dve', GG[:, 1, wh], MOD[:, 1, (1 + 3 * wh) * 16:(2 + 3 * wh) * 16, :], 1.0,
                        PAR[:, go:go + 16].unsqueeze(2).to_broadcast([128, 16, 2]), ALU.add, ALU.mult,
                        ['MOD', 'PAR'], ['GG'])
                P.barrier()
            psmod[0] = 8
            cur.pop()

        chk(3)
        with ExitStack() as ph:
            cur.append(ph)
            OA = sb("OA", [128, 8, NT], BF16)
            OB = sb("OB", [128, 8, NT], BF16)
            with ExitStack() as p2:
                cur.append(p2)
                Y0 = sb("Y0", [128, NT])
                Y1 = sb("Y1", [128, NT])
                BNb = sb("BNb", [128, NT])
                GTb = sb("GTb", [128, NT])
                DDb = sb("DDb", [128, 512])
                SQb = sb("SQb", [128, 512])
                RSb = sb("RSb", [128, 512])
                for hp in range(8):
                    P.dma('sp', Y0[:], YF[0, hp * 128:(hp + 1) * 128, :], writes=['Y0'])
                    P.dma('act', Y1[:], YF[1, hp * 128:(hp + 1) * 128, :], writes=['Y1'])
                    P.dma('sp', BNb[:], BON[hp], writes=['BNb'])
                    P.dma('act', GTb[:], GATE[hp], writes=['GTb'])
                    TT('dve', Y0[:], Y0[:], Y1[:], ALU.add, ['Y0', 'Y1'], ['Y0'])
                    lgo = PLAY['lnx_g'][0] + hp
                    lbo = PLAY['lnx_b'][0] + hp
                    for tb in range(3):
                        ts_ = slice(tb * 512, (tb + 1) * 512)
                        ps, kp = ps_next()
                        MM(ps[:], BOF[:], Y0[:, ts_], True, True, ['BOF', 'Y0'], [kp])
                        STT('dve', DDb[:], ps[:], -1.0 / 64, Y0[:, ts_], ALU.mult, ALU.add, [kp, 'Y0'], ['DDb'])
                        TT('dve', SQb[:], DDb[:], DDb[:], ALU.mult, ['DDb'], ['SQb'])
                        ps2, kp2 = ps_next()
                        MM(ps2[:], BOF[:], SQb[:], True, True, ['BOF', 'SQb'], [kp2])
                        RSQ(RSb[:], ps2[:], 1.0 / 64, GN_EPS, [kp2], ['RSb'])
                        TT('dve', DDb[:], DDb[:], RSb[:], ALU.mult, ['DDb', 'RSb'], ['DDb'])
                        TS('dve', DDb[:], DDb[:], PAR[:, lgo:lgo + 1], PAR[:, lbo:lbo + 1], ALU.mult, ALU.add,
                           ['DDb', 'PAR'], ['DDb'])
                        TT('dve', DDb[:], DDb[:], BNb[:, ts_], ALU.add, ['DDb', 'BNb'], ['DDb'])
                        TT('dve', OB[:, hp, ts_], DDb[:], GTb[:, ts_], ALU.mult, ['DDb', 'GTb'], ['OB'])
                P.barrier()
                cur.pop()
            with ExitStack() as p2:
                cur.append(p2)
                QhR = [sb("Qh%d" % i, [128, NT], BF16) for i in range(2)]
                KhR = [sb("Kh%d" % i, [128, NT], BF16) for i in range(2)]
                VTR = [sb("VT%d" % i, [128, 12, 128], BF16) for i in range(2)]
                CKR = [sb("CK%d" % i, [128, 256], BF16) for i in range(2)]
                CVR = [sb("CVt%d" % i, [128, 2, 128], BF16) for i in range(2)]
                EX = Ring("EX", 3, [128, 512], BF16)
                R1 = sb("R1", [128, 512])
                O1 = sb("O1", [128, 512])
                O2 = sb("O2", [128, 512])
                SQa = sb("SQa", [128, 512])
                RSa = sb("RSa", [128, 512])
                for h in range(8):
                    hb = h % 2
                    Qh, Kh, VT, CK, CVt = QhR[hb], KhR[hb], VTR[hb], CKR[hb], CVR[hb]
                    kQh, kKh, kVT, kCK, kCV = 'Qh%d' % hb, 'Kh%d' % hb, 'VT%d' % hb, 'CK%d' % hb, 'CVt%d' % hb
                    P.dma('sp', Qh[:], QS[h], writes=[kQh])
                    P.dma('sp', Kh[:], QS[8 + h], writes=[kKh])
                    P.dma('sp', VT[:], VS[h].rearrange("(j p) d -> p j d", p=128), writes=[kVT])
                    P.dma('pool', CK[:], cK_d[h], writes=[kCK])
                    P.dma('pool', CVt[:], cV_d[h].rearrange("(j p) d -> p j d", p=128), writes=[kCV])
                    jobs = [(0, 256, [('n', 0), ('n', 1)]), (256, 256, [('n', 2), ('n', 3)]),
                            (512, 512, [('c', 0), ('c', 1)] + [('n', j) for j in range(4, 12)]),
                            (1024, 512, [('c', 0), ('c', 1)] + [('n', j) for j in range(4, 12)])]
                    for (q0, nq, kts) in jobs:
                        acc = []
                        for m in range(2):
                            psO, kO = PS[2 * m], 'ps%d' % (2 * m)
                            psD, kD = PS[2 * m + 1], 'ps%d' % (2 * m + 1)
                            for ki, (kind, j) in enumerate(kts):
                                if kind == 'n':
                                    ksrc = Kh[m * 64:(m + 1) * 64, j * 128:(j + 1) * 128]
                                    vsrc = VT[:, j, :]
                                    kr, vr = kKh, kVT
                                else:
                                    ksrc = CK[m * 64:(m + 1) * 64, j * 128:(j + 1) * 128]
                                    vsrc = CVt[:, j, :]
                                    kr, vr = kCK, kCV
                                psS, kS = ps_hi()
                                MM(psS[:, 0:nq], ksrc, Qh[m * 64:(m + 1) * 64, q0:q0 + nq], True, True, [kr, kQh], [kS])
                                ex, ek = EX.next()
                                ACTV(ex[:, 0:nq], psS[:, 0:nq], AF.Exp, [kS], [ek], scale=0.125)
                                MM(psO[:, 0:nq], vsrc, ex[:, 0:nq], ki == 0, ki == len(kts) - 1, [vr, ek], [kO])
                                MM(psD[:, 0:nq], ONB[:], ex[:, 0:nq], ki == 0, ki == len(kts) - 1, ['ONB', ek], [kD])
                            acc.append((psO, kO, psD, kD))
                        (pO1, kO1, pD1, kD1), (pO2, kO2, pD2, kD2) = acc
                        n_ = slice(0, nq)
                        P.op('dve', lambda e, a=R1[:, n_], b=pD1[:, n_]: e.reciprocal(out=a, in_=b), [kD1], ['R1'])
                        TT('dve', O1[:, n_], pO1[:, n_], R1[:, n_], ALU.mult, [kO1, 'R1'], ['O1'])
                        P.op('dve', lambda e, a=R1[:, n_], b=pD2[:, n_]: e.reciprocal(out=a, in_=b), [kD2, 'O1'], ['R1'])
                        TT('dve', O2[:, n_], pO2[:, n_], R1[:, n_], ALU.mult, [kO2, 'R1'], ['O2'])
                        STT('dve', O1[:, n_], O2[:, n_], NEGLAM[:, 0:1], O1[:, n_], ALU.mult, ALU.add,
                            ['O2', 'O1', 'NEGLAM'], ['O1'])
                        TT('dve', SQa[:, n_], O1[:, n_], O1[:, n_], ALU.mult, ['O1'], ['SQa'])
                        psq, kq = ps_hi()
                        MM(psq[:, n_], ONF[:], SQa[:, n_], True, True, ['ONF', 'SQa'], [kq])
                        RSQ(RSa[:, n_], psq[:, n_], 1.0 / 128, LN_EPS, [kq], ['RSa'])
                        TT('dve', O1[:, n_], O1[:, n_], RSa[:, n_], ALU.mult, ['O1', 'RSa'], ['O1'])
                        TS('dve', OA[:, h, q0:q0 + nq], O1[:, n_], SUBG[:, 0:1], None, ALU.mult, None, ['O1', 'SUBG'], ['OA'])
                P.barrier()
                cur.pop()
            WB = Ring("WBo", 2, [128, KC, 128], BF16)
            for c in range(16):
                wb, wk = load_w(WB, wout_d[c])
                for tb in range(3):
                    ps, kp = ps_next()
                    for kc in range(KC):
                        src = OA if kc < 8 else OB
                        MM(ps[:], wb[:, kc, :], src[:, kc % 8, tb * 512:(tb + 1) * 512], kc == 0, kc == KC - 1,
                           [wk, 'OA', 'OB'], [kp])
                    resid_add(ps, kp, 0, 0, c, tb)
            P.barrier()
            cur.pop()

        chk(4)
        TBR = [[(0, 256, 1), (256, 256, 258)], [(0, 512, 515)], [(0, 512, 1027)]]

        def ffn(l):
            with ExitStack() as ph:
                cur.append(ph)
                H = sb("Hf%d" % l, [128, KC, NT], BF16)
                with ExitStack() as p2:
                    cur.append(p2)
                    SQ = Ring("SQf%d" % l, 2, [128, 512])
                    TMPN = Ring("TMPNf%d" % l, 2, [128, 512])
                    RSTD = sb("RSTDf%d" % l, [128, 512])
                    norm_mod(H, l, 1, SQ, RSTD, TMPN)
                    P.barrier()
                    cur.pop()
                G = 2
                WUr = Ring("WU%d_" % l, 2, [128, KC, 256], BF16)
                WDG = sb("WDG%d" % l, [128, G, D], BF16)
                H2G = sb("H2G%d" % l, [128, G, NPAD], BF16)
                UP = [sb("UP%d_%d" % (l, i), [128, NPAD]) for i in range(2)]
                CVg = sb("CVg%d" % l, [128, NPAD])
                for i in range(2):
                    MEMSET('dve', UP[i][:], 0.0, ['UP%d' % i])
                fo = PLAY['fdw%d' % l][0]
                SL = slice(1, 1539)
                c = 0
                while c < NFF:
                    gn = min(G, NFF - c)
                    for g in range(gn):
                        cc = c + g
                        WU, wuk = WUr.next()
                        P.dma('pool', WU[:], wup_d[l, cc], writes=[wuk])
                        P.dma('pool', WDG[:, g, :], wdn_d[l, cc], writes=[('WDG', g)])
                        for gv in range(2):
                            for tb in range(3):
                                ps, kp = ps_next()
                                for kc in range(KC):
                                    MM(ps[:], WU[:, kc, gv * 128:(gv + 1) * 128], H[:, kc, tb * 512:(tb + 1) * 512],
                                       kc == 0, kc == KC - 1, [wuk, ('H', tb)], [kp])
                                for (pc, n, po) in TBR[tb]:
                                    CP('act', UP[gv][:, po:po + n], ps[:, pc:pc + n], [kp], ['UP%d' % gv])
                            to = fo + (cc * 2 + gv) * 3
                            dst, dk = (CVg, 'CVg') if gv == 0 else (UP[0], 'UP0')
                            TS('dve', dst[:, SL], UP[gv][:, SL], PAR[:, to + 1:to + 2], None, ALU.mult, None,
                               ['UP%d' % gv, 'PAR'], [dk])
                            STT('dve', dst[:, SL], UP[gv][:, 0:1538], PAR[:, to:to + 1], dst[:, SL], ALU.mult, ALU.add,
                                ['UP%d' % gv, 'PAR', dk], [dk])
                            STT('dve', dst[:, SL], UP[gv][:, 2:1540], PAR[:, to + 2:to + 3], dst[:, SL], ALU.mult, ALU.add,
                                ['UP%d' % gv, 'PAR', dk], [dk])
                            if gv == 0:
                                ACTV(CVg[:, SL], CVg[:, SL], AF.Silu, ['CVg'], ['CVg'])
                        TT('dve', H2G[:, g, SL], CVg[:, SL], UP[0][:, SL], ALU.mult, ['CVg', 'UP0'], [('H2G', g)])
                        MEMSET('dve', UP[0][:, 257:258], 0.0, ['UP0'])
                        MEMSET('dve', UP[0][:, 514:515], 0.0, ['UP0'])
                    for ko in range(KC):
                        for tb in range(3):
                            ps, kp = ps_next()
                            for (pc, n, po) in TBR[tb]:
                                for g in range(gn):
                                    MM(ps[:, pc:pc + n], WDG[:, g, ko * 128:(ko + 1) * 128], H2G[:, g, po:po + n],
                                       g == 0, g == gn - 1, [('WDG', g), ('H2G', g)], [kp])
                            resid_add(ps, kp, l, 1, ko, tb)
                    c += gn
                P.barrier()
                cur.pop()

        ffn(0)

        chk(5)
        NC_ = NT + 60
        CO = [15, 286, 557]
        TBC = [[(0, 256, 15), (256, 256, 286)], [(0, 512, 557)], [(0, 512, 1069)]]
        with ExitStack() as ph:
            cur.append(ph)
            H = sb("Hc", [128, KC, NT], BF16)
            with ExitStack() as p2:
                cur.append(p2)
                SQ = Ring("SQc", 2, [128, 512])
                TMPN = Ring("TMPNc", 2, [128, 512])
                RSTD = sb("RSTDc", [128, 512])
                norm_mod(H, 1, 0, SQ, RSTD, TMPN)
                P.barrier()
                cur.pop()
            WBa = Ring("WBa", 2, [128, KC, 128], BF16)
            UAr = [sb("UA%d" % i, [128, NC_]) for i in range(2)]
            ACC = sb("ACC", [128, NC_])
            SQc = sb("SQc2", [128, NC_])
            MEAN = sb("MEAN", [128, 3, 512])
            RSC = sb("RSC", [128, 3, 512])
            for i in range(2):
                MEMSET('dve', UAr[i][:], 0.0, ['UA%d' % i])
            wdo = PLAY['w_dw'][0]
            bdo = PLAY['b_dw'][0]
            CS = slice(15, 1581)
            for c in range(16):
                wa, wak = load_w(WBa, wpw1_d[c])
                wg, wgk = load_w(WBa, wpw1_d[16 + c])
                UA, uak = UAr[c % 2], 'UA%d' % (c % 2)
                for tb in range(3):
                    psA, kA = PS[6], 'ps6'
                    psB, kB = PS[7], 'ps7'
                    for kc in range(KC):
                        MM(psA[:], wa[:, kc, :], H[:, kc, tb * 512:(tb + 1) * 512], kc == 0, kc == KC - 1, [wak, ('H', tb)], [kA])
                    for kc in range(KC):
                        MM(psB[:], wg[:, kc, :], H[:, kc, tb * 512:(tb + 1) * 512], kc == 0, kc == KC - 1, [wgk, ('H', tb)], [kB])
                    for (pc, n, po) in TBC[tb]:
                        ACTV(UA[:, po:po + n], psB[:, pc:pc + n], AF.Sigmoid, [kB], [uak])
                        TT('dve', UA[:, po:po + n], psA[:, pc:pc + n], UA[:, po:po + n], ALU.mult, [kA, uak], [uak])
                TS('dve', ACC[:, CS], UA[:, 0:1566], PAR[:, wdo + c * 31:wdo + c * 31 + 1], PAR[:, bdo + c:bdo + c + 1],
                   ALU.mult, ALU.add, [uak, 'PAR'], ['ACC'])
                for j in range(1, 31):
                    STT('dve', ACC[:, CS], UA[:, j:j + 1566], PAR[:, wdo + c * 31 + j:wdo + c * 31 + j + 1], ACC[:, CS],
                        ALU.mult, ALU.add, [uak, 'PAR', 'ACC'], ['ACC'])
                TT('dve', SQc[:, CS], ACC[:, CS], ACC[:, CS], ALU.mult, ['ACC'], ['SQc'])
                for tb in range(3):
                    for (pc, n, po) in TBC[tb]:
                        MM(PS[tb][:, pc:pc + n], ONF[:], ACC[:, po:po + n], c == 0, c == 15, ['ONF', 'ACC'], ['ps%d' % tb])
                        MM(PS[3 + tb][:, pc:pc + n], ONF[:], SQc[:, po:po + n], c == 0, c == 15, ['ONF', 'SQc'], ['ps%d' % (3 + tb)])
                for (t0, ln, po) in [(0, 256, 15), (256, 256, 286), (512, 1024, 557)]:
                    P.dma('sp', CONV[c, :, t0:t0 + ln], ACC[:, po:po + ln], reads=['ACC'], writes=[('CONV', c)])
            for tb in range(3):
                TS('dve', MEAN[:, tb, :], PS[tb][:], 1.0 / D, None, ALU.mult, None, ['ps%d' % tb], ['MEAN'])
                TT('dve', RSC[:, tb, :], MEAN[:, tb, :], MEAN[:, tb, :], ALU.mult, ['MEAN'], ['RSC'])
                STT('dve', RSC[:, tb, :], PS[3 + tb][:], 1.0 / D, RSC[:, tb, :], ALU.mult, ALU.subtract,
                    ['ps%d' % (3 + tb), 'RSC'], ['RSC'])
                RSQ(RSC[:, tb, :], RSC[:, tb, :], 1.0, LN_EPS, ['RSC'], ['RSC'])
            CL = Ring("CL", 1, [128, NT])
            MV = MEAN[:].rearrange("p a b -> p (a b)")
            RV = RSC[:].rearrange("p a b -> p (a b)")
            cgo = PLAY['cln_g'][0]
            cbo = PLAY['cln_b'][0]
            for c in range(16):
                cl, ck = CL.next()
                P.dma('sp', cl[:], CONV[c], reads=[('CONV', c)], writes=[ck])
                TT('dve', cl[:], cl[:], MV, ALU.subtract, [ck, 'MEAN'], [ck])
                TT('dve', cl[:], cl[:], RV, ALU.mult, [ck, 'RSC'], [ck])
                TS('dve', cl[:], cl[:], PAR[:, cgo + c:cgo + c + 1], PAR[:, cbo + c:cbo + c + 1], ALU.mult, ALU.add,
                   [ck, 'PAR'], [ck])
                ACTV(H[:, c, :], cl[:], AF.Silu, [ck], [('H', 0), ('H', 1), ('H', 2)])
            for c in range(16):
                wb, wk = load_w(WBa, wpw2_d[c])
                for tb in range(3):
                    ps, kp = PS[6 + (tb % 2)], 'ps%d' % (6 + (tb % 2))
                    for kc in range(KC):
                        MM(ps[:], wb[:, kc, :], H[:, kc, tb * 512:(tb + 1) * 512], kc == 0, kc == KC - 1, [wk, ('H', tb)], [kp])
                    resid_add(ps, kp, 1, 0, c, tb)
            P.barrier()
            cur.pop()

        chk(6)
        ffn(1)
        chk(7)

        with ExitStack() as ph:
            cur.append(ph)
            SQ = Ring("SQz", 2, [128, 512])
            YO = Ring("YO", 3, [128, 512])
            RSTD = sb("RSTDz", [128, 512])
            fo_ = PLAY['fng'][0]
            for tb in range(3):
                psN, kN = ps_next()
                for kc in range(KC):
                    sq, sk = SQ.next()
                    ACTV(sq[:], X[:, kc, tb * 512:(tb + 1) * 512], AF.Square, [('X', tb)], [sk])
                    MM(psN[:], ONF[:], sq[:], kc == 0, kc == KC - 1, [sk, 'ONF'], [kN])
                RSQ(RSTD[:], psN[:], 1.0 / D, RMS_EPS, [kN], ['RSTDz'])
                for kc in range(KC):
                    yo, yk = YO.next()
                    TT('dve', yo[:], X[:, kc, tb * 512:(tb + 1) * 512], RSTD[:], ALU.mult, [('X', tb), 'RSTDz'], [yk])
                    TS('dve', yo[:], yo[:], PAR[:, fo_ + kc:fo_ + kc + 1], None, ALU.mult, None, [yk, 'PAR'], [yk])
                    P.dma('sp', y_d[kc * 128:(kc + 1) * 128, tb * 512:(tb + 1) * 512], yo[:], reads=[yk])
            cur.pop()
        if debug:
            print('instr counts', {k: len(v) for k, v in P.prog.items()}, 'sems', P.nsem)
        P.finish()
    return nc


def _fm(v, n=None):
    v = np.asarray(v, np.float32).reshape(-1)
    return np.ascontiguousarray(v.reshape(-1, 128).T)


def _arr_w(W, cols=None):
    K, N = W.shape
    if cols is None:
        Wc = W.reshape(K // 128, 128, N // 128, 128)
        return np.ascontiguousarray(Wc.transpose(2, 1, 0, 3))
    out = np.zeros((len(cols), 128, K // 128, 128), np.float32)
    for c, cl in enumerate(cols):
        cl = np.asarray(cl)
        m = cl >= 0
        blk = np.zeros((K, 128), np.float32)
        blk[:, m] = W[:, cl[m]]
        out[c] = blk.reshape(K // 128, 128, 128).transpose(1, 0, 2)
    return out


def prep_shared(inp):
    f = lambda k: np.asarray(inp[k], np.float32)
    sh = {}
    pars = np.zeros((128, NPAR), np.float32)

    def put(name, a):
        o, w = PLAY[name]
        a = np.asarray(a, np.float32)
        assert a.shape == (128, w), (name, a.shape, w)
        pars[:, o:o + w] = a
    b_ada = f('b_ada')
    put('b_ada0', _fm(b_ada[0]))
    put('b_ada1', _fm(b_ada[1]))
    ng = f('norm_g')
    for l in range(2):
        for wh in range(2):
            put('ng%d%d' % (l, wh), _fm(ng[l, wh]))
    put('fng', _fm(f('final_norm_g')))
    mu = f('shift_mu')[0]
    for i in range(2):
        m = np.zeros((128, 28), np.float32)
        m[:, 0:24] = _fm(mu[i, 0:3072])
        m[:96, 24] = mu[i, 3072:3168]
        m[:96, 25] = mu[i, 3168:3264]
        m[:, 26:28] = _fm(mu[i, 3264:3520])
        put('mu%d' % i, m)
    for d in range(2):
        put('w0_%d' % d, _fm(f('w0')[0, d]))
        put('a0_%d' % d, _fm(f('a0')[0, d]))
    put('k_k', _fm(f('k_k')[0]))
    put('k_a', _fm(f('k_a')[0]))
    put('r_k', _fm(f('r_k')[0].reshape(-1)))
    put('lnx_g', _fm(f('lnx_g')[0]))
    put('lnx_b', _fm(f('lnx_b')[0]))
    put('subln', f('subln_g')[0].reshape(128, 1))
    put('dl', np.broadcast_to(f('diff_lambda')[0].reshape(1, 256), (128, 256)))
    put('b_dw', _fm(f('b_dw')[0]))
    put('cln_g', _fm(f('cln_g')[0]))
    put('cln_b', _fm(f('cln_b')[0]))
    wdw = f('w_dw')[0]
    put('w_dw', np.ascontiguousarray(wdw.reshape(31, 16, 128).transpose(2, 1, 0)).reshape(128, 496))
    fd = f('w_ffn_dw')
    for l in range(2):
        a = fd[l].reshape(3, 2, NFF, 128).transpose(3, 2, 1, 0)
        put('fdw%d' % l, np.ascontiguousarray(a).reshape(128, 258))
    sh['pars'] = pars
    sh['ident'] = np.eye(128, dtype=np.float32)
    bo = np.zeros((128, 128), np.float32)
    bo[:64, :64] = 1
    bo[64:, 64:] = 1
    sh['bones'] = bo
    rt = np.zeros((128, 128), np.float32)
    ang = np.zeros((128, 1024), np.float64)
    tt = np.arange(1024)
    row = (tt // 64).astype(np.float64)
    col = (tt % 64).astype(np.float64)
    inv = (10000.0 ** (-np.arange(16, dtype=np.float32) / 16)).astype(np.float32).astype(np.float64)
    for m in range(2):
        for d in range(64):
            half, r = divmod(d, 32)
            if r < 16:
                partner, sgn, fi = d + 16, -1.0, r
            else:
                partner, sgn, fi = d - 16, 1.0, r - 16
            rt[m * 64 + partner, m * 64 + d] = sgn
            pos = row if half == 0 else col
            ang[m * 64 + d] = (pos.astype(np.float32) * np.float32(inv[fi])).astype(np.float64)
    sh['rt'] = rt
    sh['cos'] = np.cos(ang).astype(np.float32)
    sh['sin'] = np.sin(ang).astype(np.float32)
    wa = f('w_ada')
    sh['wada'] = np.ascontiguousarray(wa.reshape(2, KC, 128, 24, 512).transpose(0, 3, 2, 1, 4))
    sh['win'] = _arr_w(f('w_in')[0], win_cols())
    lora = np.zeros((128, 3, 2, 1024), np.float32)
    lora[:96, 0] = f('w2')[0].transpose(1, 0, 2)
    lora[:96, 1] = f('a2')[0].transpose(1, 0, 2)
    lora[:, 2] = f('g2')[0].reshape(2, 128, 1024).transpose(1, 0, 2)
    sh['lora'] = lora
    sh['wout'] = _arr_w(f('w_out')[0])
    sh['wpw1'] = _arr_w(f('w_pw1')[0])
    sh['wpw2'] = _arr_w(f('w_pw2')[0])
    wu = f('w_up')
    a = wu.reshape(2, KC, 128, 2, NFF, 128).transpose(0, 4, 2, 1, 3, 5)
    sh['wup'] = np.ascontiguousarray(a).reshape(2, NFF, 128, KC, 256)
    sh['wdn'] = np.ascontiguousarray(f('w_down').reshape(2, NFF, 128, D))
    return sh


def prep_core(inp, i):
    f = lambda k: np.asarray(inp[k], np.float32)
    m = {}
    xp = f('x_prompt')[2 * i:2 * i + 2].reshape(512, D)
    xs = f('x_sample')[i]
    m['xin'] = np.ascontiguousarray(np.concatenate([xp, xs], 0).T)
    cf = np.stack([f('c')[i], f('c_ctx')], -1)
    m['cfm'] = np.ascontiguousarray(cf.reshape(KC, 128, 2).transpose(1, 0, 2))
    ck = f('cache_k')[i, 0]
    m['cK'] = np.ascontiguousarray(ck.transpose(0, 1, 3, 2)).reshape(8, 128, 256)
    m['cV'] = np.ascontiguousarray(f('cache_v')[i, 0])
    s0 = np.stack([f('state_wkv_fwd')[i, 0], f('state_wkv_bwd')[i, 0]], 0)
    m['s0'] = np.ascontiguousarray(s0.reshape(2, 2, 8, 64, 64).transpose(0, 1, 3, 2, 4)).reshape(2, 128, 512)
    return m


_NC_CACHE = {}


def kernel(**inputs):
    sh = prep_shared(inputs)
    in_maps = []
    for i in range(8):
        m = dict(sh)
        m.update(prep_core(inputs, i))
        in_maps.append(m)
    if 'nc' not in _NC_CACHE:
        _NC_CACHE['nc'] = build()
    nc = _NC_CACHE['nc']
    res = run_bass_kernel_spmd(nc, in_maps, core_ids=list(range(8)))
    yp = np.zeros((16, 256, D), np.float32)
    ys = np.zeros((8, 1024, D), np.float32)
    nk = np.zeros((16, 1, 8, 2, 256, 64), np.float32)
    nv = np.zeros((16, 1, 8, 256, 128), np.float32)
    sf = np.zeros((16, 1, 16, 64, 64), np.float32)
    sbw = np.zeros((16, 1, 16, 64, 64), np.float32)
    for i in range(8):
        r = res.results[i]
        y = np.asarray(r['y'], np.float32).T
        yp[2 * i:2 * i + 2] = y[:512].reshape(2, 256, D)
        ys[i] = y[512:]
        k = np.asarray(r['nk'], np.float32).reshape(2, 8, 2, 64, 256)
        nk[2 * i:2 * i + 2, 0] = k.transpose(0, 1, 2, 4, 3)
        nv[2 * i:2 * i + 2, 0] = np.asarray(r['nv'], np.float32)
        s = np.asarray(r['nst'], np.float32).reshape(2, 2, 2, 64, 8, 64)
        s = s.transpose(0, 1, 2, 4, 3, 5).reshape(2, 2, 16, 64, 64)
        sf[2 * i:2 * i + 2, 0] = s[:, 0]
        sbw[2 * i:2 * i + 2, 0] = s[:, 1]
    return (yp, ys, nk, nv, sf, sbw)
```

```python
import math
import numpy as np
import concourse.bass as bass
import concourse.mybir as mybir
from concourse.bass_utils import run_bass_kernel_spmd
from contextlib import ExitStack

F32 = mybir.dt.float32
BF16 = mybir.dt.bfloat16
ALU = mybir.AluOpType
AF = mybir.ActivationFunctionType
AX = mybir.AxisListType

D = 2048
KC = 16
NT = 1536
NPAD = NT + 4
SEQS = [(0, 256), (256, 256), (512, 1024)]
POFF = [1, 258, 515]
D_FF = 5504
NFF = 43
LAM_INIT = 0.8 - 0.6 * math.exp(0.0)
RMS_EPS = 1e-6
LN_EPS = 1e-5
GN_EPS = 64e-5
N_WIN = 52


class Prog:
    NS = 8

    def __init__(self, nc, stack):
        self.nc = nc
        self.stack = stack
        self.ce = ['pe', 'dve', 'act', 'pool']
        self.engs = ['pe', 'dve', 'act', 'pool', 'sp']
        self.prog = {e: [] for e in self.engs}
        self.sems = {}
        self.cnt = {}
        self.waited = {e: {} for e in self.engs}
        self.lastw = {}
        self.readers = {}
        self.ndma = {e: 0 for e in self.engs}
        self.epoch = 0
        self.nsem = 0
        self.dead = False
        for e in self.ce:
            self._mk(('c', e, 0))

    def _mk(self, key):
        self.sems[key] = self.stack.enter_context(self.nc.semaphore("s%d" % self.nsem))
        self.nsem += 1
        self.cnt[key] = 0

    def _deps(self, reads, writes):
        deps = {}
        for r in reads:
            lw = self.lastw.get(r)
            if lw is not None and deps.get(lw[0], 0) < lw[1]:
                deps[lw[0]] = lw[1]
            if isinstance(r, str) and r.startswith('ps'):
                for sk, v in self.readers.get(r, {}).items():
                    if deps.get(sk, 0) < v:
                        deps[sk] = v
        for w in writes:
            lw = self.lastw.get(w)
            if lw is not None and deps.get(lw[0], 0) < lw[1]:
                deps[lw[0]] = lw[1]
            for sk, v in self.readers.get(w, {}).items():
                if deps.get(sk, 0) < v:
                    deps[sk] = v
        return deps

    def _waits(self, eng, deps, skip=None):
        waits = []
        wd = self.waited[eng]
        for sk, v in deps.items():
            if sk == skip:
                continue
            if wd.get(sk, 0) < v:
                waits.append((self.sems[sk], v))
                wd[sk] = v
        return waits

    def _record(self, sk, val, reads, writes):
        for r in reads:
            d = self.readers.setdefault(r, {})
            if d.get(sk, 0) < val:
                d[sk] = val
        for w in writes:
            self.lastw[w] = (sk, val)
            self.readers[w] = {}

    def op(self, eng, fn, reads=(), writes=()):
        if self.dead:
            return
        own = ('c', eng, self.epoch)
        deps = self._deps(reads, writes)
        waits = self._waits(eng, deps, skip=own if eng == 'pe' else None)
        self.cnt[own] += 1
        val = self.cnt[own]
        sem = self.sems[own]

        def emit(e):
            for s, v in waits:
                e.wait_ge(s, v)
            fn(e).then_inc(sem, 1)
        self.prog[eng].append(emit)
        self._record(own, val, reads, writes)

    def dma(self, eng, out, in_, reads=(), writes=(), **kw):
        if self.dead:
            return
        i = self.ndma[eng]
        self.ndma[eng] += 1
        sk = ('d', eng, i % self.NS)
        if sk not in self.sems:
            self._mk(sk)
        target = 16 * (i // self.NS + 1)
        deps = self._deps(reads, writes)
        if target > 16 and deps.get(sk, 0) < target - 16:
            deps[sk] = target - 16
        waits = self._waits(eng, deps)
        self.cnt[sk] = target
        sem = self.sems[sk]

        def emit(e):
            for s, v in waits:
                e.wait_ge(s, v)
            e.dma_start(out=out, in_=in_, **kw).then_inc(sem, 16)
        self.prog[eng].append(emit)
        self._record(sk, target, reads, writes)

    def _finals(self):
        return {sk: c for sk, c in self.cnt.items()
                if c > 0 and (sk[0] == 'd' or sk[2] == self.epoch)}

    def barrier(self):
        if self.dead:
            return
        finals = self._finals()
        for eng in self.engs:
            waits = self._waits(eng, finals, skip=('c', eng, self.epoch))
            if waits:
                def emit(e, waits=waits):
                    for s, v in waits:
                        e.wait_ge(s, v)
                self.prog[eng].append(emit)
        self.epoch += 1
        for e in self.ce:
            self._mk(('c', e, self.epoch))
        self.lastw = {}
        self.readers = {}

    def finish(self):
        finals = self._finals()
        for eng in self.engs:
            waits = self._waits(eng, finals, skip=('c', eng, self.epoch))

            def emit(e, waits=waits):
                for s, v in waits:
                    e.wait_ge(s, v)
            self.prog[eng].append(emit)
        prog = self.prog
        with self.nc.Block() as block:
            @block.tensor
            def _(e):
                for f in prog['pe']:
                    f(e)

            @block.vector
            def _(e):
                for f in prog['dve']:
                    f(e)

            @block.scalar
            def _(e):
                for f in prog['act']:
                    f(e)

            @block.gpsimd
            def _(e):
                for f in prog['pool']:
                    f(e)

            @block.sync
            def _(e):
                for f in prog['sp']:
                    f(e)


def par_layout():
    ents = [('b_ada0', 96), ('b_ada1', 96), ('ng00', 16), ('ng01', 16), ('ng10', 16), ('ng11', 16),
            ('fng', 16), ('mu0', 28), ('mu1', 28), ('w0_0', 8), ('w0_1', 8), ('a0_0', 8), ('a0_1', 8),
            ('k_k', 8), ('k_a', 8), ('r_k', 8), ('lnx_g', 8), ('lnx_b', 8), ('subln', 1), ('dl', 256),
            ('b_dw', 16), ('cln_g', 16), ('cln_b', 16), ('w_dw', 496), ('fdw0', 258), ('fdw1', 258)]
    lay = {}
    o = 0
    for n, w in ents:
        lay[n] = (o, w)
        o += w
    return lay, o


PLAY, NPAR = par_layout()


def win_cols():
    ch = []
    for h in range(8):
        ch.append(list(range(h * 128, h * 128 + 128)))
    for h in range(8):
        ch.append(list(range(1024 + h * 128, 1024 + h * 128 + 128)))
    for h in range(8):
        ch.append(list(range(2048 + h * 128, 2048 + h * 128 + 128)))
    ch.append(list(range(6144, 6240)) + [-1] * 32)
    ch.append(list(range(6240, 6336)) + [-1] * 32)
    ch.append(list(range(6336, 6464)))
    ch.append(list(range(6464, 6592)))
    for hp in range(8):
        for base in (3072, 4096, 5120):
            ch.append(list(range(base + hp * 128, base + hp * 128 + 128)))
    return ch


def rest_idx(c):
    if c == 24:
        return 24
    if c == 25:
        return 25
    if c in (26, 27):
        return c
    hp, t = divmod(c - 28, 3)
    return t * 8 + hp


def build(debug=0, stop=99, lite=0, sub=None):
    nc = bass.Bass('TRN2', target_bir_lowering=False)

    def din(name, shape, dt=F32):
        return nc.dram_tensor(name, list(shape), dt, kind="ExternalInput").ap()

    def dout(name, shape, dt=F32):
        return nc.dram_tensor(name, list(shape), dt, kind="ExternalOutput").ap()

    def dscr(name, shape, dt=F32):
        return nc.dram_tensor(name, list(shape), dt, kind="ExternalOutput" if debug else "Internal").ap()

    xin = din("xin", [D, NT])
    cfm_d = din("cfm", [128, KC, 2])
    pars_d = din("pars", [128, NPAR])
    ident_d = din("ident", [128, 128])
    bones_d = din("bones", [128, 128])
    rt_d = din("rt", [128, 128])
    cos_d = din("cos", [128, 1024])
    sin_d = din("sin", [128, 1024])
    cK_d = din("cK", [8, 128, 256])
    cV_d = din("cV", [8, 256, 128])
    s0_d = din("s0", [2, 128, 512])
    wada_d = din("wada", [2, 24, 128, KC, 512] if not lite else [1, 1, 128, 1, 512])
    win_d = din("win", [N_WIN, 128, KC, 128])
    lora_d = din("lora", [128, 3, 2, 1024])
    wout_d = din("wout", [16, 128, KC, 128] if not (lite and stop < 4) else [1, 1, 128, 1, 128])
    wpw1_d = din("wpw1", [32, 128, KC, 128] if not (lite and stop < 6) else [1, 1, 128, 1, 128])
    wpw2_d = din("wpw2", [16, 128, KC, 128] if not (lite and stop < 6) else [1, 1, 128, 1, 128])
    wup_d = din("wup", [2, NFF, 128, KC, 256] if not (lite and stop < 5) else [1, 1, 128, 1, 128])
    wdn_d = din("wdn", [2, NFF, 128, D] if not (lite and stop < 5) else [1, 1, 128, 1, 128])
    if lite:
        class _Dm:
            def __getitem__(self, k):
                return self

            def __getattr__(self, n):
                return lambda *a, **k: self
        if stop < 4:
            wout_d = _Dm()
        if stop < 5:
            wup_d = wdn_d = _Dm()
        if stop < 6:
            wpw1_d = wpw2_d = _Dm()
    y_d = dout("y", [D, NT])
    nk_d = dout("nk", [2, 8, 128, 256])
    nv_d = dout("nv", [2, 8, 256, 128])
    nst_d = dout("nst", [2, 2, 128, 512])
    QS = dscr("QS", [16, 128, NT], BF16)
    VS = dscr("VS", [8, NT, 128], BF16)
    TMS = dscr("TMS", [10, NT, 1024], BF16)
    VBF = dscr("VBF", [1024, NT])
    BON = dscr("BON", [8, 128, NT])
    GATE = dscr("GATE", [8, 128, NT])
    YF = dscr("YF", [2, 1024, NT])
    CONV = dscr("CONV", [16, 128, NT])
    dbg = {}

    with ExitStack() as st:
        P = Prog(nc, st)

        cur = [st]

        def sb(name, shape, dt=F32):
            return cur[-1].enter_context(nc.sbuf_tensor(name, list(shape), dt))

        PS = [st.enter_context(nc.psum_tensor("ps%d" % i, [128, 512], F32)) for i in range(8)]
        psn = [0]

        psh = [0]

        def ps_hi():
            i = 4 + psh[0] % 4
            psh[0] += 1
            return PS[i], 'ps%d' % i

        psmod = [8]

        def ps_next():
            i = psn[0] % psmod[0]
            psn[0] += 1
            return PS[i], 'ps%d' % i

        class Ring:
            def __init__(self, name, n, shape, dt=F32):
                self.t = [sb("%s%d" % (name, i), shape, dt) for i in range(n)]
                self.k = ["%s%d" % (name, i) for i in range(n)]
                self.i = 0

            def next(self):
                j = self.i % len(self.t)
                self.i += 1
                return self.t[j], self.k[j]

        def TT(eng, out, in0, in1, op, r, w):
            P.op(eng, lambda e: e.tensor_tensor(out=out, in0=in0, in1=in1, op=op), r, w)

        def TS(eng, out, in0, s1, s2, op0, op1, r, w):
            if s2 is None:
                P.op(eng, lambda e: e.tensor_single_scalar(out=out, in_=in0, scalar=s1, op=op0), r, w)
            else:
                P.op(eng, lambda e: e.tensor_scalar(out=out, in0=in0, scalar1=s1, scalar2=s2, op0=op0, op1=op1), r, w)

        def STT(eng, out, in0, scalar, in1, op0, op1, r, w):
            P.op(eng, lambda e: e.scalar_tensor_tensor(out=out, in0=in0, scalar=scalar, in1=in1, op0=op0, op1=op1), r, w)

        def ACTV(out, in_, func, r, w, bias=None, scale=None):
            kw = {}
            if bias is not None:
                kw['bias'] = bias
            if scale is not None:
                kw['scale'] = scale
            P.op('act', lambda e: e.activation(out=out, in_=in_, func=func, **kw), r, w)

        EPSI = {RMS_EPS: 0, LN_EPS: 1, GN_EPS: 2}

        def RSQ(out, in_, scale, eps, r, w):
            j = EPSI[eps]
            ACTV(out, in_, AF.Ln, list(r) + ['EPS'], w, bias=EPS[:, j:j + 1], scale=scale)
            ACTV(out, out, AF.Exp, w, w, scale=-0.5)

        def CP(eng, out, in_, r, w):
            if eng == 'act':
                P.op('act', lambda e: e.copy(out=out, in_=in_), r, w)
            elif 'PSum' in type(in_.tensor).__name__:
                P.op(eng, lambda e: e.tensor_single_scalar(out=out, in_=in_, scalar=1.0, op=ALU.mult), r, w)
            else:
                P.op(eng, lambda e: e.tensor_copy(out=out, in_=in_), r, w)

        def MM(out, lhsT, rhs, start, stop, r, w):
            P.op('pe', lambda e: e.matmul(out, lhsT=lhsT, rhs=rhs, start=start, stop=stop), r, w)

        def RED(eng, out, in_, r, w):
            P.op(eng, lambda e: e.tensor_reduce(out=out, in_=in_, axis=AX.X, op=ALU.add), r, w)

        def MEMSET(eng, ap, val, w):
            P.op(eng, lambda e: e.memset(ap, val), (), w)

        def chk(k):
            if stop <= k and not P.dead:
                if debug:
                    dx = dout("dbgX", [128, KC, NT])
                    P.dma('sp', dx, X[:], reads=[('X', 0), ('X', 1), ('X', 2)])
                P.dead = True

        X = sb("X", [128, KC, NT])
        PAR = sb("PAR", [128, NPAR])
        IDF = sb("IDF", [128, 128])
        IDB = sb("IDB", [128, 128], BF16)
        ONF = sb("ONF", [128, 128])
        ONB = sb("ONB", [128, 128], BF16)
        BOF = sb("BOF", [128, 128])
        MOD = sb("MOD", [128, 2, 96, 2])
        GG = sb("GG", [128, 2, 2, KC, 2])
        NEGLAM = sb("NEGLAM", [128, 1])
        SUBG = sb("SUBG", [128, 1])
        CM = sb("CM", [128, 28])
        EPS = sb("EPS", [128, 4])
        SC = sb("SC", [128, KC, 2])

        def par(name, j0=0, n=None):
            o, w = PLAY[name]
            if n is None:
                n = w - j0
            return PAR[:, o + j0:o + j0 + n]

        P.dma('sp', PAR[:], pars_d, writes=['PAR'])
        P.dma('sp', IDF[:], ident_d, writes=['IDF'])
        P.dma('pool', IDB[:], ident_d, writes=['IDB'])
        P.dma('sp', BOF[:], bones_d, writes=['BOF'])
        MEMSET('dve', ONF[:], 1.0, ['ONF'])
        for eps_, j_ in EPSI.items():
            MEMSET('dve', EPS[:, j_:j_ + 1], eps_, ['EPS'])
        MEMSET('dve', ONB[:], 1.0, ['ONB'])
        for kc in range(KC):
            P.dma('sp', X[:, kc, :], xin[kc * 128:(kc + 1) * 128, :], writes=[('X', 0), ('X', 1), ('X', 2)])

        with ExitStack() as ph:
            cur.append(ph)
            sbp = sb
            WA = [sbp("WA%d" % i, [128, KC, 512]) for i in range(2)]
            P.dma('sp', SC[:], cfm_d, writes=['SC'])
            ACTV(SC[:], SC[:], AF.Silu, ['SC'], ['SC'])
            psM, kM = PS[0], 'ps0'
            for l in ([0, 1] if lite else [0]):
                for jb in range(8 if not lite else 0):
                    wa = WA[jb % 2]
                    wk = 'WA%d' % (jb % 2)
                    P.dma('sp' if jb % 2 == 0 else 'act', wa[:], wada_d[l, jb], writes=[wk])
                    for jj in range(4):
                        j = jb * 4 + jj
                        for kc in range(KC):
                            MM(psM[:, 2 * j:2 * j + 2], wa[:, kc, jj * 128:(jj + 1) * 128], SC[:, kc, :],
                               kc == 0, kc == KC - 1, [wk, 'SC'], [kM])
                bo = PLAY['b_ada%d' % l][0]
                if lite:
                    MEMSET('dve', MOD[:, l], 0.05, ['MOD'])
                else:
                    TT('dve', MOD[:, l, 0:32], psM[:, 0:64].rearrange("p (j s) -> p j s", s=2),
                       PAR[:, bo:bo + 32].unsqueeze(2).to_broadcast([128, 32, 2]), ALU.add, [kM, 'PAR'], ['MOD'])
                for wh in (range(2) if lite else range(1)):
                    go = PLAY['ng%d%d' % (l, wh)][0]
                    STT('dve', GG[:, l, wh], MOD[:, l, (1 + 3 * wh) * 16:(2 + 3 * wh) * 16, :], 1.0,
                        PAR[:, go:go + 16].unsqueeze(2).to_broadcast([128, 16, 2]), ALU.add, ALU.mult,
                        ['MOD', 'PAR'], ['GG'])
            DT = sbp("DT", [128, 2, 64])
            DS = sbp("DS", [128, 2])
            dlo = PLAY['dl'][0]
            dlv = PAR[:, dlo:dlo + 256].rearrange("p (a b k) -> p a b k", a=2, b=2)
            TT('dve', DT[:], dlv[:, :, 0, :], dlv[:, :, 1, :], ALU.mult, ['PAR'], ['DT'])
            RED('dve', DS[:], DT[:], ['DT'], ['DS'])
            ACTV(DS[:], DS[:], AF.Exp, ['DS'], ['DS'])
            TT('dve', NEGLAM[:], DS[:, 1:2], DS[:, 0:1], ALU.subtract, ['DS'], ['NEGLAM'])
            TS('dve', NEGLAM[:], NEGLAM[:], -LAM_INIT, None, ALU.add, None, ['NEGLAM'], ['NEGLAM'])
            TS('dve', SUBG[:], par('subln'), 1.0 - LAM_INIT, None, ALU.mult, None, ['PAR'], ['SUBG'])
            TT('dve', CM[:], par('mu0'), par('mu1'), ALU.add, ['PAR'], ['CM'])
            TS('dve', CM[:], CM[:], -1.0, 1.0, ALU.mult, ALU.add, ['CM'], ['CM'])
            P.barrier()
            cur.pop()

        def norm_mod(H, l, wh, SQ, RSTD, TMPN):
            for tb in range(3):
                s = 1 if tb == 0 else 0
                psN, kN = ps_next()
                for kc in range(KC):
                    sq, sk = SQ.next()
                    ACTV(sq[:], X[:, kc, tb * 512:(tb + 1) * 512], AF.Square, [('X', tb)], [sk])
                    MM(psN[:], ONF[:], sq[:], kc == 0, kc == KC - 1, [sk, 'ONF'], [kN])
                RSQ(RSTD[:], psN[:], 1.0 / D, RMS_EPS, [kN], ['RSTD'])
                for kc in range(KC):
                    tm, tk = TMPN.next()
                    TT('dve', tm[:], X[:, kc, tb * 512:(tb + 1) * 512], RSTD[:], ALU.mult, [('X', tb), 'RSTD'], [tk])
                    ACTV(H[:, kc, tb * 512:(tb + 1) * 512], tm[:], AF.Identity, [tk, 'GG', 'MOD'], [('H', tb)],
                         bias=MOD[:, l, (3 * wh) * 16 + kc, s:s + 1], scale=GG[:, l, wh, kc, s:s + 1])

        def resid_add(ps, kps, l, wh, kc, tb):
            s = 1 if tb == 0 else 0
            xs = X[:, kc, tb * 512:(tb + 1) * 512]
            STT('dve', xs, ps[:], MOD[:, l, (2 + 3 * wh) * 16 + kc, s:s + 1], xs, ALU.mult, ALU.add,
                [kps, 'MOD', ('X', tb)], [('X', tb)])


        def pad_off(t):
            return t + 1 if t < 256 else (t + 2 if t < 512 else t + 3)
        PCS = [(1, 385), (386, 385), (771, 384), (1155, 384)]
        SEQP = [(0, 256, 1), (256, 256, 258), (512, 1024, 515)]
        RS = dscr("RS", [28, 128, NPAD])

        def load_w(WBr, src, eng='pool'):
            wb, wk = WBr.next()
            P.dma(eng, wb[:], src, writes=[wk])
            return wb, wk

        with ExitStack() as ph:
            cur.append(ph)
            H = sb("H", [128, KC, NT], BF16)
            SQ = Ring("SQ", 2, [128, 512])
            TMPN = Ring("TMPN", 2, [128, 512])
            RSTD = sb("RSTD", [128, 512])
            norm_mod(H, 0, 0, SQ, RSTD, TMPN)
            WB = Ring("WB", 2, [128, KC, 128], BF16)
            COS = sb("COS", [128, 1024])
            SIN = sb("SIN", [128, 1024])
            RTF = sb("RTF", [128, 128])
            P.dma('sp', COS[:], cos_d, writes=['COS'])
            P.dma('sp', SIN[:], sin_d, writes=['SIN'])
            P.dma('sp', RTF[:], rt_d, writes=['RTF'])
            RAWP = sb("RAWP", [128, NPAD])
            OUTP = sb("OUTP", [128, NPAD])
            MEMSET('dve', RAWP[:], 0.0, ['RAWP'])
            QB = Ring("QB", 2, [128, 512], BF16)
            XS = Ring("XS", 2, [128, 512])
            T1 = Ring("T1", 2, [128, 512])
            T2 = Ring("T2", 2, [128, 512])
            HK = [('H', 0), ('H', 1), ('H', 2)]

            def proj_tb(wb, wk, tb):
                ps, kp = ps_next()
                for kc in range(KC):
                    MM(ps[:], wb[:, kc, :], H[:, kc, tb * 512:(tb + 1) * 512], kc == 0, kc == KC - 1,
                       [wk, ('H', tb)], [kp])
                return ps, kp

            SUB = sub if sub is not None else 'qvr'
            for c in range(16 if 'q' in SUB else 0):
                wb, wk = load_w(WB, win_d[c])
                if 'L' in SUB:
                    continue
                for tb in range(3):
                    ps, kp = proj_tb(wb, wk, tb)
                    if 'M' in SUB:
                        continue
                    if 'E' in SUB and tb > 0:
                        continue
                    qb, qk = QB.next()
                    if tb == 0:
                        CP('act', qb[:], ps[:], [kp], [qk])
                        if c >= 8 and 'A' not in SUB:
                            xs, xk = XS.next()
                            CP('dve', xs[:], ps[:], [kp, qk] if 'S' in SUB else [kp], [xk])
                            for pi in range(2 if 'C' not in SUB else 0):
                                P.dma('sp', nk_d[pi, c - 8], xs[:, pi * 256:(pi + 1) * 256], reads=[xk])
                    else:
                        xs, xk = XS.next()
                        CP('act', xs[:], ps[:], [kp], [xk])
                        ps2, kp2 = ps_next()
                        MM(ps2[:], RTF[:], xs[:], True, True, ['RTF', xk], [kp2])
                        t1, k1 = T1.next()
                        t2, k2 = T2.next()
                        TT('dve', t1[:], xs[:], COS[:, (tb - 1) * 512:tb * 512], ALU.mult, [xk, 'COS'], [k1])
                        TT('dve', t2[:], ps2[:], SIN[:, (tb - 1) * 512:tb * 512], ALU.mult, [kp2, 'SIN'], [k2])
                        TT('dve', qb[:], t1[:], t2[:], ALU.add, [k1, k2], [qk])
                    if 'B' not in SUB:
                        P.dma('sp', QS[c, :, tb * 512:(tb + 1) * 512], qb[:], reads=[qk], writes=[('QS', c)])
            for h in range(8 if 'v' in SUB else 0):
                wb, wk = load_w(WB, win_d[16 + h])
                for g in range(3):
                    ps, kp = ps_next()
                    for j in range(4):
                        tt = g * 4 + j
                        for kc in range(KC):
                            MM(ps[:, j * 128:(j + 1) * 128], H[:, kc, tt * 128:(tt + 1) * 128], wb[:, kc, :],
                               kc == 0, kc == KC - 1, [wk, ('H', g)], [kp])
                    qb, qk = QB.next()
                    CP('act', qb[:], ps[:], [kp], [qk])
                    P.dma('sp', VS[h, g * 512:(g + 1) * 512, :].rearrange("(j p) d -> p j d", p=128),
                          qb[:].rearrange("p (j d) -> p j d", d=128), reads=[qk], writes=[('VS', h)])
                    if g == 0:
                        xs, xk = XS.next()
                        CP('dve', xs[:], ps[:], [kp], [xk])
                        for pi in range(2):
                            P.dma('sp', nv_d[pi, h].rearrange("(j p) d -> p j d", p=128),
                                  xs[:, pi * 256:(pi + 1) * 256].rearrange("p (j d) -> p j d", d=128), reads=[xk])
            for c in range(24, N_WIN if 'r' in SUB else 24):
                ri = rest_idx(c)
                wb, wk = load_w(WB, win_d[c])
                for tb in range(3):
                    ps, kp = proj_tb(wb, wk, tb)
                    if tb == 0:
                        CP('act', RAWP[:, 1:257], ps[:, 0:256], [kp], ['RAWP'])
                        CP('act', RAWP[:, 258:514], ps[:, 256:512], [kp], ['RAWP'])
                    else:
                        o = 515 + (tb - 1) * 512
                        CP('act', RAWP[:, o:o + 512], ps[:], [kp], ['RAWP'])
                m0o = PLAY['mu0'][0] + ri
                m1o = PLAY['mu1'][0] + ri
                TS('dve', OUTP[:, 1:1539], RAWP[:, 1:1539], CM[:, ri:ri + 1], None, ALU.mult, None, ['RAWP', 'CM'], ['OUTP'])
                STT('dve', OUTP[:, 1:1539], RAWP[:, 0:1538], PAR[:, m0o:m0o + 1], OUTP[:, 1:1539], ALU.mult, ALU.add,
                    ['RAWP', 'PAR', 'OUTP'], ['OUTP'])
                STT('dve', OUTP[:, 1:1539], RAWP[:, 2:1540], PAR[:, m1o:m1o + 1], OUTP[:, 1:1539], ALU.mult, ALU.add,
                    ['RAWP', 'PAR', 'OUTP'], ['OUTP'])
                P.dma('sp', RS[ri], OUTP[:], reads=['OUTP'], writes=[('RS', ri)])
            P.barrier()
            cur.pop()

        chk(1)
        with ExitStack() as ph:
            cur.append(ph)
            LORA = sb("LORA", [128, 3, 2, 1024], BF16)
            P.dma('pool', LORA[:], lora_d, writes=['LORA'])
            TWr = sb("TWr", [128, NPAD])
            TW = sb("TW", [128, NPAD], BF16)
            XA = sb("XA", [128, NPAD], BF16)
            SG = sb("SG", [128, 2, NPAD], BF16)
            OMK = sb("OMK", [128, 8])
            TS('dve', OMK[:], par('k_a'), -1.0, 1.0, ALU.mult, ALU.add, ['PAR'], ['OMK'])
            P.dma('sp', TWr[:], RS[24], writes=['TWr'])
            ACTV(TW[:], TWr[:], AF.Tanh, ['TWr'], ['TW'])
            P.dma('sp', TWr[:], RS[25], reads=[], writes=['TWr'])
            CP('act', XA[:], TWr[:], ['TWr'], ['XA'])
            for i in range(2):
                P.dma('sp', TWr[:], RS[26 + i], writes=['TWr'])
                ACTV(SG[:, i, :], TWr[:], AF.Sigmoid, ['TWr'], ['SG'])
            Rb = sb("Rb", [128, NPAD])
            KBb = sb("KBb", [128, NPAD])
            VBb = sb("VBb", [128, NPAD])
            KKb = sb("KKb", [128, NPAD])
            Ab = sb("Ab", [128, NPAD])
            TA = sb("TA", [128, NPAD])
            TBb = sb("TBb", [128, NPAD])
            WDb = sb("WDb", [128, NPAD])
            ALRb = sb("ALRb", [128, NPAD])
            KDb = sb("KDb", [128, NPAD])
            BDb = sb("BDb", [128, NPAD])
            TMB = Ring("TMB", 4, [128, 512], BF16)
            SL = slice(1, 1539)

            def to_tm(arr, akey, idx, hp, hilo=False):
                for g in range(3):
                    ps, kp = ps_next()
                    for j in range(4):
                        o = pad_off((g * 4 + j) * 128)
                        MM(ps[:, j * 128:(j + 1) * 128], arr[:, o:o + 128], IDF[:], True, True, [akey, 'IDF'], [kp])
                    tmb, tk = TMB.next()
                    CP('act', tmb[:], ps[:], [kp], [tk])
                    dst = TMS[idx, g * 512:(g + 1) * 512, hp * 128:(hp + 1) * 128].rearrange("(j p) c -> p j c", p=128)
                    P.dma('sp', dst, tmb[:].rearrange("p (j c) -> p j c", c=128), reads=[tk], writes=[('TMS', idx, hp)])
                    if hilo:
                        tml, tlk = TMB.next()
                        TT('dve', tml[:], ps[:], tmb[:], ALU.subtract, [kp, tk], [tlk])
                        dst2 = TMS[idx + 1, g * 512:(g + 1) * 512, hp * 128:(hp + 1) * 128].rearrange("(j p) c -> p j c", p=128)
                        P.dma('sp', dst2, tml[:].rearrange("p (j c) -> p j c", c=128), reads=[tlk], writes=[('TMS', idx + 1, hp)])

            def seq_store(dst_fn, arr, akey, wkey):
                for (t0, ln, po) in SEQP:
                    P.dma('sp', dst_fn(t0, ln), arr[:, po:po + ln], reads=[akey], writes=[wkey])

            for hp in range(8):
                P.dma('sp', Rb[:], RS[hp], writes=['Rb'])
                P.dma('sp', KBb[:], RS[8 + hp], writes=['KBb'])
                P.dma('sp', VBb[:], RS[16 + hp], writes=['VBb'])
                seq_store(lambda t0, ln: VBF[hp * 128:(hp + 1) * 128, t0:t0 + ln], VBb, 'VBb', ('VBF', hp))
                kko = PLAY['k_k'][0] + hp
                TS('dve', KKb[:, SL], KBb[:, SL], PAR[:, kko:kko + 1], None, ALU.mult, None, ['KBb', 'PAR'], ['KKb'])
                TT('dve', TA[:, SL], KKb[:, SL], KKb[:, SL], ALU.mult, ['KKb'], ['TA'])
                for (o, n) in PCS:
                    ps, kp = ps_next()
                    MM(ps[:, 0:n], BOF[:], TA[:, o:o + n], True, True, ['BOF', 'TA'], [kp])
                    ACTV(TBb[:, o:o + n], ps[:, 0:n], AF.Sqrt, [kp], ['TBb'])
                TS('dve', TBb[:, SL], TBb[:, SL], 1e-12, None, ALU.max, None, ['TBb'], ['TBb'])
                P.op('dve', lambda e: e.reciprocal(out=TBb[:, SL], in_=TBb[:, SL]), ['TBb'], ['TBb'])
                TT('dve', KKb[:, SL], KKb[:, SL], TBb[:, SL], ALU.mult, ['KKb', 'TBb'], ['KKb'])
                TS('dve', Ab[:, SL], KKb[:, SL], -1.0, None, ALU.mult, None, ['KKb'], ['Ab'])
                to_tm(Ab, 'Ab', 0, hp)
                to_tm(Rb, 'Rb', 1, hp)
                rko = PLAY['r_k'][0] + hp
                TT('dve', TA[:, SL], Rb[:, SL], KBb[:, SL], ALU.mult, ['Rb', 'KBb'], ['TA'])
                TS('dve', TA[:, SL], TA[:, SL], PAR[:, rko:rko + 1], None, ALU.mult, None, ['TA', 'PAR'], ['TA'])
                for (o, n) in PCS:
                    ps, kp = ps_next()
                    MM(ps[:, 0:n], BOF[:], TA[:, o:o + n], True, True, ['BOF', 'TA'], [kp])
                    TT('dve', TBb[:, o:o + n], ps[:, 0:n], VBb[:, o:o + n], ALU.mult, [kp, 'VBb'], ['TBb'])
                seq_store(lambda t0, ln: BON[hp, :, t0:t0 + ln], TBb, 'TBb', ('BON', hp))
                for (o, n) in PCS:
                    ps, kp = ps_next()
                    for kc in range(2):
                        MM(ps[:, 0:n], LORA[:, 2, kc, hp * 128:(hp + 1) * 128], SG[:, kc, o:o + n], kc == 0, kc == 1,
                           ['LORA', 'SG'], [kp])
                    CP('act', TA[:, o:o + n], ps[:, 0:n], [kp], ['TA'])
                seq_store(lambda t0, ln: GATE[hp, :, t0:t0 + ln], TA, 'TA', ('GATE', hp))
                for d in range(2):
                    w0o = PLAY['w0_%d' % d][0] + hp
                    a0o = PLAY['a0_%d' % d][0] + hp
                    kao = PLAY['k_a'][0] + hp
                    for (o, n) in PCS:
                        ps, kp = ps_next()
                        MM(ps[:, 0:n], LORA[0:96, 0, d, hp * 128:(hp + 1) * 128], TW[0:96, o:o + n], True, True,
                           ['LORA', 'TW'], [kp])
                        ACTV(WDb[:, o:o + n], ps[:, 0:n], AF.Sigmoid, [kp, 'PAR'], ['WDb'], bias=PAR[:, w0o:w0o + 1])
                        ps2, kp2 = ps_next()
                        MM(ps2[:, 0:n], LORA[0:96, 1, d, hp * 128:(hp + 1) * 128], XA[0:96, o:o + n], True, True,
                           ['LORA', 'XA'], [kp2])
                        ACTV(ALRb[:, o:o + n], ps2[:, 0:n], AF.Sigmoid, [kp2, 'PAR'], ['ALRb'], bias=PAR[:, a0o:a0o + 1])
                    ACTV(WDb[:, SL], WDb[:, SL], AF.Exp, ['WDb'], ['WDb'], scale=-math.exp(-0.5))
                    TS('dve', KDb[:, SL], ALRb[:, SL], PAR[:, kao:kao + 1], OMK[:, hp:hp + 1], ALU.mult, ALU.add,
                       ['ALRb', 'PAR', 'OMK'], ['KDb'])
                    TT('dve', KDb[:, SL], KDb[:, SL], KBb[:, SL], ALU.mult, ['KDb', 'KBb'], ['KDb'])
                    TT('dve', BDb[:, SL], KKb[:, SL], ALRb[:, SL], ALU.mult, ['KKb', 'ALRb'], ['BDb'])
                    to_tm(WDb, 'WDb', 2 + 4 * d, hp, hilo=True)
                    to_tm(BDb, 'BDb', 4 + 4 * d, hp)
                    to_tm(KDb, 'KDb', 5 + 4 * d, hp)
            P.barrier()
            cur.pop()

        chk(2)
        with ExitStack() as ph:
            cur.append(ph)
            ia = IDF[:]
            pstep = ia.ap[0][0]
            ia = IDB[:]
            pstep = ia.ap[0][0]
            SELM = sb("SELM", [128, 64, 128], BF16)
            for u_ in range(2):
                src_ = bass.AP(ia.tensor, ia.offset + 64 * u_, [[pstep, 128], [1, 64], [0, 64]])
                CP('dve', SELM[:, :, u_ * 64:(u_ + 1) * 64], src_, ['IDB'], ['SELM'])
            P.barrier()
            psmod[0] = 7
            WAs = Ring("WAs", 2, [128, KC, 128])

            def adaln1_gen():
                for (l_, j0_) in ((0, 32), (1, 0)):
                    for j in range(j0_, 96):
                        jb, jj = divmod(j, 4)
                        wa, wk = WAs.next()
                        P.dma('sp', wa[:], wada_d[l_, jb][:, :, jj * 128:(jj + 1) * 128], writes=[wk])
                        for kc in range(KC):
                            MM(PS[7][:, 2 * j:2 * j + 2], wa[:, kc, :], SC[:, kc, :], kc == 0, kc == KC - 1,
                               [wk, 'SC'], ['ps7'])
                            yield
                    if l_ == 0:
                        bo0 = PLAY['b_ada0'][0]
                        TT('dve', MOD[:, 0, 32:96], PS[7][:, 64:192].rearrange("p (j s) -> p j s", s=2),
                           PAR[:, bo0 + 32:bo0 + 96].unsqueeze(2).to_broadcast([128, 64, 2]), ALU.add,
                           ['ps7', 'PAR'], ['MOD'])
                        go0 = PLAY['ng01'][0]
                        STT('dve', GG[:, 0, 1], MOD[:, 0, 64:80, :], 1.0,
                            PAR[:, go0:go0 + 16].unsqueeze(2).to_broadcast([128, 16, 2]), ALU.add, ALU.mult,
                            ['MOD', 'PAR'], ['GG'])
            agen = iter(()) if lite else adaln1_gen()
            groups = [[('Sf', 2, 0, 0, None, 'dve'), ('Sb', 2, 1, 1, None, 'pool')],
                      [('P0f', 0, 0, None, (0, 0), 'dve'), ('P0b', 0, 1, None, (0, 1), 'pool')],
                      [('P1f', 1, 0, None, (1, 0), 'dve'), ('P1b', 1, 1, None, (1, 1), 'pool')]]
            def tms_idx(slot, dr):
                return [0, 2 + 4 * dr, 3 + 4 * dr, 4 + 4 * dr, 5 + 4 * dr, 1][slot]
            for grp in groups:
                with ExitStack() as gs:
                    cur.append(gs)
                    tl = []
                    for (nm, si, dr, s0i, outi, teng) in grp:
                        t = dict(nm=nm, dr=dr, outi=outi, eng=teng)
                        t['WS'] = [sb("WS%d_%s" % (i, nm), [128, 8, 64]) for i in range(2)]
                        t['T3'] = sb("T3_" + nm, [128, 8, 64])
                        t['off'], t['T'] = SEQS[si]
                        t['S'] = sb("S_" + nm, [128, 8, 64])
                        t['TMP'] = sb("TMP_" + nm, [128, 8, 64])
                        t['T2'] = sb("T2_" + nm, [128, 8, 64])
                        t['SA'] = sb("SA_" + nm, [128, 8])
                        t['VV'] = [sb("VV%d_%s" % (i, nm), [128, 8, 64]) for i in range(2)]
                        t['YT'] = [sb("YT%d_%s" % (i, nm), [128, 8, 64]) for i in range(2)]
                        t['TMX'] = [sb("TMX%d_%s" % (i, nm), [128, 6, 512], BF16) for i in range(2)]
                        if s0i is not None:
                            P.dma('sp', t['S'][:].rearrange("p h k -> p (h k)"), s0_d[s0i], writes=['S_' + nm])
                        else:
                            MEMSET('dve', t['S'][:], 0.0, ['S_' + nm])
                        tl.append(t)
                    nch = tl[0]['T'] // 64
                    def load_chunk(cq):
                        for t in tl:
                            nm = t['nm']
                            b = cq % 2
                            tok0 = t['off'] + (cq * 64 if t['dr'] == 0 else t['T'] - 64 * (cq + 1))
                            for ti in range(6):
                                idx = tms_idx(ti, t['dr'])
                                for u in range(2):
                                    P.dma('sp' if u == 0 else 'act', t['TMX'][b][u * 64:(u + 1) * 64, ti, :],
                                          TMS[idx, tok0:tok0 + 64, u * 512:(u + 1) * 512],
                                          writes=[('TMX', nm, b)])
                            for u in range(2):
                                src = bass.AP(VBF.tensor, VBF.offset + (u * 512) * NT + tok0,
                                              [[NT, 64], [64 * NT, 8], [1, 64]])
                                P.dma('sp', t['VV'][b][u * 64:(u + 1) * 64, :, :], src, writes=[('VV', nm, b)])

                    for c in range(nch):
                        if c == 0:
                            load_chunk(0)
                        if c + 1 < nch:
                            load_chunk(c + 1)
                        def emit_back(bk):
                            (t_, S_, sk_, TMP_, tk_, pR_, kR_, b_, sel_) = bk
                            TT('dve', TMP_[:], S_[:], pR_, ALU.mult, [sk_, kR_], [tk_])
                            RED('dve', t_['YT'][b_][:, :, sel_], TMP_[:], [tk_], [('YT', t_['nm'], b_)])

                        pend = None
                        usc = [0]
                        for i in range(64):
                            for tix, t in enumerate(tl):
                                nm = t['nm']
                                b = c % 2
                                sel = i if t['dr'] == 0 else 63 - i
                                lhsT = SELM[:, sel, :]
                                S = t['S']
                                sk = 'S_' + nm
                                pss = []
                                for slots in ((0,), (1, 2), (3,), (4,), (5,)):
                                    ps, kp = ps_next()
                                    for si_, sl_ in enumerate(slots):
                                        MM(ps[:], lhsT, t['TMX'][b][:, sl_, :], si_ == 0, si_ == len(slots) - 1,
                                           [('TMX', nm, b)], [kp])
                                    pss.append((ps[:].rearrange("p (h k) -> p h k", k=64), kp))
                                (pA, kA), (pW, kW), (pB, kB), (pK, kK), (pR, kR) = pss
                                next(agen, None)
                                TMP = t['TMP'][tix % 2] if isinstance(t['TMP'], list) else t['TMP']
                                T2 = t['T2']
                                T3 = t['T3']
                                SA = t['SA']
                                WS = t['WS'][i % 2]
                                wsk = ('WS', nm, i % 2)
                                tk, t2k, t3k, sak = 'TMP_' + nm, 'T2_' + nm, 'T3_' + nm, 'SA_' + nm
                                CP('act', WS[:], pW, [kW], [wsk])
                                TT('dve', TMP[:], S[:], pA, ALU.mult, [sk, kA], [tk])
                                RED('dve', SA[:], TMP[:], [tk], [sak])
                                TT('dve', T2[:], pB, SA[:].unsqueeze(2).to_broadcast([128, 8, 64]), ALU.mult, [kB, sak], [t2k])
                                for hh in range(8):
                                    ACTV(T3[:, hh, :], pK[:, hh, :], AF.Identity, [kK, ('VV', nm, b)], [(t3k, hh)],
                                         scale=t['VV'][b][:, hh, sel:sel + 1])
                                TT('pool', S[:], S[:], WS[:], ALU.mult, [sk, wsk], [sk])
                                TT('pool', S[:], S[:], T2[:], ALU.add, [sk, t2k], [sk])
                                TT('pool', S[:], S[:], T3[:], ALU.add, [sk] + [(t3k, hh) for hh in range(8)], [sk])
                                bk = (t, S, sk, TMP, tk, pR, kR, b, sel)
                                if tix == 0:
                                    if pend is not None:
                                        emit_back(pend)
                                        pend = None
                                    backA = bk
                                else:
                                    emit_back(backA)
                                    pend = bk
                        if pend is not None:
                            emit_back(pend)
                            pend = None
                        for t in tl:
                            nm = t['nm']
                            b = c % 2
                            tok0 = t['off'] + (c * 64 if t['dr'] == 0 else t['T'] - 64 * (c + 1))
                            for u in range(2):
                                dst = bass.AP(YF.tensor, YF.offset + t['dr'] * 1024 * NT + (u * 512) * NT + tok0,
                                              [[NT, 64], [64 * NT, 8], [1, 64]])
                                P.dma('sp', dst, t['YT'][b][u * 64:(u + 1) * 64, :, :], reads=[('YT', nm, b)],
                                      writes=[('YF', t['dr'])])
                    for t in tl:
                        if t['outi'] is not None:
                            pi, di = t['outi']
                            P.dma('sp', nst_d[pi, di], t['S'][:].rearrange("p h k -> p (h k)"), reads=['S_' + t['nm']])
                    P.barrier()
                    cur.pop()
            for _ in agen:
                pass
            if not lite:
                bo1 = PLAY['b_ada1'][0]
                TT('dve', MOD[:, 1], PS[7][:, 0:192].rearrange("p (j s) -> p j s", s=2),
                   PAR[:, bo1:bo1 + 96].unsqueeze(2).to_broadcast([128, 96, 2]), ALU.add, ['ps7', 'PAR'], ['MOD'])
                for wh in range(2):
                    go = PLAY['ng1%d' % wh][0]
                    STT('dve', GG[:, 1, wh], MOD[:, 1, (1 + 3 * wh) * 16:(2 + 3 * wh) * 16, :], 1.0,
                        PAR[:, go:go + 16].unsqueeze(2).to_broadcast([128, 16, 2]), ALU.add, ALU.mult,
                        ['MOD', 'PAR'], ['GG'])
                P.barrier()
            psmod[0] = 8
            cur.pop()

        chk(3)
        with ExitStack() as ph:
            cur.append(ph)
            OA = sb("OA", [128, 8, NT], BF16)
            OB = sb("OB", [128, 8, NT], BF16)
            with ExitStack() as p2:
                cur.append(p2)
                Y0 = sb("Y0", [128, NT])
                Y1 = sb("Y1", [128, NT])
                BNb = sb("BNb", [128, NT])
                GTb = sb("GTb", [128, NT])
                DDb = sb("DDb", [128, 512])
                SQb = sb("SQb", [128, 512])
                RSb = sb("RSb", [128, 512])
                for hp in range(8):
                    P.dma('sp', Y0[:], YF[0, hp * 128:(hp + 1) * 128, :], writes=['Y0'])
                    P.dma('act', Y1[:], YF[1, hp * 128:(hp + 1) * 128, :], writes=['Y1'])
                    P.dma('sp', BNb[:], BON[hp], writes=['BNb'])
                    P.dma('act', GTb[:], GATE[hp], writes=['GTb'])
                    TT('dve', Y0[:], Y0[:], Y1[:], ALU.add, ['Y0', 'Y1'], ['Y0'])
                    lgo = PLAY['lnx_g'][0] + hp
                    lbo = PLAY['lnx_b'][0] + hp
                    for tb in range(3):
                        ts_ = slice(tb * 512, (tb + 1) * 512)
                        ps, kp = ps_next()
                        MM(ps[:], BOF[:], Y0[:, ts_], True, True, ['BOF', 'Y0'], [kp])
                        STT('dve', DDb[:], ps[:], -1.0 / 64, Y0[:, ts_], ALU.mult, ALU.add, [kp, 'Y0'], ['DDb'])
                        TT('dve', SQb[:], DDb[:], DDb[:], ALU.mult, ['DDb'], ['SQb'])
                        ps2, kp2 = ps_next()
                        MM(ps2[:], BOF[:], SQb[:], True, True, ['BOF', 'SQb'], [kp2])
                        RSQ(RSb[:], ps2[:], 1.0 / 64, GN_EPS, [kp2], ['RSb'])
                        TT('dve', DDb[:], DDb[:], RSb[:], ALU.mult, ['DDb', 'RSb'], ['DDb'])
                        TS('dve', DDb[:], DDb[:], PAR[:, lgo:lgo + 1], PAR[:, lbo:lbo + 1], ALU.mult, ALU.add,
                           ['DDb', 'PAR'], ['DDb'])
                        TT('dve', DDb[:], DDb[:], BNb[:, ts_], ALU.add, ['DDb', 'BNb'], ['DDb'])
                        TT('dve', OB[:, hp, ts_], DDb[:], GTb[:, ts_], ALU.mult, ['DDb', 'GTb'], ['OB'])
                P.barrier()
                cur.pop()
            with ExitStack() as p2:
                cur.append(p2)
                QhR = [sb("Qh%d" % i, [128, NT], BF16) for i in range(2)]
                KhR = [sb("Kh%d" % i, [128, NT], BF16) for i in range(2)]
                VTR = [sb("VT%d" % i, [128, 12, 128], BF16) for i in range(2)]
                CKR = [sb("CK%d" % i, [128, 256], BF16) for i in range(2)]
                CVR = [sb("CVt%d" % i, [128, 2, 128], BF16) for i in range(2)]
                EX = Ring("EX", 3, [128, 512], BF16)
                R1 = sb("R1", [128, 512])
                O1 = sb("O1", [128, 512])
                O2 = sb("O2", [128, 512])
                SQa = sb("SQa", [128, 512])
                RSa = sb("RSa", [128, 512])
                for h in range(8):
                    hb = h % 2
                    Qh, Kh, VT, CK, CVt = QhR[hb], KhR[hb], VTR[hb], CKR[hb], CVR[hb]
                    kQh, kKh, kVT, kCK, kCV = 'Qh%d' % hb, 'Kh%d' % hb, 'VT%d' % hb, 'CK%d' % hb, 'CVt%d' % hb
                    P.dma('sp', Qh[:], QS[h], writes=[kQh])
                    P.dma('sp', Kh[:], QS[8 + h], writes=[kKh])
                    P.dma('sp', VT[:], VS[h].rearrange("(j p) d -> p j d", p=128), writes=[kVT])
                    P.dma('pool', CK[:], cK_d[h], writes=[kCK])
                    P.dma('pool', CVt[:], cV_d[h].rearrange("(j p) d -> p j d", p=128), writes=[kCV])
                    jobs = [(0, 256, [('n', 0), ('n', 1)]), (256, 256, [('n', 2), ('n', 3)]),
                            (512, 512, [('c', 0), ('c', 1)] + [('n', j) for j in range(4, 12)]),
                            (1024, 512, [('c', 0), ('c', 1)] + [('n', j) for j in range(4, 12)])]
                    for (q0, nq, kts) in jobs:
                        acc = []
                        for m in range(2):
                            psO, kO = PS[2 * m], 'ps%d' % (2 * m)
                            psD, kD = PS[2 * m + 1], 'ps%d' % (2 * m + 1)
                            for ki, (kind, j) in enumerate(kts):
                                if kind == 'n':
                                    ksrc = Kh[m * 64:(m + 1) * 64, j * 128:(j + 1) * 128]
                                    vsrc = VT[:, j, :]
                                    kr, vr = kKh, kVT
                                else:
                                    ksrc = CK[m * 64:(m + 1) * 64, j * 128:(j + 1) * 128]
                                    vsrc = CVt[:, j, :]
                                    kr, vr = kCK, kCV
                                psS, kS = ps_hi()
                                MM(psS[:, 0:nq], ksrc, Qh[m * 64:(m + 1) * 64, q0:q0 + nq], True, True, [kr, kQh], [kS])
                                ex, ek = EX.next()
                                ACTV(ex[:, 0:nq], psS[:, 0:nq], AF.Exp, [kS], [ek], scale=0.125)
                                MM(psO[:, 0:nq], vsrc, ex[:, 0:nq], ki == 0, ki == len(kts) - 1, [vr, ek], [kO])
                                MM(psD[:, 0:nq], ONB[:], ex[:, 0:nq], ki == 0, ki == len(kts) - 1, ['ONB', ek], [kD])
                            acc.append((psO, kO, psD, kD))
                        (pO1, kO1, pD1, kD1), (pO2, kO2, pD2, kD2) = acc
                        n_ = slice(0, nq)
                        P.op('dve', lambda e, a=R1[:, n_], b=pD1[:, n_]: e.reciprocal(out=a, in_=b), [kD1], ['R1'])
                        TT('dve', O1[:, n_], pO1[:, n_], R1[:, n_], ALU.mult, [kO1, 'R1'], ['O1'])
                        P.op('dve', lambda e, a=R1[:, n_], b=pD2[:, n_]: e.reciprocal(out=a, in_=b), [kD2, 'O1'], ['R1'])
                        TT('dve', O2[:, n_], pO2[:, n_], R1[:, n_], ALU.mult, [kO2, 'R1'], ['O2'])
                        STT('dve', O1[:, n_], O2[:, n_], NEGLAM[:, 0:1], O1[:, n_], ALU.mult, ALU.add,
                            ['O2', 'O1', 'NEGLAM'], ['O1'])
                        TT('dve', SQa[:, n_], O1[:, n_], O1[:, n_], ALU.mult, ['O1'], ['SQa'])
                        psq, kq = ps_hi()
                        MM(psq[:, n_], ONF[:], SQa[:, n_], True, True, ['ONF', 'SQa'], [kq])
                        RSQ(RSa[:, n_], psq[:, n_], 1.0 / 128, LN_EPS, [kq], ['RSa'])
                        TT('dve', O1[:, n_], O1[:, n_], RSa[:, n_], ALU.mult, ['O1', 'RSa'], ['O1'])
                        TS('dve', OA[:, h, q0:q0 + nq], O1[:, n_], SUBG[:, 0:1], None, ALU.mult, None, ['O1', 'SUBG'], ['OA'])
                P.barrier()
                cur.pop()
            WB = Ring("WBo", 2, [128, KC, 128], BF16)
            for c in range(16):
                wb, wk = load_w(WB, wout_d[c])
                for tb in range(3):
                    ps, kp = ps_next()
                    for kc in range(KC):
                        src = OA if kc < 8 else OB
                        MM(ps[:], wb[:, kc, :], src[:, kc % 8, tb * 512:(tb + 1) * 512], kc == 0, kc == KC - 1,
                           [wk, 'OA', 'OB'], [kp])
                    resid_add(ps, kp, 0, 0, c, tb)
            P.barrier()
            cur.pop()

        chk(4)
        TBR = [[(0, 256, 1), (256, 256, 258)], [(0, 512, 515)], [(0, 512, 1027)]]

        def ffn(l):
            with ExitStack() as ph:
                cur.append(ph)
                H = sb("Hf%d" % l, [128, KC, NT], BF16)
                with ExitStack() as p2:
                    cur.append(p2)
                    SQ = Ring("SQf%d" % l, 2, [128, 512])
                    TMPN = Ring("TMPNf%d" % l, 2, [128, 512])
                    RSTD = sb("RSTDf%d" % l, [128, 512])
                    norm_mod(H, l, 1, SQ, RSTD, TMPN)
                    P.barrier()
                    cur.pop()
                G = 2
                WUr = Ring("WU%d_" % l, 2, [128, KC, 256], BF16)
                WDG = sb("WDG%d" % l, [128, G, D], BF16)
                H2G = sb("H2G%d" % l, [128, G, NPAD], BF16)
                UP = [sb("UP%d_%d" % (l, i), [128, NPAD]) for i in range(2)]
                CVg = sb("CVg%d" % l, [128, NPAD])
                for i in range(2):
                    MEMSET('dve', UP[i][:], 0.0, ['UP%d' % i])
                fo = PLAY['fdw%d' % l][0]
                SL = slice(1, 1539)
                c = 0
                while c < NFF:
                    gn = min(G, NFF - c)
                    for g in range(gn):
                        cc = c + g
                        WU, wuk = WUr.next()
                        P.dma('pool', WU[:], wup_d[l, cc], writes=[wuk])
                        P.dma('pool', WDG[:, g, :], wdn_d[l, cc], writes=[('WDG', g)])
                        for gv in range(2):
                            for tb in range(3):
                                ps, kp = ps_next()
                                for kc in range(KC):
                                    MM(ps[:], WU[:, kc, gv * 128:(gv + 1) * 128], H[:, kc, tb * 512:(tb + 1) * 512],
                                       kc == 0, kc == KC - 1, [wuk, ('H', tb)], [kp])
                                for (pc, n, po) in TBR[tb]:
                                    CP('act', UP[gv][:, po:po + n], ps[:, pc:pc + n], [kp], ['UP%d' % gv])
                            to = fo + (cc * 2 + gv) * 3
                            dst, dk = (CVg, 'CVg') if gv == 0 else (UP[0], 'UP0')
                            TS('dve', dst[:, SL], UP[gv][:, SL], PAR[:, to + 1:to + 2], None, ALU.mult, None,
                               ['UP%d' % gv, 'PAR'], [dk])
                            STT('dve', dst[:, SL], UP[gv][:, 0:1538], PAR[:, to:to + 1], dst[:, SL], ALU.mult, ALU.add,
                                ['UP%d' % gv, 'PAR', dk], [dk])
                            STT('dve', dst[:, SL], UP[gv][:, 2:1540], PAR[:, to + 2:to + 3], dst[:, SL], ALU.mult, ALU.add,
                                ['UP%d' % gv, 'PAR', dk], [dk])
                            if gv == 0:
                                ACTV(CVg[:, SL], CVg[:, SL], AF.Silu, ['CVg'], ['CVg'])
                        TT('dve', H2G[:, g, SL], CVg[:, SL], UP[0][:, SL], ALU.mult, ['CVg', 'UP0'], [('H2G', g)])
                        MEMSET('dve', UP[0][:, 257:258], 0.0, ['UP0'])
                        MEMSET('dve', UP[0][:, 514:515], 0.0, ['UP0'])
                    for ko in range(KC):
                        for tb in range(3):
                            ps, kp = ps_next()
                            for (pc, n, po) in TBR[tb]:
                                for g in range(gn):
                                    MM(ps[:, pc:pc + n], WDG[:, g, ko * 128:(ko + 1) * 128], H2G[:, g, po:po + n],
                                       g == 0, g == gn - 1, [('WDG', g), ('H2G', g)], [kp])
                            resid_add(ps, kp, l, 1, ko, tb)
                    c += gn
                P.barrier()
                cur.pop()

        ffn(0)

        chk(5)
        NC_ = NT + 60
        CO = [15, 286, 557]
        TBC = [[(0, 256, 15), (256, 256, 286)], [(0, 512, 557)], [(0, 512, 1069)]]
        with ExitStack() as ph:
            cur.append(ph)
            H = sb("Hc", [128, KC, NT], BF16)
            with ExitStack() as p2:
                cur.append(p2)
                SQ = Ring("SQc", 2, [128, 512])
                TMPN = Ring("TMPNc", 2, [128, 512])
                RSTD = sb("RSTDc", [128, 512])
                norm_mod(H, 1, 0, SQ, RSTD, TMPN)
                P.barrier()
                cur.pop()
            WBa = Ring("WBa", 2, [128, KC, 128], BF16)
            UAr = [sb("UA%d" % i, [128, NC_]) for i in range(2)]
            ACC = sb("ACC", [128, NC_])
            SQc = sb("SQc2", [128, NC_])
            MEAN = sb("MEAN", [128, 3, 512])
            RSC = sb("RSC", [128, 3, 512])
            for i in range(2):
                MEMSET('dve', UAr[i][:], 0.0, ['UA%d' % i])
            wdo = PLAY['w_dw'][0]
            bdo = PLAY['b_dw'][0]
            CS = slice(15, 1581)
            for c in range(16):
                wa, wak = load_w(WBa, wpw1_d[c])
                wg, wgk = load_w(WBa, wpw1_d[16 + c])
                UA, uak = UAr[c % 2], 'UA%d' % (c % 2)
                for tb in range(3):
                    psA, kA = PS[6], 'ps6'
                    psB, kB = PS[7], 'ps7'
                    for kc in range(KC):
                        MM(psA[:], wa[:, kc, :], H[:, kc, tb * 512:(tb + 1) * 512], kc == 0, kc == KC - 1, [wak, ('H', tb)], [kA])
                    for kc in range(KC):
                        MM(psB[:], wg[:, kc, :], H[:, kc, tb * 512:(tb + 1) * 512], kc == 0, kc == KC - 1, [wgk, ('H', tb)], [kB])
                    for (pc, n, po) in TBC[tb]:
                        ACTV(UA[:, po:po + n], psB[:, pc:pc + n], AF.Sigmoid, [kB], [uak])
                        TT('dve', UA[:, po:po + n], psA[:, pc:pc + n], UA[:, po:po + n], ALU.mult, [kA, uak], [uak])
                TS('dve', ACC[:, CS], UA[:, 0:1566], PAR[:, wdo + c * 31:wdo + c * 31 + 1], PAR[:, bdo + c:bdo + c + 1],
                   ALU.mult, ALU.add, [uak, 'PAR'], ['ACC'])
                for j in range(1, 31):
                    STT('dve', ACC[:, CS], UA[:, j:j + 1566], PAR[:, wdo + c * 31 + j:wdo + c * 31 + j + 1], ACC[:, CS],
                        ALU.mult, ALU.add, [uak, 'PAR', 'ACC'], ['ACC'])
                TT('dve', SQc[:, CS], ACC[:, CS], ACC[:, CS], ALU.mult, ['ACC'], ['SQc'])
                for tb in range(3):
                    for (pc, n, po) in TBC[tb]:
                        MM(PS[tb][:, pc:pc + n], ONF[:], ACC[:, po:po + n], c == 0, c == 15, ['ONF', 'ACC'], ['ps%d' % tb])
                        MM(PS[3 + tb][:, pc:pc + n], ONF[:], SQc[:, po:po + n], c == 0, c == 15, ['ONF', 'SQc'], ['ps%d' % (3 + tb)])
                for (t0, ln, po) in [(0, 256, 15), (256, 256, 286), (512, 1024, 557)]:
                    P.dma('sp', CONV[c, :, t0:t0 + ln], ACC[:, po:po + ln], reads=['ACC'], writes=[('CONV', c)])
            for tb in range(3):
                TS('dve', MEAN[:, tb, :], PS[tb][:], 1.0 / D, None, ALU.mult, None, ['ps%d' % tb], ['MEAN'])
                TT('dve', RSC[:, tb, :], MEAN[:, tb, :], MEAN[:, tb, :], ALU.mult, ['MEAN'], ['RSC'])
                STT('dve', RSC[:, tb, :], PS[3 + tb][:], 1.0 / D, RSC[:, tb, :], ALU.mult, ALU.subtract,
                    ['ps%d' % (3 + tb), 'RSC'], ['RSC'])
                RSQ(RSC[:, tb, :], RSC[:, tb, :], 1.0, LN_EPS, ['RSC'], ['RSC'])
            CL = Ring("CL", 1, [128, NT])
            MV = MEAN[:].rearrange("p a b -> p (a b)")
            RV = RSC[:].rearrange("p a b -> p (a b)")
            cgo = PLAY['cln_g'][0]
            cbo = PLAY['cln_b'][0]
            for c in range(16):
                cl, ck = CL.next()
                P.dma('sp', cl[:], CONV[c], reads=[('CONV', c)], writes=[ck])
                TT('dve', cl[:], cl[:], MV, ALU.subtract, [ck, 'MEAN'], [ck])
                TT('dve', cl[:], cl[:], RV, ALU.mult, [ck, 'RSC'], [ck])
                TS('dve', cl[:], cl[:], PAR[:, cgo + c:cgo + c + 1], PAR[:, cbo + c:cbo + c + 1], ALU.mult, ALU.add,
                   [ck, 'PAR'], [ck])
                ACTV(H[:, c, :], cl[:], AF.Silu, [ck], [('H', 0), ('H', 1), ('H', 2)])
            for c in range(16):
                wb, wk = load_w(WBa, wpw2_d[c])
                for tb in range(3):
                    ps, kp = PS[6 + (tb % 2)], 'ps%d' % (6 + (tb % 2))
                    for kc in range(KC):
                        MM(ps[:], wb[:, kc, :], H[:, kc, tb * 512:(tb + 1) * 512], kc == 0, kc == KC - 1, [wk, ('H', tb)], [kp])
                    resid_add(ps, kp, 1, 0, c, tb)
            P.barrier()
            cur.pop()

        chk(6)
        ffn(1)
        chk(7)

        with ExitStack() as ph:
            cur.append(ph)
            SQ = Ring("SQz", 2, [128, 512])
            YO = Ring("YO", 3, [128, 512])
            RSTD = sb("RSTDz", [128, 512])
            fo_ = PLAY['fng'][0]
            for tb in range(3):
                psN, kN = ps_next()
                for kc in range(KC):
                    sq, sk = SQ.next()
                    ACTV(sq[:], X[:, kc, tb * 512:(tb + 1) * 512], AF.Square, [('X', tb)], [sk])
                    MM(psN[:], ONF[:], sq[:], kc == 0, kc == KC - 1, [sk, 'ONF'], [kN])
                RSQ(RSTD[:], psN[:], 1.0 / D, RMS_EPS, [kN], ['RSTDz'])
                for kc in range(KC):
                    yo, yk = YO.next()
                    TT('dve', yo[:], X[:, kc, tb * 512:(tb + 1) * 512], RSTD[:], ALU.mult, [('X', tb), 'RSTDz'], [yk])
                    TS('dve', yo[:], yo[:], PAR[:, fo_ + kc:fo_ + kc + 1], None, ALU.mult, None, [yk, 'PAR'], [yk])
                    P.dma('sp', y_d[kc * 128:(kc + 1) * 128, tb * 512:(tb + 1) * 512], yo[:], reads=[yk])
            cur.pop()
        if debug:
            print('instr counts', {k: len(v) for k, v in P.prog.items()}, 'sems', P.nsem)
        P.finish()
    return nc


def _fm(v, n=None):
    v = np.asarray(v, np.float32).reshape(-1)
    return np.ascontiguousarray(v.reshape(-1, 128).T)


def _arr_w(W, cols=None):
    K, N = W.shape
    if cols is None:
        Wc = W.reshape(K // 128, 128, N // 128, 128)
        return np.ascontiguousarray(Wc.transpose(2, 1, 0, 3))
    out = np.zeros((len(cols), 128, K // 128, 128), np.float32)
    for c, cl in enumerate(cols):
        cl = np.asarray(cl)
        m = cl >= 0
        blk = np.zeros((K, 128), np.float32)
        blk[:, m] = W[:, cl[m]]
        out[c] = blk.reshape(K // 128, 128, 128).transpose(1, 0, 2)
    return out


def prep_shared(inp):
    f = lambda k: np.asarray(inp[k], np.float32)
    sh = {}
    pars = np.zeros((128, NPAR), np.float32)

    def put(name, a):
        o, w = PLAY[name]
        a = np.asarray(a, np.float32)
        assert a.shape == (128, w), (name, a.shape, w)
        pars[:, o:o + w] = a
    b_ada = f('b_ada')
    put('b_ada0', _fm(b_ada[0]))
    put('b_ada1', _fm(b_ada[1]))
    ng = f('norm_g')
    for l in range(2):
        for wh in range(2):
            put('ng%d%d' % (l, wh), _fm(ng[l, wh]))
    put('fng', _fm(f('final_norm_g')))
    mu = f('shift_mu')[0]
    for i in range(2):
        m = np.zeros((128, 28), np.float32)
        m[:, 0:24] = _fm(mu[i, 0:3072])
        m[:96, 24] = mu[i, 3072:3168]
        m[:96, 25] = mu[i, 3168:3264]
        m[:, 26:28] = _fm(mu[i, 3264:3520])
        put('mu%d' % i, m)
    for d in range(2):
        put('w0_%d' % d, _fm(f('w0')[0, d]))
        put('a0_%d' % d, _fm(f('a0')[0, d]))
    put('k_k', _fm(f('k_k')[0]))
    put('k_a', _fm(f('k_a')[0]))
    put('r_k', _fm(f('r_k')[0].reshape(-1)))
    put('lnx_g', _fm(f('lnx_g')[0]))
    put('lnx_b', _fm(f('lnx_b')[0]))
    put('subln', f('subln_g')[0].reshape(128, 1))
    put('dl', np.broadcast_to(f('diff_lambda')[0].reshape(1, 256), (128, 256)))
    put('b_dw', _fm(f('b_dw')[0]))
    put('cln_g', _fm(f('cln_g')[0]))
    put('cln_b', _fm(f('cln_b')[0]))
    wdw = f('w_dw')[0]
    put('w_dw', np.ascontiguousarray(wdw.reshape(31, 16, 128).transpose(2, 1, 0)).reshape(128, 496))
    fd = f('w_ffn_dw')
    for l in range(2):
        a = fd[l].reshape(3, 2, NFF, 128).transpose(3, 2, 1, 0)
        put('fdw%d' % l, np.ascontiguousarray(a).reshape(128, 258))
    sh['pars'] = pars
    sh['ident'] = np.eye(128, dtype=np.float32)
    bo = np.zeros((128, 128), np.float32)
    bo[:64, :64] = 1
    bo[64:, 64:] = 1
    sh['bones'] = bo
    rt = np.zeros((128, 128), np.float32)
    ang = np.zeros((128, 1024), np.float64)
    tt = np.arange(1024)
    row = (tt // 64).astype(np.float64)
    col = (tt % 64).astype(np.float64)
    inv = (10000.0 ** (-np.arange(16, dtype=np.float32) / 16)).astype(np.float32).astype(np.float64)
    for m in range(2):
        for d in range(64):
            half, r = divmod(d, 32)
            if r < 16:
                partner, sgn, fi = d + 16, -1.0, r
            else:
                partner, sgn, fi = d - 16, 1.0, r - 16
            rt[m * 64 + partner, m * 64 + d] = sgn
            pos = row if half == 0 else col
            ang[m * 64 + d] = (pos.astype(np.float32) * np.float32(inv[fi])).astype(np.float64)
    sh['rt'] = rt
    sh['cos'] = np.cos(ang).astype(np.float32)
    sh['sin'] = np.sin(ang).astype(np.float32)
    wa = f('w_ada')
    sh['wada'] = np.ascontiguousarray(wa.reshape(2, KC, 128, 24, 512).transpose(0, 3, 2, 1, 4))
    sh['win'] = _arr_w(f('w_in')[0], win_cols())
    lora = np.zeros((128, 3, 2, 1024), np.float32)
    lora[:96, 0] = f('w2')[0].transpose(1, 0, 2)
    lora[:96, 1] = f('a2')[0].transpose(1, 0, 2)
    lora[:, 2] = f('g2')[0].reshape(2, 128, 1024).transpose(1, 0, 2)
    sh['lora'] = lora
    sh['wout'] = _arr_w(f('w_out')[0])
    sh['wpw1'] = _arr_w(f('w_pw1')[0])
    sh['wpw2'] = _arr_w(f('w_pw2')[0])
    wu = f('w_up')
    a = wu.reshape(2, KC, 128, 2, NFF, 128).transpose(0, 4, 2, 1, 3, 5)
    sh['wup'] = np.ascontiguousarray(a).reshape(2, NFF, 128, KC, 256)
    sh['wdn'] = np.ascontiguousarray(f('w_down').reshape(2, NFF, 128, D))
    return sh


def prep_core(inp, i):
    f = lambda k: np.asarray(inp[k], np.float32)
    m = {}
    xp = f('x_prompt')[2 * i:2 * i + 2].reshape(512, D)
    xs = f('x_sample')[i]
    m['xin'] = np.ascontiguousarray(np.concatenate([xp, xs], 0).T)
    cf = np.stack([f('c')[i], f('c_ctx')], -1)
    m['cfm'] = np.ascontiguousarray(cf.reshape(KC, 128, 2).transpose(1, 0, 2))
    ck = f('cache_k')[i, 0]
    m['cK'] = np.ascontiguousarray(ck.transpose(0, 1, 3, 2)).reshape(8, 128, 256)
    m['cV'] = np.ascontiguousarray(f('cache_v')[i, 0])
    s0 = np.stack([f('state_wkv_fwd')[i, 0], f('state_wkv_bwd')[i, 0]], 0)
    m['s0'] = np.ascontiguousarray(s0.reshape(2, 2, 8, 64, 64).transpose(0, 1, 3, 2, 4)).reshape(2, 128, 512)
    return m


_NC_CACHE = {}


def kernel(**inputs):
    sh = prep_shared(inputs)
    in_maps = []
    for i in range(8):
        m = dict(sh)
        m.update(prep_core(inputs, i))
        in_maps.append(m)
    if 'nc' not in _NC_CACHE:
        _NC_CACHE['nc'] = build()
    nc = _NC_CACHE['nc']
    res = run_bass_kernel_spmd(nc, in_maps, core_ids=list(range(8)))
    yp = np.zeros((16, 256, D), np.float32)
    ys = np.zeros((8, 1024, D), np.float32)
    nk = np.zeros((16, 1, 8, 2, 256, 64), np.float32)
    nv = np.zeros((16, 1, 8, 256, 128), np.float32)
    sf = np.zeros((16, 1, 16, 64, 64), np.float32)
    sbw = np.zeros((16, 1, 16, 64, 64), np.float32)
    for i in range(8):
        r = res.results[i]
        y = np.asarray(r['y'], np.float32).T
        yp[2 * i:2 * i + 2] = y[:512].reshape(2, 256, D)
        ys[i] = y[512:]
        k = np.asarray(r['nk'], np.float32).reshape(2, 8, 2, 64, 256)
        nk[2 * i:2 * i + 2, 0] = k.transpose(0, 1, 2, 4, 3)
        nv[2 * i:2 * i + 2, 0] = np.asarray(r['nv'], np.float32)
        s = np.asarray(r['nst'], np.float32).reshape(2, 2, 2, 64, 8, 64)
        s = s.transpose(0, 1, 2, 4, 3, 5).reshape(2, 2, 16, 64, 64)
        sf[2 * i:2 * i + 2, 0] = s[:, 0]
        sbw[2 * i:2 * i + 2, 0] = s[:, 1]
    return (yp, ys, nk, nv, sf, sbw)
```
